# Optimizing a Trainium2 kernel written in Bass

```python
import jax, jax.numpy as jnp
from jax import lax
import numpy as np

D_MODEL = 1024
BATCH = 4
SEQ = 4096
DEPTH = 4

CHUNK = 64
Q_BLOCK = 128
SC_WIDTH = 512
SC_GROUPS = 8
SC_KERNEL = 3
MLA_HEADS = 8
QK_NOPE = 64
QK_ROPE = 32
V_HEAD = 64
Q_LORA = 256
KV_LORA = 128
MLA_WIDTH = MLA_HEADS * V_HEAD
ROPE_THETA = 10000.0
CONF_WIDTH = D_MODEL
CONF_KERNEL = 31
EVEN_SPLITS = (SC_WIDTH, SC_WIDTH, SC_WIDTH, SC_WIDTH, Q_LORA, KV_LORA, QK_ROPE, MLA_WIDTH)
EVEN_IN = sum(EVEN_SPLITS)
ODD_IN = 3 * CONF_WIDTH
N_EVEN = (DEPTH + 1) // 2
N_ODD = DEPTH // 2
EPS = 1e-6

kernel_name = 'hybrid_chunk_causal_conv_mla_conformer_trunk'


def rms_norm(x, g):
    xf = x.astype(jnp.float32)
    y = xf * lax.rsqrt(jnp.mean(xf * xf, axis=-1, keepdims=True) + EPS)
    return (y * g.astype(jnp.float32)).astype(x.dtype)


def layer_norm(x, g, b):
    xf = x.astype(jnp.float32)
    mu = jnp.mean(xf, axis=-1, keepdims=True)
    var = jnp.mean(jnp.square(xf - mu), axis=-1, keepdims=True)
    y = (xf - mu) * lax.rsqrt(var + EPS)
    return (y * g.astype(jnp.float32) + b.astype(jnp.float32)).astype(x.dtype)


def split_cols(z, sizes):
    idx = np.cumsum(sizes)[:-1].tolist()
    return jnp.split(z, idx, axis=-1)


def causal_depthwise_conv(u, w, b):
    k, ch = w.shape
    y = lax.conv_general_dilated(u, w[:, None, :].astype(u.dtype), window_strides=(1,),
                                 padding=[(k - 1, 0)], dimension_numbers=('NWC', 'WIO', 'NWC'),
                                 feature_group_count=ch)
    return y + b.astype(u.dtype)


def rope_tables(positions):
    inv_freq = 1.0 / (ROPE_THETA ** (jnp.arange(0, QK_ROPE, 2, dtype=jnp.float32) / QK_ROPE))
    ang = positions.astype(jnp.float32)[..., None] * inv_freq
    return jnp.cos(ang), jnp.sin(ang)


def apply_rope(t, cos, sin):
    tf = t.astype(jnp.float32)
    t1, t2 = jnp.split(tf, 2, axis=-1)
    return jnp.concatenate([t1 * cos - t2 * sin, t2 * cos + t1 * sin], axis=-1).astype(t.dtype)


def block_causal_attention(q, k, v):
    b, h, s, dqk = q.shape
    nb = s // Q_BLOCK
    scale = 1.0 / np.sqrt(dqk)
    k_chunk = jnp.arange(s) // CHUNK
    qb = q.reshape(b, h, nb, Q_BLOCK, dqk).transpose(2, 0, 1, 3, 4)

    def one_block(args):
        q_blk, blk = args
        scores = jnp.einsum('bhqd,bhkd->bhqk', q_blk, k, preferred_element_type=jnp.float32) * scale
        q_chunk = (blk * Q_BLOCK + jnp.arange(Q_BLOCK)) // CHUNK
        allowed = k_chunk[None, :] <= q_chunk[:, None]
        scores = jnp.where(allowed, scores, jnp.finfo(jnp.float32).min)
        p = jax.nn.softmax(scores, axis=-1)
        return jnp.einsum('bhqk,bhkd->bhqd', p.astype(v.dtype), v)

    out = lax.map(one_block, (qb, jnp.arange(nb)))
    return out.transpose(1, 0, 3, 2, 4).reshape(b, s, h * v.shape[-1])


def even_mixer(h, cos, sin, w_in, sc_conv_w, sc_conv_b, q_norm_g, kv_norm_g, w_uq, w_ukv, w_out):
    bsz, s, _ = h.shape
    z = h @ w_in
    a_b, a_c, a_x, a_gate, c_q, c_kv, k_rope_raw, b_gate = split_cols(z, EVEN_SPLITS)
    y_a = a_b * causal_depthwise_conv(a_c * a_x, sc_conv_w, sc_conv_b)
    y_a = y_a * jax.nn.silu(a_gate)
    q = (rms_norm(c_q, q_norm_g) @ w_uq).reshape(bsz, s, MLA_HEADS, QK_NOPE + QK_ROPE)
    q_nope, q_rope = q[..., :QK_NOPE], q[..., QK_NOPE:]
    q_rope = apply_rope(q_rope, cos[:, :, None, :], sin[:, :, None, :])
    kv = (rms_norm(c_kv, kv_norm_g) @ w_ukv).reshape(bsz, s, MLA_HEADS, QK_NOPE + V_HEAD)
    k_nope, v = kv[..., :QK_NOPE], kv[..., QK_NOPE:]
    k_rope = apply_rope(k_rope_raw, cos, sin)
    k_rope = jnp.broadcast_to(k_rope[:, :, None, :], (bsz, s, MLA_HEADS, QK_ROPE))
    q_full = jnp.concatenate([q_nope, q_rope], axis=-1).transpose(0, 2, 1, 3)
    k_full = jnp.concatenate([k_nope, k_rope], axis=-1).transpose(0, 2, 1, 3)
    y_b = block_causal_attention(q_full, k_full, v.transpose(0, 2, 1, 3))
    y_b = y_b * jax.nn.silu(b_gate)
    return jnp.concatenate([y_a, y_b], axis=-1) @ w_out


def odd_mixer(h, w_in, conv_w, conv_b, ln_g, ln_b, w_out):
    z = h @ w_in
    val, glu_gate, silu_gate = jnp.split(z, 3, axis=-1)
    u = val * jax.nn.sigmoid(glu_gate)
    u = causal_depthwise_conv(u, conv_w, conv_b)
    u = jax.nn.silu(layer_norm(u, ln_g, ln_b))
    return (u * jax.nn.silu(silu_gate)) @ w_out


def setup_inputs(seed: int = 0) -> dict:
    key = jax.random.key(seed)
    ks = iter(jax.random.split(key, 32))

    def nrm(shape, scale):
        return jax.random.normal(next(ks), shape, jnp.float32) * scale

    d = D_MODEL
    return {
        'x': nrm((BATCH, SEQ, d), 1.0),
        'c': nrm((BATCH, d), 1.0),
        'positions': (jax.random.randint(next(ks), (BATCH, 1), 0, 4096, dtype=jnp.int32)
                      + jnp.arange(SEQ, dtype=jnp.int32)[None, :]),
        'ada_w': nrm((DEPTH, d, 3 * d), 0.5 * d ** -0.5),
        'ada_b': nrm((DEPTH, 3 * d), 0.1),
        'pre_norm_g': 1.0 + nrm((DEPTH, d), 0.05),
        'post_norm_g': 1.0 + nrm((DEPTH, d), 0.05),
        'even_w_in': nrm((N_EVEN, d, EVEN_IN), d ** -0.5),
        'even_sc_conv_w': nrm((N_EVEN, SC_KERNEL, SC_WIDTH), SC_KERNEL ** -0.5),
        'even_sc_conv_b': nrm((N_EVEN, SC_WIDTH), 0.01),
        'even_q_norm_g': 1.0 + nrm((N_EVEN, Q_LORA), 0.05),
        'even_kv_norm_g': 1.0 + nrm((N_EVEN, KV_LORA), 0.05),
        'even_w_uq': nrm((N_EVEN, Q_LORA, MLA_HEADS * (QK_NOPE + QK_ROPE)), Q_LORA ** -0.5),
        'even_w_ukv': nrm((N_EVEN, KV_LORA, MLA_HEADS * (QK_NOPE + V_HEAD)), KV_LORA ** -0.5),
        'even_w_out': nrm((N_EVEN, SC_WIDTH + MLA_WIDTH, d), (SC_WIDTH + MLA_WIDTH) ** -0.5),
        'odd_w_in': nrm((N_ODD, d, ODD_IN), d ** -0.5),
        'odd_conv_w': nrm((N_ODD, CONF_KERNEL, CONF_WIDTH), CONF_KERNEL ** -0.5),
        'odd_conv_b': nrm((N_ODD, CONF_WIDTH), 0.01),
        'odd_ln_g': 1.0 + nrm((N_ODD, CONF_WIDTH), 0.05),
        'odd_ln_b': nrm((N_ODD, CONF_WIDTH), 0.01),
        'odd_w_out': nrm((N_ODD, CONF_WIDTH, d), CONF_WIDTH ** -0.5),
    }


def reference(x, c, positions, ada_w, ada_b, pre_norm_g, post_norm_g,
              even_w_in, even_sc_conv_w, even_sc_conv_b, even_q_norm_g, even_kv_norm_g,
              even_w_uq, even_w_ukv, even_w_out,
              odd_w_in, odd_conv_w, odd_conv_b, odd_ln_g, odd_ln_b, odd_w_out):
    cos, sin = rope_tables(positions)
    c_act = jax.nn.silu(c)
    for layer in range(DEPTH):
        mod = c_act @ ada_w[layer] + ada_b[layer]
        shift, scale, gate = jnp.split(mod, 3, axis=-1)
        h = rms_norm(x, pre_norm_g[layer]) * (1.0 + scale[:, None, :]) + shift[:, None, :]
        i = layer // 2
        if layer % 2 == 0:
            y = even_mixer(h, cos, sin, even_w_in[i], even_sc_conv_w[i], even_sc_conv_b[i],
                           even_q_norm_g[i], even_kv_norm_g[i], even_w_uq[i], even_w_ukv[i],
                           even_w_out[i])
        else:
            y = odd_mixer(h, odd_w_in[i], odd_conv_w[i], odd_conv_b[i], odd_ln_g[i],
                          odd_ln_b[i], odd_w_out[i])
        x = x + gate[:, None, :] * rms_norm(y, post_norm_g[layer])
    return x
```

```python
from contextlib import ExitStack
import numpy as np
import concourse.bass as bass
import concourse.mybir as mybir
from concourse.bass_utils import run_bass_kernel_spmd

F32 = mybir.dt.float32
BF16 = mybir.dt.bfloat16
I32 = mybir.dt.int32
ALU = mybir.AluOpType
AF = mybir.ActivationFunctionType

D = 1024
NT = 2048
SL = 512
NSL = 4
EPS = 1e-6
PI = 3.141592653589793
TWO_PI = 6.283185307179586

ENGS = ("pe", "dve", "act", "pool", "sp")
SAME_ENGINE_SYNC = True


class Sem:
    def __init__(self, handle, name, is_dma):
        self.h = handle
        self.name = name
        self.total = 0
        self.is_dma = is_dma


class Res:
    __slots__ = ("name", "writer", "readers", "dsem")

    def __init__(self, name):
        self.name = name
        self.writer = None
        self.readers = []
        self.dsem = None


class Buf:
    __slots__ = ("t", "r")

    def __init__(self, t, r):
        self.t = t
        self.r = r


class Prog:
    def __init__(self, nc):
        self.nc = nc
        self.stack = ExitStack()
        self.ops = {e: [] for e in ENGS}
        self.esem = {}
        for e in ENGS:
            self.esem[e] = Sem(self.stack.enter_context(nc.semaphore("s_" + e)), e, False)
        self.waited = {e: {} for e in ENGS}
        self.dma_sems = []
        self.free_dsems = []
        self.all_res = []
        self.nm = 0

    def sbuf(self, name, shape, dtype):
        return self.stack.enter_context(self.nc.sbuf_tensor("sb_" + name, list(shape), dtype))

    def psum(self, name, shape, dtype=F32):
        return self.stack.enter_context(self.nc.psum_tensor("ps_" + name, list(shape), dtype))

    def res(self, name="r"):
        r = Res(name)
        self.all_res.append(r)
        return r

    def buf(self, name, shape, dtype):
        return Buf(self.sbuf(name, shape, dtype), self.res(name))

    def _dsem(self, r):
        if r.dsem is None:
            if self.free_dsems:
                r.dsem = self.free_dsems.pop()
            else:
                self.nm += 1
                nm = "d%d" % self.nm
                r.dsem = Sem(self.stack.enter_context(self.nc.semaphore(nm)), nm, True)
                self.dma_sems.append(r.dsem)
        return r.dsem

    def _collect(self, eng, reads, writes):
        need = {}

        def add(tok):
            if tok is None:
                return
            sem, val, teng = tok
            if teng == eng and (eng == "pe" or not SAME_ENGINE_SYNC):
                return
            if sem.is_dma:
                val = max(val, sem.total)
            if need.get(sem, 0) < val:
                need[sem] = val

        for r in reads:
            add(r.writer)
        for w in writes:
            add(w.writer)
            for t in w.readers:
                add(t)
        wl = []
        wd = self.waited[eng]
        for sem, val in need.items():
            if wd.get(sem, 0) < val:
                wd[sem] = val
                wl.append((sem, val))
        return wl

    def _commit(self, tok, reads, writes):
        for r in reads:
            r.readers.append(tok)
            if len(r.readers) > 64:
                best = {}
                for t in r.readers:
                    if best.get(t[0], (None, 0, None))[1] < t[1]:
                        best[t[0]] = t
                r.readers = list(best.values())
        for w in writes:
            w.writer = tok
            w.readers = []

    def op(self, eng, fn, reads=(), writes=(), inc=True):
        waits = self._collect(eng, reads, writes)
        sem = self.esem[eng]
        if inc:
            sem.total += 1
            tok = (sem, sem.total, eng)
        else:
            tok = (sem, sem.total + 1, eng)
        self.ops[eng].append((waits, fn, (sem, 1) if inc else None))
        self._commit(tok, reads, writes)

    def dma(self, eng, out_ap, in_ap, reads=(), writes=(), sem_res=None):
        waits = self._collect(eng, reads, writes)
        if sem_res is None:
            sem_res = writes[0] if writes else reads[0]
        sem = self._dsem(sem_res)
        sem.total += 16
        tok = (sem, sem.total, "dma")

        def fn(e, out_ap=out_ap, in_ap=in_ap):
            return e.dma_start(out=out_ap, in_=in_ap)

        self.ops[eng].append((waits, fn, (sem, 16)))
        self._commit(tok, reads, writes)

    def cc(self, kind, in_ap, out_ap, groups, reads=(), writes=()):
        waits = self._collect("pool", reads, writes)
        sem = self._dsem(writes[0])
        sem.total += 1
        tok = (sem, sem.total, "dma")

        def fn(e):
            return e.collective_compute(kind, ALU.bypass, replica_groups=groups, ins=[in_ap], outs=[out_ap])

        self.ops["pool"].append((waits, fn, (sem, 1)))
        self._commit(tok, reads, writes)

    def barrier(self):
        for e in ENGS:
            waits = []
            wd = self.waited[e]
            for x in ENGS:
                s = self.esem[x]
                if x != e and s.total > wd.get(s, 0):
                    wd[s] = s.total
                    waits.append((s, s.total))
            for s in self.dma_sems:
                if s.total > wd.get(s, 0):
                    wd[s] = s.total
                    waits.append((s, s.total))
            if waits:
                self.ops[e].append((waits, None, None))
        for r in self.all_res:
            r.writer = None
            r.readers = []
            if r.dsem is not None:
                self.free_dsems.append(r.dsem)
                r.dsem = None

    def emit(self):
        nc = self.nc
        self.barrier()
        engmap = {"pe": "tensor", "dve": "vector", "act": "scalar", "pool": "gpsimd", "sp": "sync"}
        with nc.Block() as block:
            for e in ENGS:
                ops = self.ops[e]

                def body(eobj, ops=ops):
                    for waits, fn, inc in ops:
                        for sem, val in waits:
                            eobj.wait_ge(sem.h, val)
                        if fn is not None:
                            ins = fn(eobj)
                            if inc is not None:
                                ins.then_inc(inc[0].h, inc[1])

                getattr(block, engmap[e])(body)
        self.stack.close()


class Arena:
    def __init__(self, P, name, words):
        self.P = P
        self.t = P.sbuf(name, [128, words], F32)
        self.words = words
        self.off = 0

    def reset(self, mark=0):
        self.off = mark

    def mark(self):
        return self.off

    def take(self, shape, dtype, name="a"):
        n = 1
        for s in shape[1:]:
            n *= s
        esz = 4 if dtype in (F32, I32) else 2
        w = (n * esz + 3) // 4
        w = (w + 7) // 8 * 8
        assert self.off + w <= self.words, ("arena overflow", name, self.off, w, self.words)
        v = self.t[:, self.off:self.off + w]
        self.off += w
        if dtype != F32:
            v = v.bitcast(dtype)
        v = v[:, 0:n]
        if len(shape) == 3:
            v = v.rearrange("p (a b) -> p a b", a=shape[1])
        elif len(shape) == 4:
            v = v.rearrange("p (a b c) -> p a b c", a=shape[1], b=shape[2])
        if shape[0] != 128:
            v = v[0:shape[0]]
        return Buf(v, self.P.res(name))


class Ctx:
    def __init__(self, nc):
        self.nc = nc
        self.P = Prog(nc)
        P = self.P
        self.banks = [Buf(P.psum("bank%d" % i, [128, 512]), P.res("bank%d" % i)) for i in range(8)]
        self.bi = 0
        self.nbanks_rr = 8

    def nb(self):
        b = self.banks[self.bi % self.nbanks_rr]
        self.bi += 1
        return b

    def tt(self, eng, out, in0, in1, op, reads, writes):
        self.P.op(eng, lambda e: e.tensor_tensor(out=out, in0=in0, in1=in1, op=op), reads, writes)

    def ts(self, eng, out, in0, s1, op0, reads, writes, s2=None, op1=None):
        if op1 is None:
            self.P.op(eng, lambda e: e.tensor_scalar(out=out, in0=in0, scalar1=s1, scalar2=None, op0=op0), reads, writes)
        else:
            self.P.op(eng, lambda e: e.tensor_scalar(out=out, in0=in0, scalar1=s1, scalar2=s2, op0=op0, op1=op1), reads, writes)

    def stt(self, out, in0, scalar, in1, op0, op1, reads, writes):
        self.P.op("dve", lambda e: e.scalar_tensor_tensor(out=out, in0=in0, scalar=scalar, in1=in1, op0=op0, op1=op1), reads, writes)

    def act(self, out, in_, func, reads, writes, bias=None, scale=None):
        kw = {}
        if bias is not None:
            kw["bias"] = bias
        if scale is not None:
            kw["scale"] = scale
        self.P.op("act", lambda e: e.activation(out=out, in_=in_, func=func, **kw), reads, writes)

    def copy(self, eng, out, in_, reads, writes):
        if eng == "act":
            self.P.op("act", lambda e: e.activation(out=out, in_=in_, func=AF.Copy), reads, writes)
        else:
            self.P.op(eng, lambda e: e.tensor_copy(out=out, in_=in_), reads, writes)

    def memset(self, eng, ap, val, writes):
        self.P.op(eng, lambda e: e.memset(ap, val), (), writes)

    def mm(self, out, lhsT, rhs, start, stop, reads, writes, inc=None):
        if inc is None:
            inc = stop
        self.P.op("pe", lambda e: e.matmul(out, lhsT=lhsT, rhs=rhs, start=start, stop=stop), reads, writes, inc=inc)

    def load(self, buf_ap, dram_ap, writes, eng="sp"):
        self.P.dma(eng, buf_ap, dram_ap, writes=writes)

    def setup_consts(self, ident_dram, hflag_d, invf_d, sgn_d, kb_d):
        P = self.P
        self.ones_bf = P.buf("ones_bf", [128, 128], BF16)
        self.memset("pool", self.ones_bf.t[:], 1.0, [self.ones_bf.r])
        self.ones_f = P.buf("ones_f", [128, 2], F32)
        self.memset("pool", self.ones_f.t[:], 1.0, [self.ones_f.r])
        self.epsb = P.buf("epsb", [128, 1], F32)
        self.memset("pool", self.epsb.t[:], EPS, [self.epsb.r])
        self.ident = P.buf("ident_bf", [128, 128], BF16)
        P.dma("pool", self.ident.t[:], ident_dram, writes=[self.ident.r])
        self.hf = P.buf("hf", [128, 1], F32)
        self.invf = P.buf("invf", [32, 1], F32)
        self.sgn = P.buf("sgn", [32, 1], F32)
        self.keyb = P.buf("keyb", [128, 32], F32)
        for b_, d_ in ((self.hf, hflag_d), (self.invf, invf_d), (self.sgn, sgn_d), (self.keyb, kb_d)):
            self.load(b_.t[:], d_, [b_.r])

    def ada_setup(self, cT_d):
        P = self.P
        self.cT = P.buf("cT", [128, 8], F32)
        self.cact = P.buf("cact", [128, 8], F32)
        self.load(self.cT.t[:], cT_d, [self.cT.r])
        self.act(self.cact.t[:], self.cT.t[:], AF.Silu, [self.cT.r], [self.cact.r])
        self.ada_sets = []
        for l in range(4):
            d = {k: P.buf("%s_%d" % (k, l), [128, n], F32) for k, n in
                 (("adab", 24), ("preg", 8), ("postg", 8), ("modT", 24), ("A", 8), ("G2", 8))}
            d["done"] = False
            self.ada_sets.append(d)

    def ada_select(self, l):
        d = self.ada_sets[l]
        self.adab, self.preg, self.postg = d["adab"], d["preg"], d["postg"]
        self.modT, self.A, self.G2 = d["modT"], d["A"], d["G2"]
        return d

    def ada_steps(self, l, io, wb, acc):
        d = self.ada_sets[l]
        adab, preg, postg, modT, A, G2 = d["adab"], d["preg"], d["postg"], d["modT"], d["A"], d["G2"]
        cact = self.cact
        self.load(adab.t[:], io["adab"], [adab.r])
        self.load(preg.t[:], io["preg"], [preg.r])
        self.load(postg.t[:], io["postg"], [postg.r])
        nb_ = len(wb)

        def dve_piece(kc):
            b = wb[kc % nb_]
            if kc == 0:
                self.ts("dve", acc.t[:], b.t[:], cact.t[:, 0:1], ALU.mult, [b.r, cact.r], [acc.r])
            else:
                self.stt(acc.t[:], b.t[:], cact.t[:, kc:kc + 1], acc.t[:], ALU.mult, ALU.add, [b.r, cact.r, acc.r], [acc.r])

        for g in range(0, 8, nb_):
            for kc in range(g, min(8, g + nb_)):
                b = wb[kc % nb_]
                self.load(b.t[:], io["adaw"][:, kc, :], [b.r])
            yield
            for kc in range(g, min(8, g + nb_)):
                dve_piece(kc)
        yield
        bank = self.nb()
        for m in range(24):
            self.mm(bank.t[:, 2 * m:2 * m + 2], acc.t[:, m * 128:(m + 1) * 128], self.ones_f.t[:, 0:2], True, True,
                    [acc.r, self.ones_f.r], [bank.r], inc=True)
        pv = bank.t[:, 0:48].rearrange("p (m t) -> p m t", t=2)[:, :, 0]
        self.tt("dve", modT.t[:], pv, adab.t[:], ALU.add, [bank.r, adab.r], [modT.r])
        self.stt(A.t[:], modT.t[:, 8:16], 1.0, preg.t[:], ALU.add, ALU.mult, [modT.r, preg.r], [A.r])
        self.tt("dve", G2.t[:], modT.t[:, 16:24], postg.t[:], ALU.mult, [modT.r, postg.r], [G2.r])
        d["done"] = True
        yield

    def ada_phase(self, ar, l, io):
        if self.ada_sets[l]["done"]:
            return False
        wb = [ar.take([128, 3072], F32, "adaw%d" % i) for i in range(4)]
        acc = ar.take([128, 3072], F32, "adaacc")
        for _ in self.ada_steps(l, io, wb, acc):
            pass
        return True

    def rstd_from(self, ssbank, n, dim, out):
        self.act(out.t[:, 0:n], ssbank.t[:, 0:n], AF.Ln, [ssbank.r, self.epsb.r], [out.r], bias=self.epsb.t[:], scale=1.0 / dim)
        self.act(out.t[:, 0:n], out.t[:, 0:n], AF.Exp, [out.r], [out.r], scale=-0.5)

    def prenorm(self, x_d, n, h, xcs, xsqs, rstd, tmps, xres=()):
        ss = self.nb()
        for c in range(8):
            xc = xcs[c % len(xcs)]
            self.P.dma("sp", xc.t[:, 0:n], x_d[:, c, :], reads=list(xres), writes=[xc.r], sem_res=xc.r)
            sq = xsqs[c % len(xsqs)]
            self.act(sq.t[:, 0:n], xc.t[:, 0:n], AF.Square, [xc.r], [sq.r])
            self.mm(ss.t[:, 0:n], self.ones_bf.t[:], sq.t[:, 0:n], c == 0, c == 7, [self.ones_bf.r, sq.r], [ss.r], inc=True)
        self.rstd_from(ss, n, float(D), rstd)
        for c in range(8):
            xc = xcs[c % len(xcs)]
            self.P.dma("sp", xc.t[:, 0:n], x_d[:, c, :], reads=list(xres), writes=[xc.r], sem_res=xc.r)
            tm = tmps[c % len(tmps)]
            self.stt(tm.t[:, 0:n], xc.t[:, 0:n], self.A.t[:, c:c + 1], rstd.t[:, 0:n], ALU.mult, ALU.mult,
                     [xc.r, self.A.r, rstd.r], [tm.r])
            self.act(h.t[:, c, 0:n], tm.t[:, 0:n], AF.Identity, [tm.r, self.modT.r], [h.r], bias=self.modT.t[:, c:c + 1])

    def out_mm(self, wout, rhs_fn, rhs_res, ysb, ysqs, rstd):
        ss = self.banks[7]
        self.nbanks_rr = 7
        for m in range(8):
            bank = self.nb()
            for k in range(8):
                self.mm(bank.t[:], wout.t[:, k, m * 128:(m + 1) * 128], rhs_fn(k), k == 0, k == 7,
                        [wout.r] + rhs_res, [bank.r])
            self.copy("act", ysb.t[:, m, :], bank.t[:], [bank.r], [ysb.r])
            sq = ysqs[m % len(ysqs)]
            self.act(sq.t[:], bank.t[:], AF.Square, [bank.r], [sq.r])
            self.mm(ss.t[:], self.ones_bf.t[:], sq.t[:], m == 0, m == 7, [self.ones_bf.r, sq.r], [ss.r], inc=True)
        self.rstd_from(ss, SL, float(D), rstd)
        self.nbanks_rr = 8

    def out_res(self, x_d, out_d, ysb, rstd, xcs, tmps, obufs, xres=(), halo_d=None):
        for c in range(8):
            xc = xcs[c % len(xcs)]
            self.P.dma("sp", xc.t[:], x_d[:, c, :], reads=list(xres), writes=[xc.r], sem_res=xc.r)
            tm = tmps[c % len(tmps)]
            self.tt("dve", tm.t[:], ysb.t[:, c, :], rstd.t[:], ALU.mult, [ysb.r, rstd.r], [tm.r])
            ob = obufs[c % len(obufs)]
            self.stt(ob.t[:], tm.t[:], self.G2.t[:, c:c + 1], xc.t[:], ALU.mult, ALU.add, [tm.r, self.G2.r, xc.r], [ob.r])
            self.P.dma("sp", out_d[:, c, :], ob.t[:], reads=[ob.r], sem_res=ob.r)
            if halo_d is not None:
                self.P.dma("sp", halo_d[:, c, :], ob.t[:, SL - 32:SL], reads=[ob.r], sem_res=ob.r)

    def out_phase(self, wout, rhs_fn, rhs_res, x_d, out_d, ysb, ysqs, rstd, xcs, tmps, obufs, xres=(), ores=(), halo_d=None):
        self.out_mm(wout, rhs_fn, rhs_res, ysb, ysqs, rstd)
        self.out_res(x_d, out_d, ysb, rstd, xcs, tmps, obufs, xres=xres, halo_d=halo_d)

    def load_weight_bf16(self, wbuf_ap_fn, dram, nk, ncols, res):
        for k in range(nk):
            c0 = 0
            while c0 < ncols:
                c1 = min(ncols, c0 + 2048)
                self.P.dma("pool", wbuf_ap_fn(k, c0, c1), dram[:, k, c0:c1], writes=[res])
                c0 = c1


def dram_in(nc, name, shape, dtype=F32):
    return nc.dram_tensor(name, list(shape), dtype, kind="ExternalInput").ap()


def emit_odd(cx, ar, io):
    P = cx.P
    xT, xH, outT = io["xT"], io["xH"], io["out"]
    xHres = io.get("xHres", ())
    P.barrier()
    ar.reset()
    cx.ada_select(io["layer"])
    win = ar.take([128, 8, 3072], BF16, "win")
    wout = ar.take([128, 8, 1024], BF16, "wout")
    cw = ar.take([128, 8, 31], F32, "cw")
    cb = ar.take([128, 8], F32, "cb")
    lg = ar.take([128, 8], F32, "lg")
    lb = ar.take([128, 8], F32, "lb")
    base = ar.mark()
    for pre in io.get("pre", ()):
        pre()
    cx.load_weight_bf16(lambda k, a, b: win.t[:, k, a:b], io["w_in"], 8, 3072, win.r)
    cx.load_weight_bf16(lambda k, a, b: wout.t[:, k, a:b], io["w_out"], 8, 1024, wout.r)
    for b_, d_ in ((cw, io["convw"]), (cb, io["convb"]), (lg, io["lng"]), (lb, io["lnb"])):
        cx.load(b_.t[:], d_, [b_.r])
    hf = cx.hf
    if cx.ada_phase(ar, io["layer"], io):
        P.barrier()
    ar.reset(base)

    xcs = [ar.take([128, 512], F32, "xc%d" % i) for i in range(3)]
    xsqs = [ar.take([128, 512], BF16, "xsq%d" % i) for i in range(2)]
    tmps = [ar.take([128, 512], F32, "tmp%d" % i) for i in range(3)]
    rstd = ar.take([128, 512], F32, "rstd")
    hs = [ar.take([128, 8, 512], BF16, "h%d" % i) for i in range(2)]
    u = [ar.take([128, 544], BF16, "u%d" % j) for j in range(8)]
    convs = [ar.take([128, 8, 512], F32, "conv%d" % i) for i in range(2)]
    sgates = [ar.take([128, 8, 512], BF16, "sgate%d" % i) for i in range(2)]
    gated = ar.take([128, 8, 512], BF16, "gated")
    diag = [ar.take([128, 31, 128], BF16, "diag%d" % i) for i in range(2)]
    cbf = [ar.take([128, 512], BF16, "cbf%d" % i) for i in range(2)]
    csq = [ar.take([128, 512], BF16, "csq%d" % i) for i in range(2)]
    means = [ar.take([128, 512], F32, "mean%d" % i) for i in range(2)]
    vars_ = [ar.take([128, 512], F32, "var%d" % i) for i in range(2)]
    obufs = [ar.take([128, 512], F32, "ob%d" % i) for i in range(2)]
    rstd_o = ar.take([128, 512], F32, "rstd_o")
    ysqs_o = [ar.take([128, 512], BF16, "ysqo%d" % i) for i in range(2)]

    hh = hs[1]
    cx.prenorm(xH, 32, hh, xcs, xsqs, rstd, tmps, xres=xHres)
    for j in range(8):
        bv = cx.nb()
        bg = cx.nb()
        for k in range(8):
            cx.mm(bv.t[:, 0:32], win.t[:, k, j * 128:(j + 1) * 128], hh.t[:, k, 0:32], k == 0, k == 7, [win.r, hh.r], [bv.r])
        for k in range(8):
            cx.mm(bg.t[:, 0:32], win.t[:, k, 1024 + j * 128:1024 + (j + 1) * 128], hh.t[:, k, 0:32], k == 0, k == 7, [win.r, hh.r], [bg.r])
        tm = tmps[j % 3]
        cx.act(tm.t[:, 0:32], bg.t[:, 0:32], AF.Sigmoid, [bg.r], [tm.r])
        cx.tt("dve", tm.t[:, 0:32], bv.t[:, 0:32], tm.t[:, 0:32], ALU.mult, [bv.r, tm.r], [tm.r])
        cx.ts("dve", u[j].t[:, 512:542], tm.t[:, 2:32], hf.t[:, 0:1], ALU.mult, [tm.r, hf.r], [u[j].r])

    def A_pre(s):
        cx.prenorm(xT[:, :, s * SL:(s + 1) * SL], SL, hs[s % 2], xcs, xsqs, rstd, tmps)

    def A_body(s, between=None):
        conv, sgate, mean, var = convs[s % 2], sgates[s % 2], means[s % 2], vars_[s % 2]
        h = hs[s % 2]
        mbank = cx.banks[6]
        ebank = cx.banks[7]
        cx.nbanks_rr = 6
        wb = {}

        def emit_win(j):
            b0 = 3 * (j % 2)
            bv, bg, bs = cx.banks[b0], cx.banks[b0 + 1], cx.banks[b0 + 2]
            for (bk, off) in ((bv, 0), (bg, 1024), (bs, 2048)):
                for k in range(8):
                    cx.mm(bk.t[:], win.t[:, k, off + j * 128:off + (j + 1) * 128], h.t[:, k, :], k == 0, k == 7, [win.r, h.r], [bk.r])
            dg = diag[j % 2]
            cx.tt("dve", dg.t[:], cx.ident.t[:].unsqueeze(1).to_broadcast([128, 31, 128]),
                  cw.t[:, j, :].unsqueeze(2).to_broadcast([128, 31, 128]), ALU.mult, [cx.ident.r, cw.r], [dg.r])
            wb[j] = (bv, bg, bs, dg)

        emit_win(0)
        for j in range(8):
            if j + 1 < 8:
                emit_win(j + 1)
            bv, bg, bs, dg = wb.pop(j)
            cx.copy("pool", u[j].t[:, 0:30], u[j].t[:, 512:542], [u[j].r], [u[j].r])
            tm = tmps[j % 3]
            cx.act(tm.t[:], bg.t[:], AF.Sigmoid, [bg.r], [tm.r])
            cx.tt("dve", u[j].t[:, 30:542], bv.t[:], tm.t[:], ALU.mult, [bv.r, tm.r], [u[j].r])
            cx.act(sgate.t[:, j, :], bs.t[:], AF.Silu, [bs.r], [sgate.r])
            cb_ = bg
            for k in range(31):
                cx.mm(cb_.t[:], dg.t[:, k, :], u[j].t[:, k:k + 512], k == 0, k == 30, [dg.r, u[j].r], [cb_.r])
            cx.act(conv.t[:, j, :], cb_.t[:], AF.Identity, [cb_.r, cb.r], [conv.r], bias=cb.t[:, j:j + 1])
            q_ = csq[j % 2]
            cx.act(q_.t[:], cb_.t[:], AF.Square, [cb_.r, cb.r], [q_.r], bias=cb.t[:, j:j + 1])
            c_ = cbf[j % 2]
            cx.copy("dve", c_.t[:], conv.t[:, j, :], [conv.r], [c_.r])
            cx.mm(mbank.t[:], cx.ones_bf.t[:], c_.t[:], j == 0, j == 7, [cx.ones_bf.r, c_.r], [mbank.r], inc=True)
            cx.mm(ebank.t[:], cx.ones_bf.t[:], q_.t[:], j == 0, j == 7, [cx.ones_bf.r, q_.r], [ebank.r], inc=True)
            if between is not None:
                between(j)
        cx.nbanks_rr = 8
        cx.ts("dve", mean.t[:], mbank.t[:], 1.0 / D, ALU.mult, [mbank.r], [mean.r])
        cx.tt("dve", var.t[:], mean.t[:], mean.t[:], ALU.mult, [mean.r], [var.r])
        cx.stt(var.t[:], ebank.t[:], 1.0 / D, var.t[:], ALU.mult, ALU.subtract, [ebank.r, var.r], [var.r])
        cx.act(var.t[:], var.t[:], AF.Ln, [var.r, cx.epsb.r], [var.r], bias=cx.epsb.t[:])
        cx.act(var.t[:], var.t[:], AF.Exp, [var.r], [var.r], scale=-0.5)
    def B_norm_j(s, j):
        conv, sgate, mean, var = convs[s % 2], sgates[s % 2], means[s % 2], vars_[s % 2]
        tm = tmps[j % 3]
        cx.tt("dve", tm.t[:], conv.t[:, j, :], mean.t[:], ALU.subtract, [conv.r, mean.r], [tm.r])
        cx.tt("dve", tm.t[:], tm.t[:], var.t[:], ALU.mult, [tm.r, var.r], [tm.r])
        cx.act(tm.t[:], tm.t[:], AF.Silu, [tm.r, lg.r, lb.r], [tm.r], bias=lb.t[:, j:j + 1], scale=lg.t[:, j:j + 1])
        cx.tt("pool", gated.t[:, j, :], tm.t[:], sgate.t[:, j, :], ALU.mult, [tm.r, sgate.r], [gated.r])

    def B_mm(s):
        cx.out_mm(wout, lambda k: gated.t[:, k, :], [gated.r], convs[s % 2], ysqs_o, rstd_o)

    def B_res(s):
        cx.out_res(xT[:, :, s * SL:(s + 1) * SL], outT[:, :, s * SL:(s + 1) * SL], convs[s % 2], rstd_o, xcs, tmps, obufs,
                   halo_d=(io.get("halo_out") if s == NSL - 1 else None))

    A_pre(0)
    A_body(0)
    A_pre(1)
    for s in range(1, NSL):
        A_body(s, between=lambda j, s=s: B_norm_j(s - 1, j))
        B_mm(s - 1)
        if s + 1 < NSL:
            A_pre(s + 1)
        B_res(s - 1)
    for j in range(8):
        B_norm_j(NSL - 1, j)
    B_mm(NSL - 1)
    B_res(NSL - 1)


def emit_even(cx, ar, io):
    P = cx.P
    xT, xF, xH, outT = io["xT"], io["xF"], io["xH"], io["out"]
    xFres = io.get("xFres", ())
    posO, posF = io["posO"], io["posF"]
    SCALE = 1.0 / float(np.sqrt(96.0))
    hf, invf, sgn, keyb = cx.hf, cx.invf, cx.sgn, cx.keyb
    P.barrier()
    ar.reset()
    cx.ada_select(io["layer"])
    wuq = ar.take([128, 2, 1024], BF16, "wuq")
    wkk = ar.take([128, 512], BF16, "wkk")
    wvv = ar.take([128, 512], BF16, "wvv")
    scw = ar.take([128, 4, 3], F32, "scw")
    scb = ar.take([128, 4], F32, "scb")
    qg = ar.take([128, 2], F32, "qg")
    kvg = ar.take([128, 1], F32, "kvg")
    cxh = ar.take([128, 4, 2], F32, "cxh")
    ckvn_t = ar.take([128, 4096], BF16, "ckvn_all").t
    ckvn_r = [P.res("ckvn%d" % i) for i in range(8)]
    krope_t = ar.take([128, 4096], BF16, "krope_all").t
    krope_r = [P.res("krope%d" % i) for i in range(8)]
    ya_t = ar.take([128, 4, NT], BF16, "ya_all").t
    ya_r = [P.res("ya%d" % i) for i in range(4)]
    sbg_t = ar.take([128, 4, NT], BF16, "sbg_all").t
    sbg_r = [P.res("sbg%d" % i) for i in range(4)]
    q_t = ar.take([128, 8, NT], BF16, "q_all").t
    q_r = [P.res("q%d" % i) for i in range(4)]
    base = ar.mark()
    for pre in io.get("pre", ()):
        pre()
    cx.load_weight_bf16(lambda k, a, b: wuq.t[:, k, a:b], io["w_uq"], 2, 1024, wuq.r)
    cx.load_weight_bf16(lambda k, a, b: wkk.t[:, a:b], io["w_ukvk"], 1, 512, wkk.r)
    cx.load_weight_bf16(lambda k, a, b: wvv.t[:, a:b], io["w_ukvv"], 1, 512, wvv.r)
    for b_, d_ in ((scw, io["scw"]), (scb, io["scb"]), (qg, io["qg"]), (kvg, io["kvg"])):
        cx.load(b_.t[:], d_, [b_.r])
    win = ar.take([128, 8, 3008], BF16, "win")
    base1 = ar.mark()
    cx.load_weight_bf16(lambda k, a, b: win.t[:, k, a:b], io["w_in"], 8, 3008, win.r)
    if cx.ada_phase(ar, io["layer"], io):
        P.barrier()
    ar.reset(base1)

    xcs = [ar.take([128, 512], F32, "xc%d" % i) for i in range(2)]
    xsqs = [ar.take([128, 512], BF16, "xsq%d" % i) for i in range(2)]
    tmps = [ar.take([128, 512], F32, "tmp%d" % i) for i in range(2)]
    tA = [ar.take([128, 512], F32, "tA%d" % i) for i in range(1)]
    tB = [ar.take([128, 512], F32, "tB%d" % i) for i in range(1)]
    acc2 = [ar.take([128, 512], F32, "acc%d" % i) for i in range(1)]
    rstd = ar.take([128, 512], F32, "rstd")
    rstd2 = rstd
    hs = [ar.take([128, 8, 512], BF16, "h%d" % i) for i in range(2)]
    cxb = [ar.take([128, 514], F32, "cxb%d" % i) for i in range(2)]
    cqn = ar.take([128, 2, 512], BF16, "cqn")
    sqb = [ar.take([128, 512], BF16, "sqb%d" % i) for i in range(2)]
    posi = ar.take([32, 512], I32, "posi")
    ang = ar.take([32, 512], F32, "ang")
    kfl = ar.take([32, 512], F32, "kfl")
    msk = ar.take([32, 512], F32, "msk")
    cos2 = ar.take([32, 512], F32, "cos2")
    sin2 = ar.take([32, 512], F32, "sin2")
    rt1 = [ar.take([128, 512], F32, "rt1_%d" % i) for i in range(1)]
    rt2 = [ar.take([128, 512], F32, "rt2_%d" % i) for i in range(1)]

    def rope_tables(pos_d, c0):
        P.dma("sp", posi.t[:], pos_d[0:1, c0:c0 + SL].partition_broadcast(32), writes=[posi.r])
        cx.copy("dve", ang.t[:], posi.t[:], [posi.r], [ang.r])
        cx.ts("dve", ang.t[:], ang.t[:], invf.t[:, 0:1], ALU.mult, [ang.r, invf.r], [ang.r])
        cx.ts("dve", posi.t[:], ang.t[:], 1.0 / TWO_PI, ALU.mult, [ang.r], [posi.r])
        cx.copy("dve", kfl.t[:], posi.t[:], [posi.r], [kfl.r])
        cx.stt(ang.t[:], kfl.t[:], -TWO_PI, ang.t[:], ALU.mult, ALU.add, [kfl.r, ang.r], [ang.r])
        cx.ts("dve", msk.t[:], ang.t[:], PI, ALU.is_gt, [ang.r], [msk.r], s2=-TWO_PI, op1=ALU.mult)
        cx.tt("dve", ang.t[:], ang.t[:], msk.t[:], ALU.add, [ang.r, msk.r], [ang.r])
        cx.ts("dve", msk.t[:], ang.t[:], -PI, ALU.is_lt, [ang.r], [msk.r], s2=TWO_PI, op1=ALU.mult)
        cx.tt("dve", ang.t[:], ang.t[:], msk.t[:], ALU.add, [ang.r, msk.r], [ang.r])
        cx.act(sin2.t[:], ang.t[:], AF.Sin, [ang.r], [sin2.r])
        cx.ts("dve", sin2.t[:], sin2.t[:], sgn.t[:, 0:1], ALU.mult, [sin2.r, sgn.r], [sin2.r])
        cx.ts("dve", kfl.t[:], ang.t[:], PI / 2, ALU.add, [ang.r], [kfl.r])
        cx.ts("dve", msk.t[:], kfl.t[:], PI, ALU.is_gt, [kfl.r], [msk.r], s2=-TWO_PI, op1=ALU.mult)
        cx.tt("dve", kfl.t[:], kfl.t[:], msk.t[:], ALU.add, [kfl.r, msk.r], [kfl.r])
        cx.act(cos2.t[:], kfl.t[:], AF.Sin, [kfl.r], [cos2.r])

    def kv_group(h, kcol0, blk):
        bkv = cx.nb()
        bkx = cx.nb()
        bky = cx.nb()
        for k in range(8):
            cx.mm(bkv.t[:], win.t[:, k, 2304:2432], h.t[:, k, :], k == 0, k == 7, [win.r, h.r], [bkv.r])
        for k in range(8):
            cx.mm(bkx.t[64:96, :], win.t[:, k, 2432:2464], h.t[:, k, :], k == 0, k == 7, [win.r, h.r], [bkx.r])
        for k in range(8):
            cx.mm(bky.t[64:96, :], win.t[:, k, 2976:3008], h.t[:, k, :], k == 0, k == 7, [win.r, h.r], [bky.r])
        sq = sqb[0]
        cx.act(sq.t[:], bkv.t[:], AF.Square, [bkv.r], [sq.r])
        ss = cx.nb()
        cx.mm(ss.t[:], cx.ones_bf.t[:], sq.t[:], True, True, [cx.ones_bf.r, sq.r], [ss.r])
        cx.rstd_from(ss, SL, 128.0, rstd2)
        cx.stt(ckvn_t[:, kcol0:kcol0 + SL], bkv.t[:], kvg.t[:, 0:1], rstd2.t[:], ALU.mult, ALU.mult,
               [bkv.r, kvg.r, rstd2.r], [ckvn_r[blk]])
        a, b = rt1[0], rt2[0]
        cx.tt("dve", a.t[64:96, :], bkx.t[64:96, :], cos2.t[:], ALU.mult, [bkx.r, cos2.r], [a.r])
        cx.tt("dve", b.t[64:96, :], bky.t[64:96, :], sin2.t[:], ALU.mult, [bky.r, sin2.r], [b.r])
        cx.tt("pool", krope_t[64:96, kcol0:kcol0 + SL], a.t[64:96, :], b.t[64:96, :], ALU.add, [a.r, b.r], [krope_r[blk]])

    hh = hs[1]
    cx.prenorm(xH, 32, hh, xcs, xsqs, rstd, tmps, xres=xFres)
    for j in range(4):
        bc = cx.nb()
        bx = cx.nb()
        for k in range(8):
            cx.mm(bc.t[:, 0:32], win.t[:, k, 512 + j * 128:512 + (j + 1) * 128], hh.t[:, k, 0:32], k == 0, k == 7, [win.r, hh.r], [bc.r])
        for k in range(8):
            cx.mm(bx.t[:, 0:32], win.t[:, k, 1024 + j * 128:1024 + (j + 1) * 128], hh.t[:, k, 0:32], k == 0, k == 7, [win.r, hh.r], [bx.r])
        tm = tA[0]
        cx.copy("act", tm.t[:, 0:32], bc.t[:, 0:32], [bc.r], [tm.r])
        cx.tt("dve", tm.t[:, 0:32], bx.t[:, 0:32], tm.t[:, 0:32], ALU.mult, [bx.r, tm.r], [tm.r])
        cx.ts("dve", cxh.t[:, j, 0:2], tm.t[:, 30:32], hf.t[:, 0:1], ALU.mult, [tm.r, hf.r], [cxh.r])

    for s in range(NSL):
        h = hs[s % 2]
        c0 = s * SL
        cx.prenorm(xT[:, :, c0:c0 + SL], SL, h, xcs, xsqs, rstd, tmps)
        rope_tables(posO, c0)
        for j in range(4):
            bb, bc, bx, bg = cx.nb(), cx.nb(), cx.nb(), cx.nb()
            for (bk, off) in ((bb, 0), (bc, 512), (bx, 1024), (bg, 1536)):
                for k in range(8):
                    cx.mm(bk.t[:], win.t[:, k, off + j * 128:off + (j + 1) * 128], h.t[:, k, :], k == 0, k == 7, [win.r, h.r], [bk.r])
            ta = tA[0]
            tb = tB[0]
            cb_ = cxb[j % 2]
            ac = acc2[0]
            cx.copy("act", ta.t[:], bc.t[:], [bc.r], [ta.r])
            cx.copy("pool", cb_.t[:, 0:2], cxh.t[:, j, 0:2], [cxh.r], [cb_.r])
            cx.tt("dve", cb_.t[:, 2:514], bx.t[:], ta.t[:], ALU.mult, [bx.r, ta.r], [cb_.r])
            cx.copy("pool", cxh.t[:, j, 0:2], cb_.t[:, 512:514], [cb_.r], [cxh.r])
            cx.ts("dve", ac.t[:], cb_.t[:, 2:514], scw.t[:, j, 2:3], ALU.mult, [cb_.r, scw.r, scb.r], [ac.r], s2=scb.t[:, j:j + 1], op1=ALU.add)
            cx.stt(ac.t[:], cb_.t[:, 1:513], scw.t[:, j, 1:2], ac.t[:], ALU.mult, ALU.add, [cb_.r, scw.r, ac.r], [ac.r])
            cx.stt(ac.t[:], cb_.t[:, 0:512], scw.t[:, j, 0:1], ac.t[:], ALU.mult, ALU.add, [cb_.r, scw.r, ac.r], [ac.r])
            cx.act(tb.t[:], bg.t[:], AF.Silu, [bg.r], [tb.r])
            cx.tt("dve", ac.t[:], ac.t[:], bb.t[:], ALU.mult, [ac.r, bb.r], [ac.r])
            cx.tt("pool", ya_t[:, j, c0:c0 + SL], ac.t[:], tb.t[:], ALU.mult, [ac.r, tb.r], [ya_r[s]])
        kv_group(h, NT + c0, 4 + s)
        bq = [cx.nb(), cx.nb()]
        for i in range(2):
            for k in range(8):
                cx.mm(bq[i].t[:], win.t[:, k, 2048 + i * 128:2048 + (i + 1) * 128], h.t[:, k, :], k == 0, k == 7, [win.r, h.r], [bq[i].r])
        ss = cx.nb()
        for i in range(2):
            sq = sqb[i]
            cx.act(sq.t[:], bq[i].t[:], AF.Square, [bq[i].r], [sq.r])
            cx.mm(ss.t[:], cx.ones_bf.t[:], sq.t[:], i == 0, i == 1, [cx.ones_bf.r, sq.r], [ss.r], inc=True)
        cx.rstd_from(ss, SL, 256.0, rstd2)
        for i in range(2):
            cx.stt(cqn.t[:, i, :], bq[i].t[:], qg.t[:, i:i + 1], rstd2.t[:], ALU.mult, ALU.mult, [bq[i].r, qg.r, rstd2.r], [cqn.r])
        for hd in range(8):
            b1, b2 = cx.nb(), cx.nb()
            for i in range(2):
                cx.mm(b1.t[0:96, :], wuq.t[:, i, hd * 128:hd * 128 + 96], cqn.t[:, i, :], i == 0, i == 1, [wuq.r, cqn.r], [b1.r])
            for i in range(2):
                cx.mm(b2.t[64:96, :], wuq.t[:, i, hd * 128 + 96:hd * 128 + 128], cqn.t[:, i, :], i == 0, i == 1, [wuq.r, cqn.r], [b2.r])
            a, b = rt1[0], rt2[0]
            cx.tt("dve", a.t[64:96, :], b1.t[64:96, :], cos2.t[:], ALU.mult, [b1.r, cos2.r], [a.r])
            cx.tt("dve", b.t[64:96, :], b2.t[64:96, :], sin2.t[:], ALU.mult, [b2.r, sin2.r], [b.r])
            cx.tt("pool", q_t[64:96, hd, c0:c0 + SL], a.t[64:96, :], b.t[64:96, :], ALU.add, [a.r, b.r], [q_r[s]])
            cx.copy("act", q_t[0:64, hd, c0:c0 + SL], b1.t[0:64, :], [b1.r], [q_r[s]])
        for j in range(4):
            bk = cx.nb()
            for k in range(8):
                cx.mm(bk.t[:], win.t[:, k, 2464 + j * 128:2464 + (j + 1) * 128], h.t[:, k, :], k == 0, k == 7, [win.r, h.r], [bk.r])
            cx.act(sbg_t[:, j, c0:c0 + SL], bk.t[:], AF.Silu, [bk.r], [sbg_r[s]])

    for fs in range(NSL):
        h = hs[fs % 2]
        c0 = fs * SL
        cx.prenorm(xF[:, :, c0:c0 + SL], SL, h, xcs, xsqs, rstd, tmps, xres=xFres)
        rope_tables(posF, c0)
        kv_group(h, c0, fs)

    P.barrier()
    ar.reset(base)

    yb = ar.take([128, 4, NT], BF16, "yb_all")
    yb_r = [P.res("yb%d" % i) for i in range(4)]
    mark = ar.mark()
    vp = [ar.take([128, 32, 192], BF16, "vp%d" % i) for i in range(2)]
    kh = [ar.take([128, 4096], BF16, "kh%d" % i) for i in range(2)]
    pts = [ar.take([128, 512], BF16, "pt%d" % i) for i in range(6)]
    rds = [ar.take([128, 512], F32, "rd%d" % i) for i in range(2)]
    tms = [ar.take([128, 512], F32, "tm%d" % i) for i in range(2)]
    for v_ in vp:
        cx.memset("pool", v_.t[:, :, 64:128], 1.0, [v_.r])
    cx.nbanks_rr = 6
    ada_job = None
    if io.get("ada_jobs"):
        awb = [ar.take([128, 3072], F32, "adaw%d" % i) for i in range(2)]
        aacc = ar.take([128, 3072], F32, "adaacc")

        def _jobs():
            for (l2, io2) in io["ada_jobs"]:
                for _ in cx.ada_steps(l2, io2, awb, aacc):
                    yield
        ada_job = _jobs()
    cnt = 0
    pti = 0
    for j in range(4):
        v_ = vp[j % 2]
        for g in range(8):
            bank = cx.nb()
            for i in range(4):
                kb = 4 * g + i
                cx.mm(bank.t[:, i * 128:(i + 1) * 128], ckvn_t[:, kb * 128:(kb + 1) * 128], wvv.t[:, j * 128:(j + 1) * 128], True, True,
                      [ckvn_r[kb // 4], wvv.r], [bank.r], inc=(i == 3))
            bv = bank.t[:, :].rearrange("p (a b) -> p a b", a=4)
            cx.copy("dve", v_.t[:, 4 * g:4 * g + 4, 0:64], bv[:, :, 0:64], [bank.r], [v_.r])
            cx.copy("act", v_.t[:, 4 * g:4 * g + 4, 128:192], bv[:, :, 64:128], [bank.r], [v_.r])
        for hd in (2 * j, 2 * j + 1):
            k_ = kh[hd % 2]
            cx.copy("pool", k_.t[64:96, :], krope_t[64:96, :], krope_r, [k_.r])
            for kc in range(8):
                bank = cx.nb()
                cx.mm(bank.t[0:64, :], wkk.t[:, hd * 64:(hd + 1) * 64], ckvn_t[:, kc * SL:(kc + 1) * SL], True, True,
                      [wkk.r, ckvn_r[kc]], [bank.r])
                cx.copy("dve", k_.t[0:64, kc * SL:(kc + 1) * SL], bank.t[0:64, :], [bank.r], [k_.r])
            vc0 = 0 if hd % 2 == 0 else 64
            for s in range(NSL):
                if ada_job is not None:
                    try:
                        next(ada_job)
                    except StopIteration:
                        ada_job = None
                accb = cx.banks[6 + cnt % 2]
                cnt += 1
                nkb = 16 + 4 * (s + 1)
                LA = 3
                scs = {}

                def emit_sc(kb, s=s, hd=hd, k_=k_, nkb=nkb):
                    jd = kb - (nkb - 4)
                    lo = 128 * jd if jd >= 0 else 0
                    sc = cx.nb()
                    cx.mm(sc.t[:, lo:SL], k_.t[0:96, kb * 128:(kb + 1) * 128], q_t[0:96, hd, s * SL + lo:(s + 1) * SL], True, True,
                          [k_.r, q_r[s]], [sc.r])
                    scs[kb] = (sc, lo, jd)

                for kb in range(min(LA, nkb)):
                    emit_sc(kb)
                for kb in range(nkb):
                    sc, lo, jd = scs.pop(kb)
                    pt = pts[pti % 6]
                    pti += 1
                    cx.act(pt.t[:, lo:SL], sc.t[:, lo:SL], AF.Exp, [sc.r, keyb.r], [pt.r], bias=keyb.t[:, kb:kb + 1], scale=SCALE)
                    if jd >= 0:
                        cx.memset("dve", pt.t[64:128, lo:lo + 64], 0.0, [pt.r])
                    if kb + LA < nkb:
                        emit_sc(kb + LA)
                    cx.mm(accb.t[:, lo:SL], v_.t[:, kb, vc0:vc0 + 128], pt.t[:, lo:SL], kb == 0, kb == nkb - 1,
                          [v_.r, pt.r], [accb.r], inc=True)
                if hd % 2 == 0:
                    npart, dpart = slice(0, 64), slice(64, 128)
                else:
                    npart, dpart = slice(64, 128), slice(0, 64)
                rd = rds[cnt % 2]
                tm = tms[cnt % 2]
                P.op("dve", lambda e, rd=rd, accb=accb, dpart=dpart: e.reciprocal(out=rd.t[dpart, :], in_=accb.t[dpart, :]), [accb.r], [rd.r])
                cx.tt("dve", tm.t[npart, :], accb.t[npart, :], rd.t[dpart, :], ALU.mult, [accb.r, rd.r], [tm.r])
                cx.tt("pool", yb.t[npart, j, s * SL:(s + 1) * SL], tm.t[npart, :], sbg_t[npart, j, s * SL:(s + 1) * SL], ALU.mult,
                      [tm.r, sbg_r[s]], [yb_r[s]])
    cx.nbanks_rr = 8
    if ada_job is not None:
        for _ in ada_job:
            pass

    P.barrier()
    ar.reset(mark)

    wout = ar.take([128, 8, 1024], BF16, "wout")
    cx.load_weight_bf16(lambda k, a, b: wout.t[:, k, a:b], io["w_out"], 8, 1024, wout.r)
    ysbs = [ar.take([128, 8, 512], F32, "ysb%d" % i) for i in range(2)]
    ysqs = [ar.take([128, 512], BF16, "ysq%d" % i) for i in range(2)]
    xcs = [ar.take([128, 512], F32, "xc%d" % i) for i in range(6)]
    tmps = [ar.take([128, 512], F32, "tmp%d" % i) for i in range(2)]
    obufs = [ar.take([128, 512], F32, "ob%d" % i) for i in range(3)]
    rstds = [ar.take([128, 512], F32, "rstd%d" % i) for i in range(2)]
    for s in range(NSL):
        ysb = ysbs[s % 2]
        rstd = rstds[s % 2]
        c0 = s * SL

        def rhs_fn(k, c0=c0):
            if k < 4:
                return ya_t[:, k, c0:c0 + SL]
            return yb.t[:, k - 4, c0:c0 + SL]

        cx.out_phase(wout, rhs_fn, [ya_r[s], yb_r[s]], xT[:, :, c0:c0 + SL], outT[:, :, c0:c0 + SL],
                     ysb, ysqs, rstd, xcs, tmps, obufs, halo_d=(io.get("halo_out") if s == NSL - 1 else None))


PAIRS = [[0, 1], [2, 3], [4, 5], [6, 7]]
_DBG_NOCC = False
LAYER_KEYS_EVEN = ["w_in", "w_uq", "w_ukvk", "w_ukvv", "w_out", "scw", "scb", "qg", "kvg"]
LAYER_SHAPES_EVEN = {"w_in": [128, 8, 3008], "w_uq": [128, 2, 1024], "w_ukvk": [128, 1, 512], "w_ukvv": [128, 1, 512],
                     "w_out": [128, 8, 1024], "scw": [128, 4, 3], "scb": [128, 4], "qg": [128, 2], "kvg": [128, 1]}
LAYER_SHAPES_ODD = {"w_in": [128, 8, 3072], "w_out": [128, 8, 1024], "convw": [128, 8, 31], "convb": [128, 8],
                    "lng": [128, 8], "lnb": [128, 8]}
LAYER_SHAPES_COMMON = {"adaw": [128, 8, 3072], "adab": [128, 24], "preg": [128, 8], "postg": [128, 8]}


def build_fused():
    nc = bass.Bass("TRN2", target_bir_lowering=False)
    xT = dram_in(nc, "xT", [128, 8, NT])
    xF0 = dram_in(nc, "xF", [128, 8, NT])
    xH0 = dram_in(nc, "xH", [128, 8, 32])
    posO = dram_in(nc, "posO", [1, NT], I32)
    posF = dram_in(nc, "posF", [1, NT], I32)
    cT = dram_in(nc, "cT", [128, 8])
    hflag = dram_in(nc, "hflag", [128, 1])
    invf_d = dram_in(nc, "invf2", [32, 1])
    sgn_d = dram_in(nc, "sgn2", [32, 1])
    kb_d = dram_in(nc, "keybias", [128, 32])
    identd = dram_in(nc, "ident", [128, 128])
    L = []
    for l in range(4):
        d = {}
        shapes = dict(LAYER_SHAPES_COMMON)
        shapes.update(LAYER_SHAPES_EVEN if l % 2 == 0 else LAYER_SHAPES_ODD)
        for k, shp in shapes.items():
            d[k] = dram_in(nc, "L%d_%s" % (l, k), shp)
        L.append(d)
    outT = nc.dram_tensor("outT", [128, 8, NT], F32, kind="ExternalOutput").ap()
    xs_raw = [nc.dram_tensor("xs%d" % l, [8, 128, NT], F32, kind="Internal").ap() for l in range(3)]
    xs = [t_.rearrange("c p t -> p c t") for t_ in xs_raw]
    hal = [nc.dram_tensor("hal%d" % l, [128, 8, 32], F32, kind="Internal").ap() for l in range(3)]
    halg = [nc.dram_tensor("halg%d" % l, [2, 128, 8, 32], F32, kind="Internal").ap() for l in range(3)]
    xg = nc.dram_tensor("xg", [8, 2, 128, NT], F32, kind="Internal").ap()

    cx = Ctx(nc)
    P = cx.P
    cx.setup_consts(identd, hflag, invf_d, sgn_d, kb_d)
    cx.ada_setup(cT)
    ar = Arena(P, "arena", 49 * 1024 + 512)
    Rhalg = [P.res("halg%d" % l) for l in range(3)]
    Rxg = P.res("xg")
    Rsrc = P.res("ccsrc")

    for l in range(4):
        io = dict(L[l])
        io["cT"] = cT
        io["layer"] = l
        if l == 0:
            io["ada_jobs"] = [(l2, L[l2]) for l2 in (1, 2, 3)]
        io["xT"] = xT if l == 0 else xs[l - 1]
        io["out"] = outT if l == 3 else xs[l]
        if l < 3:
            io["halo_out"] = hal[l]
        pre = []
        if l == 0 or _DBG_NOCC:
            io["xF"], io["xH"] = xF0, xH0
        else:
            def gather_halo(l=l):
                P.cc("AllGather", hal[l - 1].rearrange("p c t -> p (c t)"), halg[l - 1].rearrange("r p c t -> (r p) (c t)"), PAIRS, reads=[Rsrc], writes=[Rhalg[l - 1]])
            pre.append(gather_halo)
            io["xH"] = halg[l - 1][0]
            io["xHres"] = [Rhalg[l - 1]]
            if l == 2:
                def gather_x():
                    for c_ in range(8):
                        P.cc("AllGather", xs_raw[1][c_], xg[c_].rearrange("r p t -> (r p) t"), PAIRS, reads=[Rsrc], writes=[Rxg])
                pre.append(gather_x)
                io["xF"] = xg[:, 0].rearrange("c p t -> p c t")
                io["xFres"] = [Rxg, Rhalg[l - 1]]
        io["pre"] = pre
        io["posO"], io["posF"] = posO, posF
        if l % 2 == 0:
            emit_even(cx, ar, io)
        else:
            emit_odd(cx, ar, io)
    P.emit()
    return nc


def fm(v):
    v = np.asarray(v)
    return np.ascontiguousarray(v.reshape(-1, 128).T)


def wk(w):
    w = np.asarray(w)
    k, n = w.shape
    return np.ascontiguousarray(w.reshape(k // 128, 128, n).transpose(1, 0, 2))


def xfm(xb):
    t = xb.shape[0]
    return np.ascontiguousarray(xb.T.reshape(8, 128, t).transpose(1, 0, 2))


def xfm_inv(o):
    t = o.shape[2]
    return o.transpose(1, 0, 2).reshape(D, t).T


def _perm_uq(w_uq):
    w = np.asarray(w_uq)
    blocks = []
    for h in range(8):
        b = h * 96
        blocks += [w[:, b:b + 64], w[:, b + 64:b + 96], w[:, b + 80:b + 96], w[:, b + 64:b + 80]]
    return np.concatenate(blocks, axis=1)


_NC_CACHE = {}


def kernel(x, c, positions, ada_w, ada_b, pre_norm_g, post_norm_g,
           even_w_in, even_sc_conv_w, even_sc_conv_b, even_q_norm_g, even_kv_norm_g,
           even_w_uq, even_w_ukv, even_w_out,
           odd_w_in, odd_conv_w, odd_conv_b, odd_ln_g, odd_ln_b, odd_w_out):
    x = np.ascontiguousarray(np.asarray(x, dtype=np.float32))
    c = np.asarray(c, dtype=np.float32)
    positions = np.asarray(positions)
    if "fused" not in _NC_CACHE:
        _NC_CACHE["fused"] = build_fused()
    nc = _NC_CACHE["fused"]
    invf = (np.float32(1.0) / (np.float32(10000.0) ** (np.arange(0, 32, 2, dtype=np.float32) / np.float32(32)))).astype(np.float32)
    shared = {
        "ident": np.eye(128, dtype=np.float32),
        "invf2": np.ascontiguousarray(np.concatenate([invf, invf])[:, None]),
        "sgn2": np.ascontiguousarray(np.concatenate([-np.ones(16, np.float32), np.ones(16, np.float32)])[:, None]),
    }
    for l in range(4):
        i = l // 2
        p = "L%d_" % l
        shared[p + "adaw"] = wk(ada_w[l])
        shared[p + "adab"] = fm(ada_b[l])
        shared[p + "preg"] = fm(pre_norm_g[l])
        shared[p + "postg"] = fm(post_norm_g[l])
        if l % 2 == 0:
            w_in = np.asarray(even_w_in[i])
            w_ukv = np.asarray(even_w_ukv[i])
            shared[p + "w_in"] = wk(np.concatenate([w_in, w_in[:, 2448:2464], w_in[:, 2432:2448]], axis=1))
            shared[p + "w_uq"] = wk(_perm_uq(even_w_uq[i]))
            shared[p + "w_ukvk"] = np.ascontiguousarray(np.concatenate([w_ukv[:, h * 128:h * 128 + 64] for h in range(8)], axis=1)[:, None, :])
            shared[p + "w_ukvv"] = np.ascontiguousarray(np.concatenate([w_ukv[:, h * 128 + 64:h * 128 + 128] for h in range(8)], axis=1)[:, None, :])
            shared[p + "w_out"] = wk(even_w_out[i])
            shared[p + "scw"] = np.ascontiguousarray(np.asarray(even_sc_conv_w[i]).T.reshape(4, 128, 3).transpose(1, 0, 2))
            shared[p + "scb"] = fm(even_sc_conv_b[i])
            shared[p + "qg"] = fm(even_q_norm_g[i])
            shared[p + "kvg"] = fm(even_kv_norm_g[i])
        else:
            shared[p + "w_in"] = wk(odd_w_in[i])
            shared[p + "w_out"] = wk(odd_w_out[i])
            shared[p + "convw"] = np.ascontiguousarray(np.asarray(odd_conv_w[i]).T.reshape(8, 128, 31).transpose(1, 0, 2))
            shared[p + "convb"] = fm(odd_conv_b[i])
            shared[p + "lng"] = fm(odd_ln_g[i])
            shared[p + "lnb"] = fm(odd_ln_b[i])
    maps = []
    for core in range(8):
        b, half = core // 2, core % 2
        t0 = half * NT
        m = dict(shared)
        m["xT"] = xfm(x[b, t0:t0 + NT])
        m["xF"] = xfm(x[b, 0:NT])
        m["xH"] = xfm(x[b, NT - 32:NT]) if half == 1 else np.zeros((128, 8, 32), np.float32)
        m["cT"] = fm(c[b])
        m["hflag"] = np.full((128, 1), float(half), np.float32)
        m["posO"] = np.ascontiguousarray(positions[b, t0:t0 + NT][None, :]).astype(np.int32)
        m["posF"] = np.ascontiguousarray(positions[b, 0:NT][None, :]).astype(np.int32)
        kbias = np.zeros((128, 32), np.float32)
        if half == 0:
            kbias[:, 0:16] = -30000.0
        m["keybias"] = kbias
        maps.append(m)
    res = run_bass_kernel_spmd(nc, maps, core_ids=list(range(8)))
    out = np.empty_like(x)
    for core in range(8):
        b, half = core // 2, core % 2
        out[b, half * NT:(half + 1) * NT] = xfm_inv(res.results[core]["outT"])
    return out
```

```python
from contextlib import ExitStack
import numpy as np
import concourse.bass as bass
import concourse.mybir as mybir
from concourse.bass_utils import run_bass_kernel_spmd

F32 = mybir.dt.float32
BF16 = mybir.dt.bfloat16
I32 = mybir.dt.int32
ALU = mybir.AluOpType
AF = mybir.ActivationFunctionType

D = 1024
NT = 2048
SL = 512
NSL = 4
EPS = 1e-6
PI = 3.141592653589793
TWO_PI = 6.283185307179586

ENGS = ("pe", "dve", "act", "pool", "sp")
SAME_ENGINE_SYNC = True


class Sem:
    def __init__(self, handle, name, is_dma):
        self.h = handle
        self.name = name
        self.total = 0
        self.is_dma = is_dma


class Res:
    __slots__ = ("name", "writer", "readers", "dsem")

    def __init__(self, name):
        self.name = name
        self.writer = None
        self.readers = []
        self.dsem = None


class Buf:
    __slots__ = ("t", "r")

    def __init__(self, t, r):
        self.t = t
        self.r = r


class Prog:
    def __init__(self, nc):
        self.nc = nc
        self.stack = ExitStack()
        self.ops = {e: [] for e in ENGS}
        self.esem = {}
        for e in ENGS:
            self.esem[e] = Sem(self.stack.enter_context(nc.semaphore("s_" + e)), e, False)
        self.waited = {e: {} for e in ENGS}
        self.dma_sems = []
        self.free_dsems = []
        self.all_res = []
        self.nm = 0

    def sbuf(self, name, shape, dtype):
        return self.stack.enter_context(self.nc.sbuf_tensor("sb_" + name, list(shape), dtype))

    def psum(self, name, shape, dtype=F32):
        return self.stack.enter_context(self.nc.psum_tensor("ps_" + name, list(shape), dtype))

    def res(self, name="r"):
        r = Res(name)
        self.all_res.append(r)
        return r

    def buf(self, name, shape, dtype):
        return Buf(self.sbuf(name, shape, dtype), self.res(name))

    def _dsem(self, r):
        if r.dsem is None:
            if self.free_dsems:
                r.dsem = self.free_dsems.pop()
            else:
                self.nm += 1
                nm = "d%d" % self.nm
                r.dsem = Sem(self.stack.enter_context(self.nc.semaphore(nm)), nm, True)
                self.dma_sems.append(r.dsem)
        return r.dsem

    def _collect(self, eng, reads, writes):
        need = {}

        def add(tok):
            if tok is None:
                return
            sem, val, teng = tok
            if teng == eng and (eng == "pe" or not SAME_ENGINE_SYNC):
                return
            if sem.is_dma:
                val = max(val, sem.total)
            if need.get(sem, 0) < val:
                need[sem] = val

        for r in reads:
            add(r.writer)
        for w in writes:
            add(w.writer)
            for t in w.readers:
                add(t)
        wl = []
        wd = self.waited[eng]
        for sem, val in need.items():
            if wd.get(sem, 0) < val:
                wd[sem] = val
                wl.append((sem, val))
        return wl

    def _commit(self, tok, reads, writes):
        for r in reads:
            r.readers.append(tok)
            if len(r.readers) > 64:
                best = {}
                for t in r.readers:
                    if best.get(t[0], (None, 0, None))[1] < t[1]:
                        best[t[0]] = t
                r.readers = list(best.values())
        for w in writes:
            w.writer = tok
            w.readers = []

    def op(self, eng, fn, reads=(), writes=(), inc=True):
        waits = self._collect(eng, reads, writes)
        sem = self.esem[eng]
        if inc:
            sem.total += 1
            tok = (sem, sem.total, eng)
        else:
            tok = (sem, sem.total + 1, eng)
        self.ops[eng].append((waits, fn, (sem, 1) if inc else None))
        self._commit(tok, reads, writes)

    def dma(self, eng, out_ap, in_ap, reads=(), writes=(), sem_res=None):
        waits = self._collect(eng, reads, writes)
        if sem_res is None:
            sem_res = writes[0] if writes else reads[0]
        sem = self._dsem(sem_res)
        sem.total += 16
        tok = (sem, sem.total, "dma")

        def fn(e, out_ap=out_ap, in_ap=in_ap):
            return e.dma_start(out=out_ap, in_=in_ap)

        self.ops[eng].append((waits, fn, (sem, 16)))
        self._commit(tok, reads, writes)

    def cc(self, kind, in_ap, out_ap, groups, reads=(), writes=()):
        waits = self._collect("pool", reads, writes)
        sem = self._dsem(writes[0])
        sem.total += 1
        tok = (sem, sem.total, "dma")

        def fn(e):
            return e.collective_compute(kind, ALU.bypass, replica_groups=groups, ins=[in_ap], outs=[out_ap])

        self.ops["pool"].append((waits, fn, (sem, 1)))
        self._commit(tok, reads, writes)

    def barrier(self):
        for e in ENGS:
            waits = []
            wd = self.waited[e]
            for x in ENGS:
                s = self.esem[x]
                if x != e and s.total > wd.get(s, 0):
                    wd[s] = s.total
                    waits.append((s, s.total))
            for s in self.dma_sems:
                if s.total > wd.get(s, 0):
                    wd[s] = s.total
                    waits.append((s, s.total))
            if waits:
                self.ops[e].append((waits, None, None))
        for r in self.all_res:
            r.writer = None
            r.readers = []
            if r.dsem is not None:
                self.free_dsems.append(r.dsem)
                r.dsem = None

    def emit(self):
        nc = self.nc
        self.barrier()
        engmap = {"pe": "tensor", "dve": "vector", "act": "scalar", "pool": "gpsimd", "sp": "sync"}
        with nc.Block() as block:
            for e in ENGS:
                ops = self.ops[e]

                def body(eobj, ops=ops):
                    for waits, fn, inc in ops:
                        for sem, val in waits:
                            eobj.wait_ge(sem.h, val)
                        if fn is not None:
                            ins = fn(eobj)
                            if inc is not None:
                                ins.then_inc(inc[0].h, inc[1])

                getattr(block, engmap[e])(body)
        self.stack.close()


class Arena:
    def __init__(self, P, name, words):
        self.P = P
        self.t = P.sbuf(name, [128, words], F32)
        self.words = words
        self.off = 0

    def reset(self, mark=0):
        self.off = mark

    def mark(self):
        return self.off

    def take(self, shape, dtype, name="a"):
        n = 1
        for s in shape[1:]:
            n *= s
        esz = 4 if dtype in (F32, I32) else 2
        w = (n * esz + 3) // 4
        w = (w + 7) // 8 * 8
        assert self.off + w <= self.words, ("arena overflow", name, self.off, w, self.words)
        v = self.t[:, self.off:self.off + w]
        self.off += w
        if dtype != F32:
            v = v.bitcast(dtype)
        v = v[:, 0:n]
        if len(shape) == 3:
            v = v.rearrange("p (a b) -> p a b", a=shape[1])
        elif len(shape) == 4:
            v = v.rearrange("p (a b c) -> p a b c", a=shape[1], b=shape[2])
        if shape[0] != 128:
            v = v[0:shape[0]]
        return Buf(v, self.P.res(name))


class Ctx:
    def __init__(self, nc):
        self.nc = nc
        self.P = Prog(nc)
        P = self.P
        self.banks = [Buf(P.psum("bank%d" % i, [128, 512]), P.res("bank%d" % i)) for i in range(8)]
        self.bi = 0
        self.nbanks_rr = 8

    def nb(self):
        b = self.banks[self.bi % self.nbanks_rr]
        self.bi += 1
        return b

    def tt(self, eng, out, in0, in1, op, reads, writes):
        self.P.op(eng, lambda e: e.tensor_tensor(out=out, in0=in0, in1=in1, op=op), reads, writes)

    def ts(self, eng, out, in0, s1, op0, reads, writes, s2=None, op1=None):
        if op1 is None:
            self.P.op(eng, lambda e: e.tensor_scalar(out=out, in0=in0, scalar1=s1, scalar2=None, op0=op0), reads, writes)
        else:
            self.P.op(eng, lambda e: e.tensor_scalar(out=out, in0=in0, scalar1=s1, scalar2=s2, op0=op0, op1=op1), reads, writes)

    def stt(self, out, in0, scalar, in1, op0, op1, reads, writes):
        self.P.op("dve", lambda e: e.scalar_tensor_tensor(out=out, in0=in0, scalar=scalar, in1=in1, op0=op0, op1=op1), reads, writes)

    def act(self, out, in_, func, reads, writes, bias=None, scale=None):
        kw = {}
        if bias is not None:
            kw["bias"] = bias
        if scale is not None:
            kw["scale"] = scale
        self.P.op("act", lambda e: e.activation(out=out, in_=in_, func=func, **kw), reads, writes)

    def copy(self, eng, out, in_, reads, writes):
        if eng == "act":
            self.P.op("act", lambda e: e.activation(out=out, in_=in_, func=AF.Copy), reads, writes)
        else:
            self.P.op(eng, lambda e: e.tensor_copy(out=out, in_=in_), reads, writes)

    def memset(self, eng, ap, val, writes):
        self.P.op(eng, lambda e: e.memset(ap, val), (), writes)

    def mm(self, out, lhsT, rhs, start, stop, reads, writes, inc=None):
        if inc is None:
            inc = stop
        self.P.op("pe", lambda e: e.matmul(out, lhsT=lhsT, rhs=rhs, start=start, stop=stop), reads, writes, inc=inc)

    def load(self, buf_ap, dram_ap, writes, eng="sp"):
        self.P.dma(eng, buf_ap, dram_ap, writes=writes)

    def setup_consts(self, ident_dram, hflag_d, invf_d, sgn_d, kb_d):
        P = self.P
        self.ones_bf = P.buf("ones_bf", [128, 128], BF16)
        self.memset("pool", self.ones_bf.t[:], 1.0, [self.ones_bf.r])
        self.ones_f = P.buf("ones_f", [128, 2], F32)
        self.memset("pool", self.ones_f.t[:], 1.0, [self.ones_f.r])
        self.epsb = P.buf("epsb", [128, 1], F32)
        self.memset("pool", self.epsb.t[:], EPS, [self.epsb.r])
        self.ident = P.buf("ident_bf", [128, 128], BF16)
        P.dma("pool", self.ident.t[:], ident_dram, writes=[self.ident.r])
        self.hf = P.buf("hf", [128, 1], F32)
        self.invf = P.buf("invf", [32, 1], F32)
        self.sgn = P.buf("sgn", [32, 1], F32)
        self.keyb = P.buf("keyb", [128, 32], F32)
        for b_, d_ in ((self.hf, hflag_d), (self.invf, invf_d), (self.sgn, sgn_d), (self.keyb, kb_d)):
            self.load(b_.t[:], d_, [b_.r])

    def ada_setup(self, cT_d):
        P = self.P
        self.cT = P.buf("cT", [128, 8], F32)
        self.cact = P.buf("cact", [128, 8], F32)
        self.load(self.cT.t[:], cT_d, [self.cT.r])
        self.act(self.cact.t[:], self.cT.t[:], AF.Silu, [self.cT.r], [self.cact.r])
        self.ada_sets = []
        for l in range(4):
            d = {k: P.buf("%s_%d" % (k, l), [128, n], F32) for k, n in
                 (("adab", 24), ("preg", 8), ("postg", 8), ("modT", 24), ("A", 8), ("G2", 8))}
            d["done"] = False
            self.ada_sets.append(d)

    def ada_select(self, l):
        d = self.ada_sets[l]
        self.adab, self.preg, self.postg = d["adab"], d["preg"], d["postg"]
        self.modT, self.A, self.G2 = d["modT"], d["A"], d["G2"]
        return d

    def ada_steps(self, l, io, wb, acc):
        d = self.ada_sets[l]
        adab, preg, postg, modT, A, G2 = d["adab"], d["preg"], d["postg"], d["modT"], d["A"], d["G2"]
        cact = self.cact
        self.load(adab.t[:], io["adab"], [adab.r])
        self.load(preg.t[:], io["preg"], [preg.r])
        self.load(postg.t[:], io["postg"], [postg.r])
        nb_ = len(wb)

        def dve_piece(kc):
            b = wb[kc % nb_]
            if kc == 0:
                self.ts("dve", acc.t[:], b.t[:], cact.t[:, 0:1], ALU.mult, [b.r, cact.r], [acc.r])
            else:
                self.stt(acc.t[:], b.t[:], cact.t[:, kc:kc + 1], acc.t[:], ALU.mult, ALU.add, [b.r, cact.r, acc.r], [acc.r])

        for g in range(0, 8, nb_):
            for kc in range(g, min(8, g + nb_)):
                b = wb[kc % nb_]
                self.load(b.t[:], io["adaw"][:, kc, :], [b.r])
            yield
            for kc in range(g, min(8, g + nb_)):
                dve_piece(kc)
        yield
        bank = self.nb()
        for m in range(24):
            self.mm(bank.t[:, 2 * m:2 * m + 2], acc.t[:, m * 128:(m + 1) * 128], self.ones_f.t[:, 0:2], True, True,
                    [acc.r, self.ones_f.r], [bank.r], inc=True)
        pv = bank.t[:, 0:48].rearrange("p (m t) -> p m t", t=2)[:, :, 0]
        self.tt("dve", modT.t[:], pv, adab.t[:], ALU.add, [bank.r, adab.r], [modT.r])
        self.stt(A.t[:], modT.t[:, 8:16], 1.0, preg.t[:], ALU.add, ALU.mult, [modT.r, preg.r], [A.r])
        self.tt("dve", G2.t[:], modT.t[:, 16:24], postg.t[:], ALU.mult, [modT.r, postg.r], [G2.r])
        d["done"] = True
        yield

    def ada_phase(self, ar, l, io):
        if self.ada_sets[l]["done"]:
            return False
        wb = [ar.take([128, 3072], F32, "adaw%d" % i) for i in range(4)]
        acc = ar.take([128, 3072], F32, "adaacc")
        for _ in self.ada_steps(l, io, wb, acc):
            pass
        return True

    def rstd_from(self, ssbank, n, dim, out):
        self.act(out.t[:, 0:n], ssbank.t[:, 0:n], AF.Ln, [ssbank.r, self.epsb.r], [out.r], bias=self.epsb.t[:], scale=1.0 / dim)
        self.act(out.t[:, 0:n], out.t[:, 0:n], AF.Exp, [out.r], [out.r], scale=-0.5)

    def prenorm(self, x_d, n, h, xcs, xsqs, rstd, tmps, xres=()):
        ss = self.nb()
        for c in range(8):
            xc = xcs[c % len(xcs)]
            self.P.dma("sp", xc.t[:, 0:n], x_d[:, c, :], reads=list(xres), writes=[xc.r], sem_res=xc.r)
            sq = xsqs[c % len(xsqs)]
            self.act(sq.t[:, 0:n], xc.t[:, 0:n], AF.Square, [xc.r], [sq.r])
            self.mm(ss.t[:, 0:n], self.ones_bf.t[:], sq.t[:, 0:n], c == 0, c == 7, [self.ones_bf.r, sq.r], [ss.r], inc=True)
        self.rstd_from(ss, n, float(D), rstd)
        for c in range(8):
            xc = xcs[c % len(xcs)]
            self.P.dma("sp", xc.t[:, 0:n], x_d[:, c, :], reads=list(xres), writes=[xc.r], sem_res=xc.r)
            tm = tmps[c % len(tmps)]
            self.stt(tm.t[:, 0:n], xc.t[:, 0:n], self.A.t[:, c:c + 1], rstd.t[:, 0:n], ALU.mult, ALU.mult,
                     [xc.r, self.A.r, rstd.r], [tm.r])
            self.act(h.t[:, c, 0:n], tm.t[:, 0:n], AF.Identity, [tm.r, self.modT.r], [h.r], bias=self.modT.t[:, c:c + 1])

    def out_mm(self, wout, rhs_fn, rhs_res, ysb, ysqs, rstd):
        ss = self.banks[7]
        self.nbanks_rr = 7
        for m in range(8):
            bank = self.nb()
            for k in range(8):
                self.mm(bank.t[:], wout.t[:, k, m * 128:(m + 1) * 128], rhs_fn(k), k == 0, k == 7,
                        [wout.r] + rhs_res, [bank.r])
            self.copy("act", ysb.t[:, m, :], bank.t[:], [bank.r], [ysb.r])
            sq = ysqs[m % len(ysqs)]
            self.act(sq.t[:], bank.t[:], AF.Square, [bank.r], [sq.r])
            self.mm(ss.t[:], self.ones_bf.t[:], sq.t[:], m == 0, m == 7, [self.ones_bf.r, sq.r], [ss.r], inc=True)
        self.rstd_from(ss, SL, float(D), rstd)
        self.nbanks_rr = 8

    def out_res(self, x_d, out_d, ysb, rstd, xcs, tmps, obufs, xres=(), halo_d=None):
        for c in range(8):
            xc = xcs[c % len(xcs)]
            self.P.dma("sp", xc.t[:], x_d[:, c, :], reads=list(xres), writes=[xc.r], sem_res=xc.r)
            tm = tmps[c % len(tmps)]
            self.tt("dve", tm.t[:], ysb.t[:, c, :], rstd.t[:], ALU.mult, [ysb.r, rstd.r], [tm.r])
            ob = obufs[c % len(obufs)]
            self.stt(ob.t[:], tm.t[:], self.G2.t[:, c:c + 1], xc.t[:], ALU.mult, ALU.add, [tm.r, self.G2.r, xc.r], [ob.r])
            self.P.dma("sp", out_d[:, c, :], ob.t[:], reads=[ob.r], sem_res=ob.r)
            if halo_d is not None:
                self.P.dma("sp", halo_d[:, c, :], ob.t[:, SL - 32:SL], reads=[ob.r], sem_res=ob.r)

    def out_phase(self, wout, rhs_fn, rhs_res, x_d, out_d, ysb, ysqs, rstd, xcs, tmps, obufs, xres=(), ores=(), halo_d=None):
        self.out_mm(wout, rhs_fn, rhs_res, ysb, ysqs, rstd)
        self.out_res(x_d, out_d, ysb, rstd, xcs, tmps, obufs, xres=xres, halo_d=halo_d)

    def load_weight_bf16(self, wbuf_ap_fn, dram, nk, ncols, res):
        for k in range(nk):
            c0 = 0
            while c0 < ncols:
                c1 = min(ncols, c0 + 2048)
                self.P.dma("pool", wbuf_ap_fn(k, c0, c1), dram[:, k, c0:c1], writes=[res])
                c0 = c1


def dram_in(nc, name, shape, dtype=F32):
    return nc.dram_tensor(name, list(shape), dtype, kind="ExternalInput").ap()


def emit_odd(cx, ar, io):
    P = cx.P
    xT, xH, outT = io["xT"], io["xH"], io["out"]
    xHres = io.get("xHres", ())
    P.barrier()
    ar.reset()
    cx.ada_select(io["layer"])
    win = ar.take([128, 8, 3072], BF16, "win")
    wout = ar.take([128, 8, 1024], BF16, "wout")
    cw = ar.take([128, 8, 31], F32, "cw")
    cb = ar.take([128, 8], F32, "cb")
    lg = ar.take([128, 8], F32, "lg")
    lb = ar.take([128, 8], F32, "lb")
    base = ar.mark()
    cx.load_weight_bf16(lambda k, a, b: win.t[:, k, a:b], io["w_in"], 8, 3072, win.r)
    cx.load_weight_bf16(lambda k, a, b: wout.t[:, k, a:b], io["w_out"], 8, 1024, wout.r)
    for pre in io.get("pre", ()):
        pre()
    for b_, d_ in ((cw, io["convw"]), (cb, io["convb"]), (lg, io["lng"]), (lb, io["lnb"])):
        cx.load(b_.t[:], d_, [b_.r])
    hf = cx.hf
    if cx.ada_phase(ar, io["layer"], io):
        P.barrier()
    ar.reset(base)

    xcs = [ar.take([128, 512], F32, "xc%d" % i) for i in range(3)]
    xsqs = [ar.take([128, 512], BF16, "xsq%d" % i) for i in range(2)]
    tmps = [ar.take([128, 512], F32, "tmp%d" % i) for i in range(3)]
    rstd = ar.take([128, 512], F32, "rstd")
    hs = [ar.take([128, 8, 512], BF16, "h%d" % i) for i in range(2)]
    u = [ar.take([128, 544], BF16, "u%d" % j) for j in range(8)]
    convs = [ar.take([128, 8, 512], F32, "conv%d" % i) for i in range(2)]
    sgates = [ar.take([128, 8, 512], BF16, "sgate%d" % i) for i in range(2)]
    gated = ar.take([128, 8, 512], BF16, "gated")
    diag = [ar.take([128, 31, 128], BF16, "diag%d" % i) for i in range(2)]
    cbf = [ar.take([128, 512], BF16, "cbf%d" % i) for i in range(2)]
    csq = [ar.take([128, 512], BF16, "csq%d" % i) for i in range(2)]
    means = [ar.take([128, 512], F32, "mean%d" % i) for i in range(2)]
    vars_ = [ar.take([128, 512], F32, "var%d" % i) for i in range(2)]
    obufs = [ar.take([128, 512], F32, "ob%d" % i) for i in range(2)]
    rstd_o = ar.take([128, 512], F32, "rstd_o")
    ysqs_o = [ar.take([128, 512], BF16, "ysqo%d" % i) for i in range(2)]

    hh = hs[1]
    cx.prenorm(xH, 32, hh, xcs, xsqs, rstd, tmps, xres=xHres)
    for j in range(8):
        bv = cx.nb()
        bg = cx.nb()
        for k in range(8):
            cx.mm(bv.t[:, 0:32], win.t[:, k, j * 128:(j + 1) * 128], hh.t[:, k, 0:32], k == 0, k == 7, [win.r, hh.r], [bv.r])
        for k in range(8):
            cx.mm(bg.t[:, 0:32], win.t[:, k, 1024 + j * 128:1024 + (j + 1) * 128], hh.t[:, k, 0:32], k == 0, k == 7, [win.r, hh.r], [bg.r])
        tm = tmps[j % 3]
        cx.act(tm.t[:, 0:32], bg.t[:, 0:32], AF.Sigmoid, [bg.r], [tm.r])
        cx.tt("dve", tm.t[:, 0:32], bv.t[:, 0:32], tm.t[:, 0:32], ALU.mult, [bv.r, tm.r], [tm.r])
        cx.ts("dve", u[j].t[:, 512:542], tm.t[:, 2:32], hf.t[:, 0:1], ALU.mult, [tm.r, hf.r], [u[j].r])

    def A_pre(s):
        cx.prenorm(xT[:, :, s * SL:(s + 1) * SL], SL, hs[s % 2], xcs, xsqs, rstd, tmps)

    def A_body(s, between=None):
        conv, sgate, mean, var = convs[s % 2], sgates[s % 2], means[s % 2], vars_[s % 2]
        h = hs[s % 2]
        mbank = cx.banks[6]
        ebank = cx.banks[7]
        cx.nbanks_rr = 6
        wb = {}

        def emit_win(j):
            b0 = 3 * (j % 2)
            bv, bg, bs = cx.banks[b0], cx.banks[b0 + 1], cx.banks[b0 + 2]
            for (bk, off) in ((bv, 0), (bg, 1024), (bs, 2048)):
                for k in range(8):
                    cx.mm(bk.t[:], win.t[:, k, off + j * 128:off + (j + 1) * 128], h.t[:, k, :], k == 0, k == 7, [win.r, h.r], [bk.r])
            dg = diag[j % 2]
            cx.tt("dve", dg.t[:], cx.ident.t[:].unsqueeze(1).to_broadcast([128, 31, 128]),
                  cw.t[:, j, :].unsqueeze(2).to_broadcast([128, 31, 128]), ALU.mult, [cx.ident.r, cw.r], [dg.r])
            wb[j] = (bv, bg, bs, dg)

        emit_win(0)
        for j in range(8):
            if j + 1 < 8:
                emit_win(j + 1)
            bv, bg, bs, dg = wb.pop(j)
            cx.copy("pool", u[j].t[:, 0:30], u[j].t[:, 512:542], [u[j].r], [u[j].r])
            tm = tmps[j % 3]
            cx.act(tm.t[:], bg.t[:], AF.Sigmoid, [bg.r], [tm.r])
            cx.tt("dve", u[j].t[:, 30:542], bv.t[:], tm.t[:], ALU.mult, [bv.r, tm.r], [u[j].r])
            cx.act(sgate.t[:, j, :], bs.t[:], AF.Silu, [bs.r], [sgate.r])
            cb_ = bg
            for k in range(31):
                cx.mm(cb_.t[:], dg.t[:, k, :], u[j].t[:, k:k + 512], k == 0, k == 30, [dg.r, u[j].r], [cb_.r])
            cx.act(conv.t[:, j, :], cb_.t[:], AF.Identity, [cb_.r, cb.r], [conv.r], bias=cb.t[:, j:j + 1])
            q_ = csq[j % 2]
            cx.act(q_.t[:], cb_.t[:], AF.Square, [cb_.r, cb.r], [q_.r], bias=cb.t[:, j:j + 1])
            c_ = cbf[j % 2]
            cx.copy("dve", c_.t[:], conv.t[:, j, :], [conv.r], [c_.r])
            cx.mm(mbank.t[:], cx.ones_bf.t[:], c_.t[:], j == 0, j == 7, [cx.ones_bf.r, c_.r], [mbank.r], inc=True)
            cx.mm(ebank.t[:], cx.ones_bf.t[:], q_.t[:], j == 0, j == 7, [cx.ones_bf.r, q_.r], [ebank.r], inc=True)
            if between is not None:
                between(j)
        cx.nbanks_rr = 8
        cx.ts("dve", mean.t[:], mbank.t[:], 1.0 / D, ALU.mult, [mbank.r], [mean.r])
        cx.tt("dve", var.t[:], mean.t[:], mean.t[:], ALU.mult, [mean.r], [var.r])
        cx.stt(var.t[:], ebank.t[:], 1.0 / D, var.t[:], ALU.mult, ALU.subtract, [ebank.r, var.r], [var.r])
        cx.act(var.t[:], var.t[:], AF.Ln, [var.r, cx.epsb.r], [var.r], bias=cx.epsb.t[:])
        cx.act(var.t[:], var.t[:], AF.Exp, [var.r], [var.r], scale=-0.5)
    def B_norm_j(s, j):
        conv, sgate, mean, var = convs[s % 2], sgates[s % 2], means[s % 2], vars_[s % 2]
        tm = tmps[j % 3]
        cx.tt("dve", tm.t[:], conv.t[:, j, :], mean.t[:], ALU.subtract, [conv.r, mean.r], [tm.r])
        cx.tt("dve", tm.t[:], tm.t[:], var.t[:], ALU.mult, [tm.r, var.r], [tm.r])
        cx.act(tm.t[:], tm.t[:], AF.Silu, [tm.r, lg.r, lb.r], [tm.r], bias=lb.t[:, j:j + 1], scale=lg.t[:, j:j + 1])
        cx.tt("pool", gated.t[:, j, :], tm.t[:], sgate.t[:, j, :], ALU.mult, [tm.r, sgate.r], [gated.r])

    def B_mm(s):
        cx.out_mm(wout, lambda k: gated.t[:, k, :], [gated.r], convs[s % 2], ysqs_o, rstd_o)

    def B_res(s):
        cx.out_res(xT[:, :, s * SL:(s + 1) * SL], outT[:, :, s * SL:(s + 1) * SL], convs[s % 2], rstd_o, xcs, tmps, obufs,
                   halo_d=(io.get("halo_out") if s == NSL - 1 else None))

    A_pre(0)
    A_body(0)
    A_pre(1)
    for s in range(1, NSL):
        A_body(s, between=lambda j, s=s: B_norm_j(s - 1, j))
        B_mm(s - 1)
        if s + 1 < NSL:
            A_pre(s + 1)
        B_res(s - 1)
    for j in range(8):
        B_norm_j(NSL - 1, j)
    B_mm(NSL - 1)
    B_res(NSL - 1)


def emit_even(cx, ar, io):
    P = cx.P
    xT, xF, xH, outT = io["xT"], io["xF"], io["xH"], io["out"]
    xFres = io.get("xFres", ())
    posO, posF = io["posO"], io["posF"]
    SCALE = 1.0 / float(np.sqrt(96.0))
    hf, invf, sgn, keyb = cx.hf, cx.invf, cx.sgn, cx.keyb
    P.barrier()
    ar.reset()
    cx.ada_select(io["layer"])
    wuq = ar.take([128, 2, 1024], BF16, "wuq")
    wkk = ar.take([128, 512], BF16, "wkk")
    wvv = ar.take([128, 512], BF16, "wvv")
    scw = ar.take([128, 4, 3], F32, "scw")
    scb = ar.take([128, 4], F32, "scb")
    qg = ar.take([128, 2], F32, "qg")
    kvg = ar.take([128, 1], F32, "kvg")
    cxh = ar.take([128, 4, 2], F32, "cxh")
    ckvn_t = ar.take([128, 4096], BF16, "ckvn_all").t
    ckvn_r = [P.res("ckvn%d" % i) for i in range(8)]
    krope_t = ar.take([128, 4096], BF16, "krope_all").t
    krope_r = [P.res("krope%d" % i) for i in range(8)]
    ya_t = ar.take([128, 4, NT], BF16, "ya_all").t
    ya_r = [P.res("ya%d" % i) for i in range(4)]
    sbg_t = ar.take([128, 4, NT], BF16, "sbg_all").t
    sbg_r = [P.res("sbg%d" % i) for i in range(4)]
    q_t = ar.take([128, 8, NT], BF16, "q_all").t
    q_r = [P.res("q%d" % i) for i in range(4)]
    base = ar.mark()
    cx.load_weight_bf16(lambda k, a, b: wuq.t[:, k, a:b], io["w_uq"], 2, 1024, wuq.r)
    cx.load_weight_bf16(lambda k, a, b: wkk.t[:, a:b], io["w_ukvk"], 1, 512, wkk.r)
    cx.load_weight_bf16(lambda k, a, b: wvv.t[:, a:b], io["w_ukvv"], 1, 512, wvv.r)
    for b_, d_ in ((scw, io["scw"]), (scb, io["scb"]), (qg, io["qg"]), (kvg, io["kvg"])):
        cx.load(b_.t[:], d_, [b_.r])
    win = ar.take([128, 8, 3008], BF16, "win")
    base1 = ar.mark()
    cx.load_weight_bf16(lambda k, a, b: win.t[:, k, a:b], io["w_in"], 8, 3008, win.r)
    for pre in io.get("pre", ()):
        pre()
    if cx.ada_phase(ar, io["layer"], io):
        P.barrier()
    ar.reset(base1)

    xcs = [ar.take([128, 512], F32, "xc%d" % i) for i in range(2)]
    xsqs = [ar.take([128, 512], BF16, "xsq%d" % i) for i in range(2)]
    tmps = [ar.take([128, 512], F32, "tmp%d" % i) for i in range(2)]
    tA = [ar.take([128, 512], F32, "tA%d" % i) for i in range(1)]
    tB = [ar.take([128, 512], F32, "tB%d" % i) for i in range(1)]
    acc2 = [ar.take([128, 512], F32, "acc%d" % i) for i in range(1)]
    rstd = ar.take([128, 512], F32, "rstd")
    rstd2 = rstd
    hs = [ar.take([128, 8, 512], BF16, "h%d" % i) for i in range(2)]
    cxb = [ar.take([128, 514], F32, "cxb%d" % i) for i in range(2)]
    cqn = ar.take([128, 2, 512], BF16, "cqn")
    sqb = [ar.take([128, 512], BF16, "sqb%d" % i) for i in range(2)]
    posi = ar.take([32, 512], I32, "posi")
    ang = ar.take([32, 512], F32, "ang")
    kfl = ar.take([32, 512], F32, "kfl")
    msk = ar.take([32, 512], F32, "msk")
    cos2 = ar.take([32, 512], F32, "cos2")
    sin2 = ar.take([32, 512], F32, "sin2")
    rt1 = [ar.take([128, 512], F32, "rt1_%d" % i) for i in range(1)]
    rt2 = [ar.take([128, 512], F32, "rt2_%d" % i) for i in range(1)]

    def rope_tables(pos_d, c0):
        P.dma("sp", posi.t[:], pos_d[0:1, c0:c0 + SL].partition_broadcast(32), writes=[posi.r])
        cx.copy("dve", ang.t[:], posi.t[:], [posi.r], [ang.r])
        cx.ts("dve", ang.t[:], ang.t[:], invf.t[:, 0:1], ALU.mult, [ang.r, invf.r], [ang.r])
        cx.ts("dve", posi.t[:], ang.t[:], 1.0 / TWO_PI, ALU.mult, [ang.r], [posi.r])
        cx.copy("dve", kfl.t[:], posi.t[:], [posi.r], [kfl.r])
        cx.stt(ang.t[:], kfl.t[:], -TWO_PI, ang.t[:], ALU.mult, ALU.add, [kfl.r, ang.r], [ang.r])
        cx.ts("dve", msk.t[:], ang.t[:], PI, ALU.is_gt, [ang.r], [msk.r], s2=-TWO_PI, op1=ALU.mult)
        cx.tt("dve", ang.t[:], ang.t[:], msk.t[:], ALU.add, [ang.r, msk.r], [ang.r])
        cx.ts("dve", msk.t[:], ang.t[:], -PI, ALU.is_lt, [ang.r], [msk.r], s2=TWO_PI, op1=ALU.mult)
        cx.tt("dve", ang.t[:], ang.t[:], msk.t[:], ALU.add, [ang.r, msk.r], [ang.r])
        cx.act(sin2.t[:], ang.t[:], AF.Sin, [ang.r], [sin2.r])
        cx.ts("dve", sin2.t[:], sin2.t[:], sgn.t[:, 0:1], ALU.mult, [sin2.r, sgn.r], [sin2.r])
        cx.ts("dve", kfl.t[:], ang.t[:], PI / 2, ALU.add, [ang.r], [kfl.r])
        cx.ts("dve", msk.t[:], kfl.t[:], PI, ALU.is_gt, [kfl.r], [msk.r], s2=-TWO_PI, op1=ALU.mult)
        cx.tt("dve", kfl.t[:], kfl.t[:], msk.t[:], ALU.add, [kfl.r, msk.r], [kfl.r])
        cx.act(cos2.t[:], kfl.t[:], AF.Sin, [kfl.r], [cos2.r])

    def kv_group(h, kcol0, blk):
        bkv = cx.nb()
        bkx = cx.nb()
        bky = cx.nb()
        for k in range(8):
            cx.mm(bkv.t[:], win.t[:, k, 2304:2432], h.t[:, k, :], k == 0, k == 7, [win.r, h.r], [bkv.r])
        for k in range(8):
            cx.mm(bkx.t[64:96, :], win.t[:, k, 2432:2464], h.t[:, k, :], k == 0, k == 7, [win.r, h.r], [bkx.r])
        for k in range(8):
            cx.mm(bky.t[64:96, :], win.t[:, k, 2976:3008], h.t[:, k, :], k == 0, k == 7, [win.r, h.r], [bky.r])
        sq = sqb[0]
        cx.act(sq.t[:], bkv.t[:], AF.Square, [bkv.r], [sq.r])
        ss = cx.nb()
        cx.mm(ss.t[:], cx.ones_bf.t[:], sq.t[:], True, True, [cx.ones_bf.r, sq.r], [ss.r])
        cx.rstd_from(ss, SL, 128.0, rstd2)
        cx.stt(ckvn_t[:, kcol0:kcol0 + SL], bkv.t[:], kvg.t[:, 0:1], rstd2.t[:], ALU.mult, ALU.mult,
               [bkv.r, kvg.r, rstd2.r], [ckvn_r[blk]])
        a, b = rt1[0], rt2[0]
        cx.tt("dve", a.t[64:96, :], bkx.t[64:96, :], cos2.t[:], ALU.mult, [bkx.r, cos2.r], [a.r])
        cx.tt("dve", b.t[64:96, :], bky.t[64:96, :], sin2.t[:], ALU.mult, [bky.r, sin2.r], [b.r])
        cx.tt("pool", krope_t[64:96, kcol0:kcol0 + SL], a.t[64:96, :], b.t[64:96, :], ALU.add, [a.r, b.r], [krope_r[blk]])

    hh = hs[1]
    cx.prenorm(xH, 32, hh, xcs, xsqs, rstd, tmps, xres=xFres)
    for j in range(4):
        bc = cx.nb()
        bx = cx.nb()
        for k in range(8):
            cx.mm(bc.t[:, 0:32], win.t[:, k, 512 + j * 128:512 + (j + 1) * 128], hh.t[:, k, 0:32], k == 0, k == 7, [win.r, hh.r], [bc.r])
        for k in range(8):
            cx.mm(bx.t[:, 0:32], win.t[:, k, 1024 + j * 128:1024 + (j + 1) * 128], hh.t[:, k, 0:32], k == 0, k == 7, [win.r, hh.r], [bx.r])
        tm = tA[0]
        cx.copy("act", tm.t[:, 0:32], bc.t[:, 0:32], [bc.r], [tm.r])
        cx.tt("dve", tm.t[:, 0:32], bx.t[:, 0:32], tm.t[:, 0:32], ALU.mult, [bx.r, tm.r], [tm.r])
        cx.ts("dve", cxh.t[:, j, 0:2], tm.t[:, 30:32], hf.t[:, 0:1], ALU.mult, [tm.r, hf.r], [cxh.r])

    for s in range(NSL):
        h = hs[s % 2]
        c0 = s * SL
        cx.prenorm(xT[:, :, c0:c0 + SL], SL, h, xcs, xsqs, rstd, tmps)
        rope_tables(posO, c0)
        for j in range(4):
            bb, bc, bx, bg = cx.nb(), cx.nb(), cx.nb(), cx.nb()
            for (bk, off) in ((bb, 0), (bc, 512), (bx, 1024), (bg, 1536)):
                for k in range(8):
                    cx.mm(bk.t[:], win.t[:, k, off + j * 128:off + (j + 1) * 128], h.t[:, k, :], k == 0, k == 7, [win.r, h.r], [bk.r])
            ta = tA[0]
            tb = tB[0]
            cb_ = cxb[j % 2]
            ac = acc2[0]
            cx.copy("act", ta.t[:], bc.t[:], [bc.r], [ta.r])
            cx.copy("pool", cb_.t[:, 0:2], cxh.t[:, j, 0:2], [cxh.r], [cb_.r])
            cx.tt("dve", cb_.t[:, 2:514], bx.t[:], ta.t[:], ALU.mult, [bx.r, ta.r], [cb_.r])
            cx.copy("pool", cxh.t[:, j, 0:2], cb_.t[:, 512:514], [cb_.r], [cxh.r])
            cx.ts("dve", ac.t[:], cb_.t[:, 2:514], scw.t[:, j, 2:3], ALU.mult, [cb_.r, scw.r, scb.r], [ac.r], s2=scb.t[:, j:j + 1], op1=ALU.add)
            cx.stt(ac.t[:], cb_.t[:, 1:513], scw.t[:, j, 1:2], ac.t[:], ALU.mult, ALU.add, [cb_.r, scw.r, ac.r], [ac.r])
            cx.stt(ac.t[:], cb_.t[:, 0:512], scw.t[:, j, 0:1], ac.t[:], ALU.mult, ALU.add, [cb_.r, scw.r, ac.r], [ac.r])
            cx.act(tb.t[:], bg.t[:], AF.Silu, [bg.r], [tb.r])
            cx.tt("dve", ac.t[:], ac.t[:], bb.t[:], ALU.mult, [ac.r, bb.r], [ac.r])
            cx.tt("pool", ya_t[:, j, c0:c0 + SL], ac.t[:], tb.t[:], ALU.mult, [ac.r, tb.r], [ya_r[s]])
        kv_group(h, NT + c0, 4 + s)
        bq = [cx.nb(), cx.nb()]
        for i in range(2):
            for k in range(8):
                cx.mm(bq[i].t[:], win.t[:, k, 2048 + i * 128:2048 + (i + 1) * 128], h.t[:, k, :], k == 0, k == 7, [win.r, h.r], [bq[i].r])
        ss = cx.nb()
        for i in range(2):
            sq = sqb[i]
            cx.act(sq.t[:], bq[i].t[:], AF.Square, [bq[i].r], [sq.r])
            cx.mm(ss.t[:], cx.ones_bf.t[:], sq.t[:], i == 0, i == 1, [cx.ones_bf.r, sq.r], [ss.r], inc=True)
        cx.rstd_from(ss, SL, 256.0, rstd2)
        for i in range(2):
            cx.stt(cqn.t[:, i, :], bq[i].t[:], qg.t[:, i:i + 1], rstd2.t[:], ALU.mult, ALU.mult, [bq[i].r, qg.r, rstd2.r], [cqn.r])
        for hd in range(8):
            b1, b2 = cx.nb(), cx.nb()
            for i in range(2):
                cx.mm(b1.t[0:96, :], wuq.t[:, i, hd * 128:hd * 128 + 96], cqn.t[:, i, :], i == 0, i == 1, [wuq.r, cqn.r], [b1.r])
            for i in range(2):
                cx.mm(b2.t[64:96, :], wuq.t[:, i, hd * 128 + 96:hd * 128 + 128], cqn.t[:, i, :], i == 0, i == 1, [wuq.r, cqn.r], [b2.r])
            a, b = rt1[0], rt2[0]
            cx.tt("dve", a.t[64:96, :], b1.t[64:96, :], cos2.t[:], ALU.mult, [b1.r, cos2.r], [a.r])
            cx.tt("dve", b.t[64:96, :], b2.t[64:96, :], sin2.t[:], ALU.mult, [b2.r, sin2.r], [b.r])
            cx.tt("pool", q_t[64:96, hd, c0:c0 + SL], a.t[64:96, :], b.t[64:96, :], ALU.add, [a.r, b.r], [q_r[s]])
            cx.copy("act", q_t[0:64, hd, c0:c0 + SL], b1.t[0:64, :], [b1.r], [q_r[s]])
        for j in range(4):
            bk = cx.nb()
            for k in range(8):
                cx.mm(bk.t[:], win.t[:, k, 2464 + j * 128:2464 + (j + 1) * 128], h.t[:, k, :], k == 0, k == 7, [win.r, h.r], [bk.r])
            cx.act(sbg_t[:, j, c0:c0 + SL], bk.t[:], AF.Silu, [bk.r], [sbg_r[s]])

    lat, latg = io["lat"], io["latg"]
    Rlat, Rlatg = P.res("lat"), P.res("latg")
    P.dma("sp", lat[0:128, :], ckvn_t[:, NT:2 * NT], reads=ckvn_r[4:8], writes=[Rlat])
    P.dma("sp", lat[128:160, :], krope_t[64:96, NT:2 * NT], reads=krope_r[4:8], writes=[Rlat])
    P.cc("AllGather", lat, latg.rearrange("r p t -> (r p) t"), PAIRS, reads=[Rlat], writes=[Rlatg])
    P.dma("sp", ckvn_t[:, 0:NT], latg[0, 0:128, :], reads=[Rlatg], writes=ckvn_r[0:4], sem_res=ckvn_r[0])
    P.dma("sp", krope_t[64:96, 0:NT], latg[0, 128:160, :], reads=[Rlatg], writes=krope_r[0:4], sem_res=krope_r[0])

    P.barrier()
    ar.reset(base)

    yb = ar.take([128, 4, NT], BF16, "yb_all")
    yb_r = [P.res("yb%d" % i) for i in range(4)]
    mark = ar.mark()
    vp = [ar.take([128, 32, 192], BF16, "vp%d" % i) for i in range(2)]
    kh = [ar.take([128, 4096], BF16, "kh%d" % i) for i in range(2)]
    pts = [ar.take([128, 512], BF16, "pt%d" % i) for i in range(6)]
    rds = [ar.take([128, 512], F32, "rd%d" % i) for i in range(2)]
    tms = [ar.take([128, 512], F32, "tm%d" % i) for i in range(2)]
    for v_ in vp:
        cx.memset("pool", v_.t[:, :, 64:128], 1.0, [v_.r])
    cx.nbanks_rr = 6
    ada_job = None
    if io.get("ada_jobs"):
        awb = [ar.take([128, 3072], F32, "adaw%d" % i) for i in range(2)]
        aacc = ar.take([128, 3072], F32, "adaacc")

        def _jobs():
            for (l2, io2) in io["ada_jobs"]:
                for _ in cx.ada_steps(l2, io2, awb, aacc):
                    yield
        ada_job = _jobs()
    cnt = 0
    pti = 0
    for j in range(4):
        v_ = vp[j % 2]
        for g in range(8):
            bank = cx.nb()
            for i in range(4):
                kb = 4 * g + i
                cx.mm(bank.t[:, i * 128:(i + 1) * 128], ckvn_t[:, kb * 128:(kb + 1) * 128], wvv.t[:, j * 128:(j + 1) * 128], True, True,
                      [ckvn_r[kb // 4], wvv.r], [bank.r], inc=(i == 3))
            bv = bank.t[:, :].rearrange("p (a b) -> p a b", a=4)
            cx.copy("dve", v_.t[:, 4 * g:4 * g + 4, 0:64], bv[:, :, 0:64], [bank.r], [v_.r])
            cx.copy("act", v_.t[:, 4 * g:4 * g + 4, 128:192], bv[:, :, 64:128], [bank.r], [v_.r])
        for hd in (2 * j, 2 * j + 1):
            k_ = kh[hd % 2]
            cx.copy("pool", k_.t[64:96, :], krope_t[64:96, :], krope_r, [k_.r])
            for kc in range(8):
                bank = cx.nb()
                cx.mm(bank.t[0:64, :], wkk.t[:, hd * 64:(hd + 1) * 64], ckvn_t[:, kc * SL:(kc + 1) * SL], True, True,
                      [wkk.r, ckvn_r[kc]], [bank.r])
                cx.copy("dve", k_.t[0:64, kc * SL:(kc + 1) * SL], bank.t[0:64, :], [bank.r], [k_.r])
            vc0 = 0 if hd % 2 == 0 else 64
            for s in range(NSL):
                if ada_job is not None:
                    try:
                        next(ada_job)
                    except StopIteration:
                        ada_job = None
                accb = cx.banks[6 + cnt % 2]
                cnt += 1
                nkb = 16 + 4 * (s + 1)
                LA = 3
                scs = {}

                def emit_sc(kb, s=s, hd=hd, k_=k_, nkb=nkb):
                    jd = kb - (nkb - 4)
                    lo = 128 * jd if jd >= 0 else 0
                    sc = cx.nb()
                    cx.mm(sc.t[:, lo:SL], k_.t[0:96, kb * 128:(kb + 1) * 128], q_t[0:96, hd, s * SL + lo:(s + 1) * SL], True, True,
                          [k_.r, q_r[s]], [sc.r])
                    scs[kb] = (sc, lo, jd)

                for kb in range(min(LA, nkb)):
                    emit_sc(kb)
                for kb in range(nkb):
                    sc, lo, jd = scs.pop(kb)
                    pt = pts[pti % 6]
                    pti += 1
                    cx.act(pt.t[:, lo:SL], sc.t[:, lo:SL], AF.Exp, [sc.r, keyb.r], [pt.r], bias=keyb.t[:, kb:kb + 1], scale=SCALE)
                    if jd >= 0:
                        cx.memset("dve", pt.t[64:128, lo:lo + 64], 0.0, [pt.r])
                    if kb + LA < nkb:
                        emit_sc(kb + LA)
                    cx.mm(accb.t[:, lo:SL], v_.t[:, kb, vc0:vc0 + 128], pt.t[:, lo:SL], kb == 0, kb == nkb - 1,
                          [v_.r, pt.r], [accb.r], inc=True)
                if hd % 2 == 0:
                    npart, dpart = slice(0, 64), slice(64, 128)
                else:
                    npart, dpart = slice(64, 128), slice(0, 64)
                rd = rds[cnt % 2]
                tm = tms[cnt % 2]
                P.op("dve", lambda e, rd=rd, accb=accb, dpart=dpart: e.reciprocal(out=rd.t[dpart, :], in_=accb.t[dpart, :]), [accb.r], [rd.r])
                cx.tt("dve", tm.t[npart, :], accb.t[npart, :], rd.t[dpart, :], ALU.mult, [accb.r, rd.r], [tm.r])
                cx.tt("pool", yb.t[npart, j, s * SL:(s + 1) * SL], tm.t[npart, :], sbg_t[npart, j, s * SL:(s + 1) * SL], ALU.mult,
                      [tm.r, sbg_r[s]], [yb_r[s]])
    cx.nbanks_rr = 8
    if ada_job is not None:
        for _ in ada_job:
            pass

    P.barrier()
    ar.reset(mark)

    wout = ar.take([128, 8, 1024], BF16, "wout")
    cx.load_weight_bf16(lambda k, a, b: wout.t[:, k, a:b], io["w_out"], 8, 1024, wout.r)
    ysbs = [ar.take([128, 8, 512], F32, "ysb%d" % i) for i in range(2)]
    ysqs = [ar.take([128, 512], BF16, "ysq%d" % i) for i in range(2)]
    xcs = [ar.take([128, 512], F32, "xc%d" % i) for i in range(6)]
    tmps = [ar.take([128, 512], F32, "tmp%d" % i) for i in range(2)]
    obufs = [ar.take([128, 512], F32, "ob%d" % i) for i in range(3)]
    rstds = [ar.take([128, 512], F32, "rstd%d" % i) for i in range(2)]
    for s in range(NSL):
        ysb = ysbs[s % 2]
        rstd = rstds[s % 2]
        c0 = s * SL

        def rhs_fn(k, c0=c0):
            if k < 4:
                return ya_t[:, k, c0:c0 + SL]
            return yb.t[:, k - 4, c0:c0 + SL]

        cx.out_phase(wout, rhs_fn, [ya_r[s], yb_r[s]], xT[:, :, c0:c0 + SL], outT[:, :, c0:c0 + SL],
                     ysb, ysqs, rstd, xcs, tmps, obufs, halo_d=(io.get("halo_out") if s == NSL - 1 else None))


PAIRS = [[0, 1], [2, 3], [4, 5], [6, 7]]
_DBG_NOCC = False
LAYER_KEYS_EVEN = ["w_in", "w_uq", "w_ukvk", "w_ukvv", "w_out", "scw", "scb", "qg", "kvg"]
LAYER_SHAPES_EVEN = {"w_in": [128, 8, 3008], "w_uq": [128, 2, 1024], "w_ukvk": [128, 1, 512], "w_ukvv": [128, 1, 512],
                     "w_out": [128, 8, 1024], "scw": [128, 4, 3], "scb": [128, 4], "qg": [128, 2], "kvg": [128, 1]}
LAYER_SHAPES_ODD = {"w_in": [128, 8, 3072], "w_out": [128, 8, 1024], "convw": [128, 8, 31], "convb": [128, 8],
                    "lng": [128, 8], "lnb": [128, 8]}
LAYER_SHAPES_COMMON = {"adaw": [128, 8, 3072], "adab": [128, 24], "preg": [128, 8], "postg": [128, 8]}


def build_fused():
    nc = bass.Bass("TRN2", target_bir_lowering=False)
    xT = dram_in(nc, "xT", [128, 8, NT])
    xH0 = dram_in(nc, "xH", [128, 8, 32])
    posO = dram_in(nc, "posO", [1, NT], I32)
    cT = dram_in(nc, "cT", [128, 8])
    hflag = dram_in(nc, "hflag", [128, 1])
    invf_d = dram_in(nc, "invf2", [32, 1])
    sgn_d = dram_in(nc, "sgn2", [32, 1])
    kb_d = dram_in(nc, "keybias", [128, 32])
    identd = dram_in(nc, "ident", [128, 128])
    L = []
    for l in range(4):
        d = {}
        shapes = dict(LAYER_SHAPES_COMMON)
        shapes.update(LAYER_SHAPES_EVEN if l % 2 == 0 else LAYER_SHAPES_ODD)
        for k, shp in shapes.items():
            d[k] = dram_in(nc, "L%d_%s" % (l, k), shp)
        L.append(d)
    outT = nc.dram_tensor("outT", [128, 8, NT], F32, kind="ExternalOutput").ap()
    xs_raw = [nc.dram_tensor("xs%d" % l, [8, 128, NT], F32, kind="Internal").ap() for l in range(3)]
    xs = [t_.rearrange("c p t -> p c t") for t_ in xs_raw]
    hal = [nc.dram_tensor("hal%d" % l, [128, 8, 32], F32, kind="Internal").ap() for l in range(3)]
    halg = [nc.dram_tensor("halg%d" % l, [2, 128, 8, 32], F32, kind="Internal").ap() for l in range(3)]
    lats = {l: nc.dram_tensor("lat%d" % l, [160, NT], BF16, kind="Internal").ap() for l in (0, 2)}
    latgs = {l: nc.dram_tensor("latg%d" % l, [2, 160, NT], BF16, kind="Internal").ap() for l in (0, 2)}

    cx = Ctx(nc)
    P = cx.P
    cx.setup_consts(identd, hflag, invf_d, sgn_d, kb_d)
    cx.ada_setup(cT)
    ar = Arena(P, "arena", 49 * 1024 + 512)
    Rhalg = [P.res("halg%d" % l) for l in range(3)]
    Rsrc = P.res("ccsrc")

    for l in range(4):
        io = dict(L[l])
        io["cT"] = cT
        io["layer"] = l
        if l == 0:
            io["ada_jobs"] = [(l2, L[l2]) for l2 in (1, 2, 3)]
        io["xT"] = xT if l == 0 else xs[l - 1]
        io["out"] = outT if l == 3 else xs[l]
        if l < 3:
            io["halo_out"] = hal[l]
        pre = []
        if l == 0 or _DBG_NOCC:
            io["xF"], io["xH"] = None, xH0
        else:
            def gather_halo(l=l):
                P.cc("AllGather", hal[l - 1].rearrange("p c t -> p (c t)"), halg[l - 1].rearrange("r p c t -> (r p) (c t)"), PAIRS, reads=[Rsrc], writes=[Rhalg[l - 1]])
            pre.append(gather_halo)
            io["xH"] = halg[l - 1][0]
            io["xHres"] = [Rhalg[l - 1]]
            io["xFres"] = [Rhalg[l - 1]]
        io["pre"] = pre
        io["posO"], io["posF"] = posO, None
        if l % 2 == 0:
            io["lat"], io["latg"] = lats[l], latgs[l]
            io.setdefault("xF", None)
        if l % 2 == 0:
            emit_even(cx, ar, io)
        else:
            emit_odd(cx, ar, io)
    P.emit()
    return nc


def fm(v):
    v = np.asarray(v)
    return np.ascontiguousarray(v.reshape(-1, 128).T)


def wk(w):
    w = np.asarray(w)
    k, n = w.shape
    return np.ascontiguousarray(w.reshape(k // 128, 128, n).transpose(1, 0, 2))


def xfm(xb):
    t = xb.shape[0]
    return np.ascontiguousarray(xb.T.reshape(8, 128, t).transpose(1, 0, 2))


def xfm_inv(o):
    t = o.shape[2]
    return o.transpose(1, 0, 2).reshape(D, t).T


def _perm_uq(w_uq):
    w = np.asarray(w_uq)
    blocks = []
    for h in range(8):
        b = h * 96
        blocks += [w[:, b:b + 64], w[:, b + 64:b + 96], w[:, b + 80:b + 96], w[:, b + 64:b + 80]]
    return np.concatenate(blocks, axis=1)


_NC_CACHE = {}


def kernel(x, c, positions, ada_w, ada_b, pre_norm_g, post_norm_g,
           even_w_in, even_sc_conv_w, even_sc_conv_b, even_q_norm_g, even_kv_norm_g,
           even_w_uq, even_w_ukv, even_w_out,
           odd_w_in, odd_conv_w, odd_conv_b, odd_ln_g, odd_ln_b, odd_w_out):
    x = np.ascontiguousarray(np.asarray(x, dtype=np.float32))
    c = np.asarray(c, dtype=np.float32)
    positions = np.asarray(positions)
    if "fused" not in _NC_CACHE:
        _NC_CACHE["fused"] = build_fused()
    nc = _NC_CACHE["fused"]
    invf = (np.float32(1.0) / (np.float32(10000.0) ** (np.arange(0, 32, 2, dtype=np.float32) / np.float32(32)))).astype(np.float32)
    shared = {
        "ident": np.eye(128, dtype=np.float32),
        "invf2": np.ascontiguousarray(np.concatenate([invf, invf])[:, None]),
        "sgn2": np.ascontiguousarray(np.concatenate([-np.ones(16, np.float32), np.ones(16, np.float32)])[:, None]),
    }
    for l in range(4):
        i = l // 2
        p = "L%d_" % l
        shared[p + "adaw"] = wk(ada_w[l])
        shared[p + "adab"] = fm(ada_b[l])
        shared[p + "preg"] = fm(pre_norm_g[l])
        shared[p + "postg"] = fm(post_norm_g[l])
        if l % 2 == 0:
            w_in = np.asarray(even_w_in[i])
            w_ukv = np.asarray(even_w_ukv[i])
            shared[p + "w_in"] = wk(np.concatenate([w_in, w_in[:, 2448:2464], w_in[:, 2432:2448]], axis=1))
            shared[p + "w_uq"] = wk(_perm_uq(even_w_uq[i]))
            shared[p + "w_ukvk"] = np.ascontiguousarray(np.concatenate([w_ukv[:, h * 128:h * 128 + 64] for h in range(8)], axis=1)[:, None, :])
            shared[p + "w_ukvv"] = np.ascontiguousarray(np.concatenate([w_ukv[:, h * 128 + 64:h * 128 + 128] for h in range(8)], axis=1)[:, None, :])
            shared[p + "w_out"] = wk(even_w_out[i])
            shared[p + "scw"] = np.ascontiguousarray(np.asarray(even_sc_conv_w[i]).T.reshape(4, 128, 3).transpose(1, 0, 2))
            shared[p + "scb"] = fm(even_sc_conv_b[i])
            shared[p + "qg"] = fm(even_q_norm_g[i])
            shared[p + "kvg"] = fm(even_kv_norm_g[i])
        else:
            shared[p + "w_in"] = wk(odd_w_in[i])
            shared[p + "w_out"] = wk(odd_w_out[i])
            shared[p + "convw"] = np.ascontiguousarray(np.asarray(odd_conv_w[i]).T.reshape(8, 128, 31).transpose(1, 0, 2))
            shared[p + "convb"] = fm(odd_conv_b[i])
            shared[p + "lng"] = fm(odd_ln_g[i])
            shared[p + "lnb"] = fm(odd_ln_b[i])
    maps = []
    for core in range(8):
        b, half = core // 2, core % 2
        t0 = half * NT
        m = dict(shared)
        m["xT"] = xfm(x[b, t0:t0 + NT])
        m["xH"] = xfm(x[b, NT - 32:NT]) if half == 1 else np.zeros((128, 8, 32), np.float32)
        m["cT"] = fm(c[b])
        m["hflag"] = np.full((128, 1), float(half), np.float32)
        m["posO"] = np.ascontiguousarray(positions[b, t0:t0 + NT][None, :]).astype(np.int32)
        kbias = np.zeros((128, 32), np.float32)
        if half == 0:
            kbias[:, 0:16] = -30000.0
        m["keybias"] = kbias
        maps.append(m)
    res = run_bass_kernel_spmd(nc, maps, core_ids=list(range(8)))
    out = np.empty_like(x)
    for core in range(8):
        b, half = core // 2, core % 2
        out[b, half * NT:(half + 1) * NT] = xfm_inv(res.results[core]["outT"])
    return out
```

```python
from contextlib import ExitStack
import numpy as np
import concourse.bass as bass
import concourse.mybir as mybir
from concourse.bass_utils import run_bass_kernel_spmd

F32 = mybir.dt.float32
BF16 = mybir.dt.bfloat16
I32 = mybir.dt.int32
ALU = mybir.AluOpType
AF = mybir.ActivationFunctionType

D = 1024
NT = 2048
SL = 512
NSL = 4
EPS = 1e-6
PI = 3.141592653589793
TWO_PI = 6.283185307179586

ENGS = ("pe", "dve", "act", "pool", "sp")
SAME_ENGINE_SYNC = True


class Sem:
    def __init__(self, handle, name, is_dma):
        self.h = handle
        self.name = name
        self.total = 0
        self.is_dma = is_dma


class Res:
    __slots__ = ("name", "writer", "readers", "dsem")

    def __init__(self, name):
        self.name = name
        self.writer = None
        self.readers = []
        self.dsem = None


class Buf:
    __slots__ = ("t", "r")

    def __init__(self, t, r):
        self.t = t
        self.r = r


class Prog:
    def __init__(self, nc):
        self.nc = nc
        self.stack = ExitStack()
        self.ops = {e: [] for e in ENGS}
        self.esem = {}
        for e in ENGS:
            self.esem[e] = Sem(self.stack.enter_context(nc.semaphore("s_" + e)), e, False)
        self.waited = {e: {} for e in ENGS}
        self.dma_sems = []
        self.free_dsems = []
        self.all_res = []
        self.nm = 0

    def sbuf(self, name, shape, dtype):
        return self.stack.enter_context(self.nc.sbuf_tensor("sb_" + name, list(shape), dtype))

    def psum(self, name, shape, dtype=F32):
        return self.stack.enter_context(self.nc.psum_tensor("ps_" + name, list(shape), dtype))

    def res(self, name="r"):
        r = Res(name)
        self.all_res.append(r)
        return r

    def buf(self, name, shape, dtype):
        return Buf(self.sbuf(name, shape, dtype), self.res(name))

    def _dsem(self, r):
        if r.dsem is None:
            if self.free_dsems:
                r.dsem = self.free_dsems.pop()
            else:
                self.nm += 1
                nm = "d%d" % self.nm
                r.dsem = Sem(self.stack.enter_context(self.nc.semaphore(nm)), nm, True)
                self.dma_sems.append(r.dsem)
        return r.dsem

    def _collect(self, eng, reads, writes):
        need = {}

        def add(tok):
            if tok is None:
                return
            sem, val, teng = tok
            if teng == eng and (eng == "pe" or not SAME_ENGINE_SYNC):
                return
            if sem.is_dma:
                val = max(val, sem.total)
            if need.get(sem, 0) < val:
                need[sem] = val

        for r in reads:
            add(r.writer)
        for w in writes:
            add(w.writer)
            for t in w.readers:
                add(t)
        wl = []
        wd = self.waited[eng]
        for sem, val in need.items():
            if wd.get(sem, 0) < val:
                wd[sem] = val
                wl.append((sem, val))
        return wl

    def _commit(self, tok, reads, writes):
        for r in reads:
            r.readers.append(tok)
            if len(r.readers) > 64:
                best = {}
                for t in r.readers:
                    if best.get(t[0], (None, 0, None))[1] < t[1]:
                        best[t[0]] = t
                r.readers = list(best.values())
        for w in writes:
            w.writer = tok
            w.readers = []

    def op(self, eng, fn, reads=(), writes=(), inc=True):
        waits = self._collect(eng, reads, writes)
        sem = self.esem[eng]
        if inc:
            sem.total += 1
            tok = (sem, sem.total, eng)
        else:
            tok = (sem, sem.total + 1, eng)
        self.ops[eng].append((waits, fn, (sem, 1) if inc else None))
        self._commit(tok, reads, writes)

    def dma(self, eng, out_ap, in_ap, reads=(), writes=(), sem_res=None):
        waits = self._collect(eng, reads, writes)
        if sem_res is None:
            sem_res = writes[0] if writes else reads[0]
        sem = self._dsem(sem_res)
        sem.total += 16
        tok = (sem, sem.total, "dma")

        def fn(e, out_ap=out_ap, in_ap=in_ap):
            return e.dma_start(out=out_ap, in_=in_ap)

        self.ops[eng].append((waits, fn, (sem, 16)))
        self._commit(tok, reads, writes)

    def cc(self, kind, in_ap, out_ap, groups, reads=(), writes=()):
        waits = self._collect("pool", reads, writes)
        sem = self._dsem(writes[0])
        sem.total += 1
        tok = (sem, sem.total, "dma")

        def fn(e):
            return e.collective_compute(kind, ALU.bypass, replica_groups=groups, ins=[in_ap], outs=[out_ap])

        self.ops["pool"].append((waits, fn, (sem, 1)))
        self._commit(tok, reads, writes)

    def barrier(self):
        for e in ENGS:
            waits = []
            wd = self.waited[e]
            for x in ENGS:
                s = self.esem[x]
                if x != e and s.total > wd.get(s, 0):
                    wd[s] = s.total
                    waits.append((s, s.total))
            for s in self.dma_sems:
                if s.total > wd.get(s, 0):
                    wd[s] = s.total
                    waits.append((s, s.total))
            if waits:
                self.ops[e].append((waits, None, None))
        for r in self.all_res:
            r.writer = None
            r.readers = []
            if r.dsem is not None:
                self.free_dsems.append(r.dsem)
                r.dsem = None

    def emit(self):
        nc = self.nc
        self.barrier()
        engmap = {"pe": "tensor", "dve": "vector", "act": "scalar", "pool": "gpsimd", "sp": "sync"}
        with nc.Block() as block:
            for e in ENGS:
                ops = self.ops[e]

                def body(eobj, ops=ops):
                    for waits, fn, inc in ops:
                        for sem, val in waits:
                            eobj.wait_ge(sem.h, val)
                        if fn is not None:
                            ins = fn(eobj)
                            if inc is not None:
                                ins.then_inc(inc[0].h, inc[1])

                getattr(block, engmap[e])(body)
        self.stack.close()


class Arena:
    def __init__(self, P, name, words):
        self.P = P
        self.t = P.sbuf(name, [128, words], F32)
        self.words = words
        self.off = 0

    def reset(self, mark=0):
        self.off = mark

    def mark(self):
        return self.off

    def take(self, shape, dtype, name="a"):
        n = 1
        for s in shape[1:]:
            n *= s
        esz = 4 if dtype in (F32, I32) else 2
        w = (n * esz + 3) // 4
        w = (w + 7) // 8 * 8
        assert self.off + w <= self.words, ("arena overflow", name, self.off, w, self.words)
        v = self.t[:, self.off:self.off + w]
        self.off += w
        if dtype != F32:
            v = v.bitcast(dtype)
        v = v[:, 0:n]
        if len(shape) == 3:
            v = v.rearrange("p (a b) -> p a b", a=shape[1])
        elif len(shape) == 4:
            v = v.rearrange("p (a b c) -> p a b c", a=shape[1], b=shape[2])
        if shape[0] != 128:
            v = v[0:shape[0]]
        return Buf(v, self.P.res(name))


class Ctx:
    def __init__(self, nc):
        self.nc = nc
        self.P = Prog(nc)
        P = self.P
        self.banks = [Buf(P.psum("bank%d" % i, [128, 512]), P.res("bank%d" % i)) for i in range(8)]
        self.bi = 0
        self.nbanks_rr = 8
        self.stg_i = 0

    def nb(self):
        b = self.banks[self.bi % self.nbanks_rr]
        self.bi += 1
        return b

    def tt(self, eng, out, in0, in1, op, reads, writes):
        self.P.op(eng, lambda e: e.tensor_tensor(out=out, in0=in0, in1=in1, op=op), reads, writes)

    def ts(self, eng, out, in0, s1, op0, reads, writes, s2=None, op1=None):
        if op1 is None:
            self.P.op(eng, lambda e: e.tensor_scalar(out=out, in0=in0, scalar1=s1, scalar2=None, op0=op0), reads, writes)
        else:
            self.P.op(eng, lambda e: e.tensor_scalar(out=out, in0=in0, scalar1=s1, scalar2=s2, op0=op0, op1=op1), reads, writes)

    def stt(self, out, in0, scalar, in1, op0, op1, reads, writes):
        self.P.op("dve", lambda e: e.scalar_tensor_tensor(out=out, in0=in0, scalar=scalar, in1=in1, op0=op0, op1=op1), reads, writes)

    def act(self, out, in_, func, reads, writes, bias=None, scale=None):
        kw = {}
        if bias is not None:
            kw["bias"] = bias
        if scale is not None:
            kw["scale"] = scale
        self.P.op("act", lambda e: e.activation(out=out, in_=in_, func=func, **kw), reads, writes)

    def copy(self, eng, out, in_, reads, writes):
        if eng == "act":
            self.P.op("act", lambda e: e.activation(out=out, in_=in_, func=AF.Copy), reads, writes)
        else:
            self.P.op(eng, lambda e: e.tensor_copy(out=out, in_=in_), reads, writes)

    def memset(self, eng, ap, val, writes):
        self.P.op(eng, lambda e: e.memset(ap, val), (), writes)

    def mm(self, out, lhsT, rhs, start, stop, reads, writes, inc=None):
        if inc is None:
            inc = stop
        self.P.op("pe", lambda e: e.matmul(out, lhsT=lhsT, rhs=rhs, start=start, stop=stop), reads, writes, inc=inc)

    def load(self, buf_ap, dram_ap, writes, eng="sp"):
        self.P.dma(eng, buf_ap, dram_ap, writes=writes)

    def setup_consts(self, ident_dram, hflag_d, invf_d, sgn_d, kb_d):
        P = self.P
        self.ones_bf = P.buf("ones_bf", [128, 128], BF16)
        self.memset("pool", self.ones_bf.t[:], 1.0, [self.ones_bf.r])
        self.ones_f = P.buf("ones_f", [128, 2], F32)
        self.memset("pool", self.ones_f.t[:], 1.0, [self.ones_f.r])
        self.epsb = P.buf("epsb", [128, 1], F32)
        self.memset("pool", self.epsb.t[:], EPS, [self.epsb.r])
        self.ident = P.buf("ident_bf", [128, 128], BF16)
        P.dma("pool", self.ident.t[:], ident_dram, writes=[self.ident.r])
        self.hf = P.buf("hf", [128, 1], F32)
        self.invf = P.buf("invf", [32, 1], F32)
        self.sgn = P.buf("sgn", [32, 1], F32)
        self.keyb = P.buf("keyb", [128, 32], F32)
        for b_, d_ in ((self.hf, hflag_d), (self.invf, invf_d), (self.sgn, sgn_d), (self.keyb, kb_d)):
            self.load(b_.t[:], d_, [b_.r])

    def ada_setup(self, cT_d):
        P = self.P
        self.cT = P.buf("cT", [128, 8], F32)
        self.cact = P.buf("cact", [128, 8], F32)
        self.load(self.cT.t[:], cT_d, [self.cT.r])
        self.act(self.cact.t[:], self.cT.t[:], AF.Silu, [self.cT.r], [self.cact.r])
        self.ada_sets = []
        for l in range(4):
            d = {k: P.buf("%s_%d" % (k, l), [128, n], F32) for k, n in
                 (("adab", 24), ("preg", 8), ("postg", 8), ("modT", 24), ("A", 8), ("G2", 8))}
            d["done"] = False
            self.ada_sets.append(d)

    def ada_select(self, l):
        d = self.ada_sets[l]
        self.adab, self.preg, self.postg = d["adab"], d["preg"], d["postg"]
        self.modT, self.A, self.G2 = d["modT"], d["A"], d["G2"]
        return d

    def ada_steps(self, l, io, wb, acc):
        d = self.ada_sets[l]
        adab, preg, postg, modT, A, G2 = d["adab"], d["preg"], d["postg"], d["modT"], d["A"], d["G2"]
        cact = self.cact
        self.load(adab.t[:], io["adab"], [adab.r])
        self.load(preg.t[:], io["preg"], [preg.r])
        self.load(postg.t[:], io["postg"], [postg.r])
        nb_ = len(wb)

        def dve_piece(kc):
            b = wb[kc % nb_]
            if kc == 0:
                self.ts("dve", acc.t[:], b.t[:], cact.t[:, 0:1], ALU.mult, [b.r, cact.r], [acc.r])
            else:
                self.stt(acc.t[:], b.t[:], cact.t[:, kc:kc + 1], acc.t[:], ALU.mult, ALU.add, [b.r, cact.r, acc.r], [acc.r])

        for g in range(0, 8, nb_):
            for kc in range(g, min(8, g + nb_)):
                b = wb[kc % nb_]
                self.load(b.t[:], io["adaw"][:, kc, :], [b.r])
            yield
            for kc in range(g, min(8, g + nb_)):
                dve_piece(kc)
        yield
        bank = self.nb()
        for m in range(24):
            self.mm(bank.t[:, 2 * m:2 * m + 2], acc.t[:, m * 128:(m + 1) * 128], self.ones_f.t[:, 0:2], True, True,
                    [acc.r, self.ones_f.r], [bank.r], inc=True)
        pv = bank.t[:, 0:48].rearrange("p (m t) -> p m t", t=2)[:, :, 0]
        self.tt("dve", modT.t[:], pv, adab.t[:], ALU.add, [bank.r, adab.r], [modT.r])
        self.stt(A.t[:], modT.t[:, 8:16], 1.0, preg.t[:], ALU.add, ALU.mult, [modT.r, preg.r], [A.r])
        self.tt("dve", G2.t[:], modT.t[:, 16:24], postg.t[:], ALU.mult, [modT.r, postg.r], [G2.r])
        d["done"] = True
        yield

    def ada_phase(self, ar, l, io):
        if self.ada_sets[l]["done"]:
            return False
        wb = [ar.take([128, 3072], F32, "adaw%d" % i) for i in range(2)]
        acc = ar.take([128, 3072], F32, "adaacc")
        for _ in self.ada_steps(l, io, wb, acc):
            pass
        return True

    def rstd_from(self, ssbank, n, dim, out):
        self.act(out.t[:, 0:n], ssbank.t[:, 0:n], AF.Ln, [ssbank.r, self.epsb.r], [out.r], bias=self.epsb.t[:], scale=1.0 / dim)
        self.act(out.t[:, 0:n], out.t[:, 0:n], AF.Exp, [out.r], [out.r], scale=-0.5)

    def prenorm(self, x_d, n, h, xcs, xsqs, rstd, tmps, xres=()):
        ss = self.nb()
        for c in range(8):
            xc = xcs[c % len(xcs)]
            self.P.dma("sp", xc.t[:, 0:n], x_d[:, c, :], reads=list(xres), writes=[xc.r], sem_res=xc.r)
            sq = xsqs[c % len(xsqs)]
            self.act(sq.t[:, 0:n], xc.t[:, 0:n], AF.Square, [xc.r], [sq.r])
            self.mm(ss.t[:, 0:n], self.ones_bf.t[:], sq.t[:, 0:n], c == 0, c == 7, [self.ones_bf.r, sq.r], [ss.r], inc=True)
        self.rstd_from(ss, n, float(D), rstd)
        for c in range(8):
            xc = xcs[c % len(xcs)]
            self.P.dma("sp", xc.t[:, 0:n], x_d[:, c, :], reads=list(xres), writes=[xc.r], sem_res=xc.r)
            tm = tmps[c % len(tmps)]
            self.stt(tm.t[:, 0:n], xc.t[:, 0:n], self.A.t[:, c:c + 1], rstd.t[:, 0:n], ALU.mult, ALU.mult,
                     [xc.r, self.A.r, rstd.r], [tm.r])
            self.act(h.t[:, c, 0:n], tm.t[:, 0:n], AF.Identity, [tm.r, self.modT.r], [h.r], bias=self.modT.t[:, c:c + 1])

    def out_mm(self, wout, rhs_fn, rhs_res, ysb, ysqs, rstd):
        ss = self.banks[7]
        self.nbanks_rr = 7
        for m in range(8):
            bank = self.nb()
            for k in range(8):
                self.mm(bank.t[:], wout.t[:, k, m * 128:(m + 1) * 128], rhs_fn(k), k == 0, k == 7,
                        [wout.r] + rhs_res, [bank.r])
            self.copy("act", ysb.t[:, m, :], bank.t[:], [bank.r], [ysb.r])
            sq = ysqs[m % len(ysqs)]
            self.act(sq.t[:], bank.t[:], AF.Square, [bank.r], [sq.r])
            self.mm(ss.t[:], self.ones_bf.t[:], sq.t[:], m == 0, m == 7, [self.ones_bf.r, sq.r], [ss.r], inc=True)
        self.rstd_from(ss, SL, float(D), rstd)
        self.nbanks_rr = 8

    def out_res(self, x_d, out_d, ysb, rstd, xcs, tmps, obufs, xres=(), halo_d=None):
        for c in range(8):
            xc = xcs[c % len(xcs)]
            self.P.dma("sp", xc.t[:], x_d[:, c, :], reads=list(xres), writes=[xc.r], sem_res=xc.r)
            tm = tmps[c % len(tmps)]
            self.tt("dve", tm.t[:], ysb.t[:, c, :], rstd.t[:], ALU.mult, [ysb.r, rstd.r], [tm.r])
            ob = obufs[c % len(obufs)]
            self.stt(ob.t[:], tm.t[:], self.G2.t[:, c:c + 1], xc.t[:], ALU.mult, ALU.add, [tm.r, self.G2.r, xc.r], [ob.r])
            self.P.dma("sp", out_d[:, c, :], ob.t[:], reads=[ob.r], sem_res=ob.r)
            if halo_d is not None:
                self.P.dma("sp", halo_d[:, c, :], ob.t[:, SL - 32:SL], reads=[ob.r], sem_res=ob.r)

    def out_phase(self, wout, rhs_fn, rhs_res, x_d, out_d, ysb, ysqs, rstd, xcs, tmps, obufs, xres=(), ores=(), halo_d=None):
        self.out_mm(wout, rhs_fn, rhs_res, ysb, ysqs, rstd)
        self.out_res(x_d, out_d, ysb, rstd, xcs, tmps, obufs, xres=xres, halo_d=halo_d)

    def load_weight_bf16(self, wbuf_ap_fn, dram, nk, ncols, res, stg):
        for k in range(nk):
            c0 = 0
            while c0 < ncols:
                c1 = min(ncols, c0 + 2048)
                i = self.stg_i
                self.stg_i += 1
                sb = stg[i % len(stg)]
                self.P.dma("sp", sb.t[:, 0:c1 - c0], dram[:, k, c0:c1], writes=[sb.r])
                self.copy("dve" if i % 2 == 0 else "act", wbuf_ap_fn(k, c0, c1), sb.t[:, 0:c1 - c0], [sb.r], [res])
                c0 = c1


def dram_in(nc, name, shape, dtype=F32):
    return nc.dram_tensor(name, list(shape), dtype, kind="ExternalInput").ap()


def emit_odd(cx, ar, io):
    P = cx.P
    xT, xH, outT = io["xT"], io["xH"], io["out"]
    xHres = io.get("xHres", ())
    P.barrier()
    ar.reset()
    cx.ada_select(io["layer"])
    win = ar.take([128, 8, 3072], BF16, "win")
    wout = ar.take([128, 8, 1024], BF16, "wout")
    cw = ar.take([128, 8, 31], F32, "cw")
    cb = ar.take([128, 8], F32, "cb")
    lg = ar.take([128, 8], F32, "lg")
    lb = ar.take([128, 8], F32, "lb")
    base = ar.mark()
    stg = [ar.take([128, 2048], F32, "stg%d" % i) for i in range(4)]
    cx.load_weight_bf16(lambda k, a, b: win.t[:, k, a:b], io["w_in"], 8, 3072, win.r, stg)
    cx.load_weight_bf16(lambda k, a, b: wout.t[:, k, a:b], io["w_out"], 8, 1024, wout.r, stg)
    for pre in io.get("pre", ()):
        pre()
    for b_, d_ in ((cw, io["convw"]), (cb, io["convb"]), (lg, io["lng"]), (lb, io["lnb"])):
        cx.load(b_.t[:], d_, [b_.r])
    hf = cx.hf
    cx.ada_phase(ar, io["layer"], io)
    P.barrier()
    ar.reset(base)

    xcs = [ar.take([128, 512], F32, "xc%d" % i) for i in range(3)]
    xsqs = [ar.take([128, 512], BF16, "xsq%d" % i) for i in range(2)]
    tmps = [ar.take([128, 512], F32, "tmp%d" % i) for i in range(3)]
    rstd = ar.take([128, 512], F32, "rstd")
    hs = [ar.take([128, 8, 512], BF16, "h%d" % i) for i in range(2)]
    u = [ar.take([128, 544], BF16, "u%d" % j) for j in range(8)]
    convs = [ar.take([128, 8, 512], F32, "conv%d" % i) for i in range(2)]
    sgates = [ar.take([128, 8, 512], BF16, "sgate%d" % i) for i in range(2)]
    gated = ar.take([128, 8, 512], BF16, "gated")
    diag = [ar.take([128, 31, 128], BF16, "diag%d" % i) for i in range(2)]
    cbf = [ar.take([128, 512], BF16, "cbf%d" % i) for i in range(2)]
    csq = [ar.take([128, 512], BF16, "csq%d" % i) for i in range(2)]
    means = [ar.take([128, 512], F32, "mean%d" % i) for i in range(2)]
    vars_ = [ar.take([128, 512], F32, "var%d" % i) for i in range(2)]
    obufs = [ar.take([128, 512], F32, "ob%d" % i) for i in range(2)]
    rstd_o = ar.take([128, 512], F32, "rstd_o")
    ysqs_o = [ar.take([128, 512], BF16, "ysqo%d" % i) for i in range(2)]

    hh = hs[1]
    cx.prenorm(xH, 32, hh, xcs, xsqs, rstd, tmps, xres=xHres)
    for j in range(8):
        bv = cx.nb()
        bg = cx.nb()
        for k in range(8):
            cx.mm(bv.t[:, 0:32], win.t[:, k, j * 128:(j + 1) * 128], hh.t[:, k, 0:32], k == 0, k == 7, [win.r, hh.r], [bv.r])
        for k in range(8):
            cx.mm(bg.t[:, 0:32], win.t[:, k, 1024 + j * 128:1024 + (j + 1) * 128], hh.t[:, k, 0:32], k == 0, k == 7, [win.r, hh.r], [bg.r])
        tm = tmps[j % 3]
        cx.act(tm.t[:, 0:32], bg.t[:, 0:32], AF.Sigmoid, [bg.r], [tm.r])
        cx.tt("dve", tm.t[:, 0:32], bv.t[:, 0:32], tm.t[:, 0:32], ALU.mult, [bv.r, tm.r], [tm.r])
        cx.ts("dve", u[j].t[:, 512:542], tm.t[:, 2:32], hf.t[:, 0:1], ALU.mult, [tm.r, hf.r], [u[j].r])

    def A_pre(s):
        cx.prenorm(xT[:, :, s * SL:(s + 1) * SL], SL, hs[s % 2], xcs, xsqs, rstd, tmps)

    def A_body(s, between=None):
        conv, sgate, mean, var = convs[s % 2], sgates[s % 2], means[s % 2], vars_[s % 2]
        h = hs[s % 2]
        mbank = cx.banks[6]
        ebank = cx.banks[7]
        cx.nbanks_rr = 6
        wb = {}

        def emit_win(j):
            b0 = 3 * (j % 2)
            bv, bg, bs = cx.banks[b0], cx.banks[b0 + 1], cx.banks[b0 + 2]
            for (bk, off) in ((bv, 0), (bg, 1024), (bs, 2048)):
                for k in range(8):
                    cx.mm(bk.t[:], win.t[:, k, off + j * 128:off + (j + 1) * 128], h.t[:, k, :], k == 0, k == 7, [win.r, h.r], [bk.r])
            dg = diag[j % 2]
            cx.tt("dve", dg.t[:], cx.ident.t[:].unsqueeze(1).to_broadcast([128, 31, 128]),
                  cw.t[:, j, :].unsqueeze(2).to_broadcast([128, 31, 128]), ALU.mult, [cx.ident.r, cw.r], [dg.r])
            wb[j] = (bv, bg, bs, dg)

        emit_win(0)
        for j in range(8):
            if j + 1 < 8:
                emit_win(j + 1)
            bv, bg, bs, dg = wb.pop(j)
            cx.copy("pool", u[j].t[:, 0:30], u[j].t[:, 512:542], [u[j].r], [u[j].r])
            tm = tmps[j % 3]
            cx.act(tm.t[:], bg.t[:], AF.Sigmoid, [bg.r], [tm.r])
            cx.tt("dve", u[j].t[:, 30:542], bv.t[:], tm.t[:], ALU.mult, [bv.r, tm.r], [u[j].r])
            cx.act(sgate.t[:, j, :], bs.t[:], AF.Silu, [bs.r], [sgate.r])
            cb_ = bg
            for k in range(31):
                cx.mm(cb_.t[:], dg.t[:, k, :], u[j].t[:, k:k + 512], k == 0, k == 30, [dg.r, u[j].r], [cb_.r])
            cx.act(conv.t[:, j, :], cb_.t[:], AF.Identity, [cb_.r, cb.r], [conv.r], bias=cb.t[:, j:j + 1])
            q_ = csq[j % 2]
            cx.act(q_.t[:], cb_.t[:], AF.Square, [cb_.r, cb.r], [q_.r], bias=cb.t[:, j:j + 1])
            c_ = cbf[j % 2]
            cx.copy("dve", c_.t[:], conv.t[:, j, :], [conv.r], [c_.r])
            cx.mm(mbank.t[:], cx.ones_bf.t[:], c_.t[:], j == 0, j == 7, [cx.ones_bf.r, c_.r], [mbank.r], inc=True)
            cx.mm(ebank.t[:], cx.ones_bf.t[:], q_.t[:], j == 0, j == 7, [cx.ones_bf.r, q_.r], [ebank.r], inc=True)
            if between is not None:
                between(j)
        cx.nbanks_rr = 8
        cx.ts("dve", mean.t[:], mbank.t[:], 1.0 / D, ALU.mult, [mbank.r], [mean.r])
        cx.tt("dve", var.t[:], mean.t[:], mean.t[:], ALU.mult, [mean.r], [var.r])
        cx.stt(var.t[:], ebank.t[:], 1.0 / D, var.t[:], ALU.mult, ALU.subtract, [ebank.r, var.r], [var.r])
        cx.act(var.t[:], var.t[:], AF.Ln, [var.r, cx.epsb.r], [var.r], bias=cx.epsb.t[:])
        cx.act(var.t[:], var.t[:], AF.Exp, [var.r], [var.r], scale=-0.5)
    def B_norm_j(s, j):
        conv, sgate, mean, var = convs[s % 2], sgates[s % 2], means[s % 2], vars_[s % 2]
        tm = tmps[j % 3]
        cx.tt("dve", tm.t[:], conv.t[:, j, :], mean.t[:], ALU.subtract, [conv.r, mean.r], [tm.r])
        cx.tt("dve", tm.t[:], tm.t[:], var.t[:], ALU.mult, [tm.r, var.r], [tm.r])
        cx.act(tm.t[:], tm.t[:], AF.Silu, [tm.r, lg.r, lb.r], [tm.r], bias=lb.t[:, j:j + 1], scale=lg.t[:, j:j + 1])
        cx.tt("pool", gated.t[:, j, :], tm.t[:], sgate.t[:, j, :], ALU.mult, [tm.r, sgate.r], [gated.r])

    def B_mm(s):
        cx.out_mm(wout, lambda k: gated.t[:, k, :], [gated.r], convs[s % 2], ysqs_o, rstd_o)

    def B_res(s):
        cx.out_res(xT[:, :, s * SL:(s + 1) * SL], outT[:, :, s * SL:(s + 1) * SL], convs[s % 2], rstd_o, xcs, tmps, obufs,
                   halo_d=(io.get("halo_out") if s == NSL - 1 else None))

    A_pre(0)
    A_body(0)
    A_pre(1)
    for s in range(1, NSL):
        A_body(s, between=lambda j, s=s: B_norm_j(s - 1, j))
        B_mm(s - 1)
        if s + 1 < NSL:
            A_pre(s + 1)
        B_res(s - 1)
    for j in range(8):
        B_norm_j(NSL - 1, j)
    B_mm(NSL - 1)
    B_res(NSL - 1)


def emit_even(cx, ar, io):
    P = cx.P
    xT, xF, xH, outT = io["xT"], io["xF"], io["xH"], io["out"]
    xFres = io.get("xFres", ())
    posO, posF = io["posO"], io["posF"]
    SCALE = 1.0 / float(np.sqrt(96.0))
    hf, invf, sgn, keyb = cx.hf, cx.invf, cx.sgn, cx.keyb
    P.barrier()
    ar.reset()
    cx.ada_select(io["layer"])
    wuq = ar.take([128, 2, 1024], BF16, "wuq")
    wkk = ar.take([128, 512], BF16, "wkk")
    wvv = ar.take([128, 512], BF16, "wvv")
    scw = ar.take([128, 4, 3], F32, "scw")
    scb = ar.take([128, 4], F32, "scb")
    qg = ar.take([128, 2], F32, "qg")
    kvg = ar.take([128, 1], F32, "kvg")
    cxh = ar.take([128, 4, 2], F32, "cxh")
    ckvn_t = ar.take([128, 4096], BF16, "ckvn_all").t
    ckvn_r = [P.res("ckvn%d" % i) for i in range(8)]
    krope_t = ar.take([128, 4096], BF16, "krope_all").t
    krope_r = [P.res("krope%d" % i) for i in range(8)]
    ya_t = ar.take([128, 4, NT], BF16, "ya_all").t
    ya_r = [P.res("ya%d" % i) for i in range(4)]
    sbg_t = ar.take([128, 4, NT], BF16, "sbg_all").t
    sbg_r = [P.res("sbg%d" % i) for i in range(4)]
    q_t = ar.take([128, 8, NT], BF16, "q_all").t
    q_r = [P.res("q%d" % i) for i in range(4)]
    base = ar.mark()
    for b_, d_ in ((scw, io["scw"]), (scb, io["scb"]), (qg, io["qg"]), (kvg, io["kvg"])):
        cx.load(b_.t[:], d_, [b_.r])
    win = ar.take([128, 8, 3008], BF16, "win")
    base1 = ar.mark()
    stg = [ar.take([128, 2048], F32, "stg%d" % i) for i in range(3)]
    cx.load_weight_bf16(lambda k, a, b: win.t[:, k, a:b], io["w_in"], 8, 3008, win.r, stg)
    cx.load_weight_bf16(lambda k, a, b: wuq.t[:, k, a:b], io["w_uq"], 2, 1024, wuq.r, stg)
    cx.load_weight_bf16(lambda k, a, b: wkk.t[:, a:b], io["w_ukvk"], 1, 512, wkk.r, stg)
    cx.load_weight_bf16(lambda k, a, b: wvv.t[:, a:b], io["w_ukvv"], 1, 512, wvv.r, stg)
    for pre in io.get("pre", ()):
        pre()
    cx.ada_phase(ar, io["layer"], io)
    P.barrier()
    ar.reset(base1)

    xcs = [ar.take([128, 512], F32, "xc%d" % i) for i in range(2)]
    xsqs = [ar.take([128, 512], BF16, "xsq%d" % i) for i in range(2)]
    tmps = [ar.take([128, 512], F32, "tmp%d" % i) for i in range(2)]
    tA = [ar.take([128, 512], F32, "tA%d" % i) for i in range(1)]
    tB = [ar.take([128, 512], F32, "tB%d" % i) for i in range(1)]
    acc2 = [ar.take([128, 512], F32, "acc%d" % i) for i in range(1)]
    rstd = ar.take([128, 512], F32, "rstd")
    rstd2 = rstd
    hs = [ar.take([128, 8, 512], BF16, "h%d" % i) for i in range(2)]
    cxb = [ar.take([128, 514], F32, "cxb%d" % i) for i in range(2)]
    cqn = ar.take([128, 2, 512], BF16, "cqn")
    sqb = [ar.take([128, 512], BF16, "sqb%d" % i) for i in range(2)]
    posi = ar.take([32, 512], I32, "posi")
    ang = ar.take([32, 512], F32, "ang")
    kfl = ar.take([32, 512], F32, "kfl")
    msk = ar.take([32, 512], F32, "msk")
    cos2 = ar.take([32, 512], F32, "cos2")
    sin2 = ar.take([32, 512], F32, "sin2")
    rt1 = [ar.take([128, 512], F32, "rt1_%d" % i) for i in range(1)]
    rt2 = [ar.take([128, 512], F32, "rt2_%d" % i) for i in range(1)]

    def rope_tables(pos_d, c0):
        P.dma("sp", posi.t[:], pos_d[0:1, c0:c0 + SL].partition_broadcast(32), writes=[posi.r])
        cx.copy("dve", ang.t[:], posi.t[:], [posi.r], [ang.r])
        cx.ts("dve", ang.t[:], ang.t[:], invf.t[:, 0:1], ALU.mult, [ang.r, invf.r], [ang.r])
        cx.ts("dve", posi.t[:], ang.t[:], 1.0 / TWO_PI, ALU.mult, [ang.r], [posi.r])
        cx.copy("dve", kfl.t[:], posi.t[:], [posi.r], [kfl.r])
        cx.stt(ang.t[:], kfl.t[:], -TWO_PI, ang.t[:], ALU.mult, ALU.add, [kfl.r, ang.r], [ang.r])
        cx.ts("dve", msk.t[:], ang.t[:], PI, ALU.is_gt, [ang.r], [msk.r], s2=-TWO_PI, op1=ALU.mult)
        cx.tt("dve", ang.t[:], ang.t[:], msk.t[:], ALU.add, [ang.r, msk.r], [ang.r])
        cx.ts("dve", msk.t[:], ang.t[:], -PI, ALU.is_lt, [ang.r], [msk.r], s2=TWO_PI, op1=ALU.mult)
        cx.tt("dve", ang.t[:], ang.t[:], msk.t[:], ALU.add, [ang.r, msk.r], [ang.r])
        cx.act(sin2.t[:], ang.t[:], AF.Sin, [ang.r], [sin2.r])
        cx.ts("dve", sin2.t[:], sin2.t[:], sgn.t[:, 0:1], ALU.mult, [sin2.r, sgn.r], [sin2.r])
        cx.ts("dve", kfl.t[:], ang.t[:], PI / 2, ALU.add, [ang.r], [kfl.r])
        cx.ts("dve", msk.t[:], kfl.t[:], PI, ALU.is_gt, [kfl.r], [msk.r], s2=-TWO_PI, op1=ALU.mult)
        cx.tt("dve", kfl.t[:], kfl.t[:], msk.t[:], ALU.add, [kfl.r, msk.r], [kfl.r])
        cx.act(cos2.t[:], kfl.t[:], AF.Sin, [kfl.r], [cos2.r])

    def kv_group(h, kcol0, blk):
        bkv = cx.nb()
        bkx = cx.nb()
        bky = cx.nb()
        for k in range(8):
            cx.mm(bkv.t[:], win.t[:, k, 2304:2432], h.t[:, k, :], k == 0, k == 7, [win.r, h.r], [bkv.r])
        for k in range(8):
            cx.mm(bkx.t[64:96, :], win.t[:, k, 2432:2464], h.t[:, k, :], k == 0, k == 7, [win.r, h.r], [bkx.r])
        for k in range(8):
            cx.mm(bky.t[64:96, :], win.t[:, k, 2976:3008], h.t[:, k, :], k == 0, k == 7, [win.r, h.r], [bky.r])
        sq = sqb[0]
        cx.act(sq.t[:], bkv.t[:], AF.Square, [bkv.r], [sq.r])
        ss = cx.nb()
        cx.mm(ss.t[:], cx.ones_bf.t[:], sq.t[:], True, True, [cx.ones_bf.r, sq.r], [ss.r])
        cx.rstd_from(ss, SL, 128.0, rstd2)
        cx.stt(ckvn_t[:, kcol0:kcol0 + SL], bkv.t[:], kvg.t[:, 0:1], rstd2.t[:], ALU.mult, ALU.mult,
               [bkv.r, kvg.r, rstd2.r], [ckvn_r[blk]])
        a, b = rt1[0], rt2[0]
        cx.tt("dve", a.t[64:96, :], bkx.t[64:96, :], cos2.t[:], ALU.mult, [bkx.r, cos2.r], [a.r])
        cx.tt("dve", b.t[64:96, :], bky.t[64:96, :], sin2.t[:], ALU.mult, [bky.r, sin2.r], [b.r])
        cx.tt("pool", krope_t[64:96, kcol0:kcol0 + SL], a.t[64:96, :], b.t[64:96, :], ALU.add, [a.r, b.r], [krope_r[blk]])

    hh = hs[1]
    cx.prenorm(xH, 32, hh, xcs, xsqs, rstd, tmps, xres=xFres)
    for j in range(4):
        bc = cx.nb()
        bx = cx.nb()
        for k in range(8):
            cx.mm(bc.t[:, 0:32], win.t[:, k, 512 + j * 128:512 + (j + 1) * 128], hh.t[:, k, 0:32], k == 0, k == 7, [win.r, hh.r], [bc.r])
        for k in range(8):
            cx.mm(bx.t[:, 0:32], win.t[:, k, 1024 + j * 128:1024 + (j + 1) * 128], hh.t[:, k, 0:32], k == 0, k == 7, [win.r, hh.r], [bx.r])
        tm = tA[0]
        cx.copy("act", tm.t[:, 0:32], bc.t[:, 0:32], [bc.r], [tm.r])
        cx.tt("dve", tm.t[:, 0:32], bx.t[:, 0:32], tm.t[:, 0:32], ALU.mult, [bx.r, tm.r], [tm.r])
        cx.ts("dve", cxh.t[:, j, 0:2], tm.t[:, 30:32], hf.t[:, 0:1], ALU.mult, [tm.r, hf.r], [cxh.r])

    for s in range(NSL):
        h = hs[s % 2]
        c0 = s * SL
        cx.prenorm(xT[:, :, c0:c0 + SL], SL, h, xcs, xsqs, rstd, tmps)
        rope_tables(posO, c0)
        for j in range(4):
            bb, bc, bx, bg = cx.nb(), cx.nb(), cx.nb(), cx.nb()
            for (bk, off) in ((bb, 0), (bc, 512), (bx, 1024), (bg, 1536)):
                for k in range(8):
                    cx.mm(bk.t[:], win.t[:, k, off + j * 128:off + (j + 1) * 128], h.t[:, k, :], k == 0, k == 7, [win.r, h.r], [bk.r])
            ta = tA[0]
            tb = tB[0]
            cb_ = cxb[j % 2]
            ac = acc2[0]
            cx.copy("act", ta.t[:], bc.t[:], [bc.r], [ta.r])
            cx.copy("pool", cb_.t[:, 0:2], cxh.t[:, j, 0:2], [cxh.r], [cb_.r])
            cx.tt("dve", cb_.t[:, 2:514], bx.t[:], ta.t[:], ALU.mult, [bx.r, ta.r], [cb_.r])
            cx.copy("pool", cxh.t[:, j, 0:2], cb_.t[:, 512:514], [cb_.r], [cxh.r])
            cx.ts("dve", ac.t[:], cb_.t[:, 2:514], scw.t[:, j, 2:3], ALU.mult, [cb_.r, scw.r, scb.r], [ac.r], s2=scb.t[:, j:j + 1], op1=ALU.add)
            cx.stt(ac.t[:], cb_.t[:, 1:513], scw.t[:, j, 1:2], ac.t[:], ALU.mult, ALU.add, [cb_.r, scw.r, ac.r], [ac.r])
            cx.stt(ac.t[:], cb_.t[:, 0:512], scw.t[:, j, 0:1], ac.t[:], ALU.mult, ALU.add, [cb_.r, scw.r, ac.r], [ac.r])
            cx.act(tb.t[:], bg.t[:], AF.Silu, [bg.r], [tb.r])
            cx.tt("dve", ac.t[:], ac.t[:], bb.t[:], ALU.mult, [ac.r, bb.r], [ac.r])
            cx.tt("pool", ya_t[:, j, c0:c0 + SL], ac.t[:], tb.t[:], ALU.mult, [ac.r, tb.r], [ya_r[s]])
        kv_group(h, NT + c0, 4 + s)
        bq = [cx.nb(), cx.nb()]
        for i in range(2):
            for k in range(8):
                cx.mm(bq[i].t[:], win.t[:, k, 2048 + i * 128:2048 + (i + 1) * 128], h.t[:, k, :], k == 0, k == 7, [win.r, h.r], [bq[i].r])
        ss = cx.nb()
        for i in range(2):
            sq = sqb[i]
            cx.act(sq.t[:], bq[i].t[:], AF.Square, [bq[i].r], [sq.r])
            cx.mm(ss.t[:], cx.ones_bf.t[:], sq.t[:], i == 0, i == 1, [cx.ones_bf.r, sq.r], [ss.r], inc=True)
        cx.rstd_from(ss, SL, 256.0, rstd2)
        for i in range(2):
            cx.stt(cqn.t[:, i, :], bq[i].t[:], qg.t[:, i:i + 1], rstd2.t[:], ALU.mult, ALU.mult, [bq[i].r, qg.r, rstd2.r], [cqn.r])
        for hd in range(8):
            b1, b2 = cx.nb(), cx.nb()
            for i in range(2):
                cx.mm(b1.t[0:96, :], wuq.t[:, i, hd * 128:hd * 128 + 96], cqn.t[:, i, :], i == 0, i == 1, [wuq.r, cqn.r], [b1.r])
            for i in range(2):
                cx.mm(b2.t[64:96, :], wuq.t[:, i, hd * 128 + 96:hd * 128 + 128], cqn.t[:, i, :], i == 0, i == 1, [wuq.r, cqn.r], [b2.r])
            a, b = rt1[0], rt2[0]
            cx.tt("dve", a.t[64:96, :], b1.t[64:96, :], cos2.t[:], ALU.mult, [b1.r, cos2.r], [a.r])
            cx.tt("dve", b.t[64:96, :], b2.t[64:96, :], sin2.t[:], ALU.mult, [b2.r, sin2.r], [b.r])
            cx.tt("pool", q_t[64:96, hd, c0:c0 + SL], a.t[64:96, :], b.t[64:96, :], ALU.add, [a.r, b.r], [q_r[s]])
            cx.copy("act", q_t[0:64, hd, c0:c0 + SL], b1.t[0:64, :], [b1.r], [q_r[s]])
        for j in range(4):
            bk = cx.nb()
            for k in range(8):
                cx.mm(bk.t[:], win.t[:, k, 2464 + j * 128:2464 + (j + 1) * 128], h.t[:, k, :], k == 0, k == 7, [win.r, h.r], [bk.r])
            cx.act(sbg_t[:, j, c0:c0 + SL], bk.t[:], AF.Silu, [bk.r], [sbg_r[s]])

    lat, latg = io["lat"], io["latg"]
    Rlat, Rlatg = P.res("lat"), P.res("latg")
    P.dma("sp", lat[0:128, :], ckvn_t[:, NT:2 * NT], reads=ckvn_r[4:8], writes=[Rlat])
    P.dma("sp", lat[128:160, :], krope_t[64:96, NT:2 * NT], reads=krope_r[4:8], writes=[Rlat])
    P.cc("AllGather", lat, latg.rearrange("r p t -> (r p) t"), PAIRS, reads=[Rlat], writes=[Rlatg])
    P.dma("sp", ckvn_t[:, 0:NT], latg[0, 0:128, :], reads=[Rlatg], writes=ckvn_r[0:4], sem_res=ckvn_r[0])
    P.dma("sp", krope_t[64:96, 0:NT], latg[0, 128:160, :], reads=[Rlatg], writes=krope_r[0:4], sem_res=krope_r[0])

    P.barrier()
    ar.reset(base)

    yb = ar.take([128, 4, NT], BF16, "yb_all")
    yb_r = [P.res("yb%d" % i) for i in range(4)]
    mark = ar.mark()
    vp = [ar.take([128, 32, 192], BF16, "vp%d" % i) for i in range(2)]
    kh = [ar.take([128, 4096], BF16, "kh%d" % i) for i in range(2)]
    pts = [ar.take([128, 512], BF16, "pt%d" % i) for i in range(6)]
    rds = [ar.take([128, 512], F32, "rd%d" % i) for i in range(2)]
    tms = [ar.take([128, 512], F32, "tm%d" % i) for i in range(2)]
    for v_ in vp:
        cx.memset("pool", v_.t[:, :, 64:128], 1.0, [v_.r])
    cx.nbanks_rr = 6
    ada_job = None
    if io.get("ada_jobs"):
        awb = [ar.take([128, 3072], F32, "adaw%d" % i) for i in range(2)]
        aacc = ar.take([128, 3072], F32, "adaacc")

        def _jobs():
            for (l2, io2) in io["ada_jobs"]:
                for _ in cx.ada_steps(l2, io2, awb, aacc):
                    yield
        ada_job = _jobs()
    cnt = 0
    pti = 0
    for j in range(4):
        v_ = vp[j % 2]
        for g in range(8):
            bank = cx.nb()
            for i in range(4):
                kb = 4 * g + i
                cx.mm(bank.t[:, i * 128:(i + 1) * 128], ckvn_t[:, kb * 128:(kb + 1) * 128], wvv.t[:, j * 128:(j + 1) * 128], True, True,
                      [ckvn_r[kb // 4], wvv.r], [bank.r], inc=(i == 3))
            bv = bank.t[:, :].rearrange("p (a b) -> p a b", a=4)
            cx.copy("dve", v_.t[:, 4 * g:4 * g + 4, 0:64], bv[:, :, 0:64], [bank.r], [v_.r])
            cx.copy("act", v_.t[:, 4 * g:4 * g + 4, 128:192], bv[:, :, 64:128], [bank.r], [v_.r])
        for hd in (2 * j, 2 * j + 1):
            k_ = kh[hd % 2]
            cx.copy("pool", k_.t[64:96, :], krope_t[64:96, :], krope_r, [k_.r])
            for kc in range(8):
                bank = cx.nb()
                cx.mm(bank.t[0:64, :], wkk.t[:, hd * 64:(hd + 1) * 64], ckvn_t[:, kc * SL:(kc + 1) * SL], True, True,
                      [wkk.r, ckvn_r[kc]], [bank.r])
                cx.copy("dve", k_.t[0:64, kc * SL:(kc + 1) * SL], bank.t[0:64, :], [bank.r], [k_.r])
            vc0 = 0 if hd % 2 == 0 else 64
            for s in range(NSL):
                if ada_job is not None:
                    try:
                        next(ada_job)
                    except StopIteration:
                        ada_job = None
                accb = cx.banks[6 + cnt % 2]
                cnt += 1
                nkb = 16 + 4 * (s + 1)
                LA = 3
                scs = {}

                def emit_sc(kb, s=s, hd=hd, k_=k_, nkb=nkb):
                    jd = kb - (nkb - 4)
                    lo = 128 * jd if jd >= 0 else 0
                    sc = cx.nb()
                    cx.mm(sc.t[:, lo:SL], k_.t[0:96, kb * 128:(kb + 1) * 128], q_t[0:96, hd, s * SL + lo:(s + 1) * SL], True, True,
                          [k_.r, q_r[s]], [sc.r])
                    scs[kb] = (sc, lo, jd)

                for kb in range(min(LA, nkb)):
                    emit_sc(kb)
                for kb in range(nkb):
                    sc, lo, jd = scs.pop(kb)
                    pt = pts[pti % 6]
                    pti += 1
                    cx.act(pt.t[:, lo:SL], sc.t[:, lo:SL], AF.Exp, [sc.r, keyb.r], [pt.r], bias=keyb.t[:, kb:kb + 1], scale=SCALE)
                    if jd >= 0:
                        cx.memset("dve", pt.t[64:128, lo:lo + 64], 0.0, [pt.r])
                    if kb + LA < nkb:
                        emit_sc(kb + LA)
                    cx.mm(accb.t[:, lo:SL], v_.t[:, kb, vc0:vc0 + 128], pt.t[:, lo:SL], kb == 0, kb == nkb - 1,
                          [v_.r, pt.r], [accb.r], inc=True)
                if hd % 2 == 0:
                    npart, dpart = slice(0, 64), slice(64, 128)
                else:
                    npart, dpart = slice(64, 128), slice(0, 64)
                rd = rds[cnt % 2]
                tm = tms[cnt % 2]
                P.op("dve", lambda e, rd=rd, accb=accb, dpart=dpart: e.reciprocal(out=rd.t[dpart, :], in_=accb.t[dpart, :]), [accb.r], [rd.r])
                cx.tt("dve", tm.t[npart, :], accb.t[npart, :], rd.t[dpart, :], ALU.mult, [accb.r, rd.r], [tm.r])
                cx.tt("pool", yb.t[npart, j, s * SL:(s + 1) * SL], tm.t[npart, :], sbg_t[npart, j, s * SL:(s + 1) * SL], ALU.mult,
                      [tm.r, sbg_r[s]], [yb_r[s]])
    cx.nbanks_rr = 8
    if ada_job is not None:
        for _ in ada_job:
            pass

    P.barrier()
    ar.reset(mark)

    wout = ar.take([128, 8, 1024], BF16, "wout")
    ysbs = [ar.take([128, 8, 512], F32, "ysb%d" % i) for i in range(2)]
    ysqs = [ar.take([128, 512], BF16, "ysq%d" % i) for i in range(2)]
    xcs = [ar.take([128, 512], F32, "xc%d" % i) for i in range(6)]
    tmps = [ar.take([128, 512], F32, "tmp%d" % i) for i in range(2)]
    obufs = [ar.take([128, 512], F32, "ob%d" % i) for i in range(3)]
    rstds = [ar.take([128, 512], F32, "rstd%d" % i) for i in range(2)]
    stg = [ar.take([128, 2048], F32, "stg%d" % i) for i in range(2)]
    cx.load_weight_bf16(lambda k, a, b: wout.t[:, k, a:b], io["w_out"], 8, 1024, wout.r, stg)
    for s in range(NSL):
        ysb = ysbs[s % 2]
        rstd = rstds[s % 2]
        c0 = s * SL

        def rhs_fn(k, c0=c0):
            if k < 4:
                return ya_t[:, k, c0:c0 + SL]
            return yb.t[:, k - 4, c0:c0 + SL]

        cx.out_phase(wout, rhs_fn, [ya_r[s], yb_r[s]], xT[:, :, c0:c0 + SL], outT[:, :, c0:c0 + SL],
                     ysb, ysqs, rstd, xcs, tmps, obufs, halo_d=(io.get("halo_out") if s == NSL - 1 else None))


PAIRS = [[0, 1], [2, 3], [4, 5], [6, 7]]
_DBG_NOCC = False
LAYER_KEYS_EVEN = ["w_in", "w_uq", "w_ukvk", "w_ukvv", "w_out", "scw", "scb", "qg", "kvg"]
LAYER_SHAPES_EVEN = {"w_in": [128, 8, 3008], "w_uq": [128, 2, 1024], "w_ukvk": [128, 1, 512], "w_ukvv": [128, 1, 512],
                     "w_out": [128, 8, 1024], "scw": [128, 4, 3], "scb": [128, 4], "qg": [128, 2], "kvg": [128, 1]}
LAYER_SHAPES_ODD = {"w_in": [128, 8, 3072], "w_out": [128, 8, 1024], "convw": [128, 8, 31], "convb": [128, 8],
                    "lng": [128, 8], "lnb": [128, 8]}
LAYER_SHAPES_COMMON = {"adaw": [128, 8, 3072], "adab": [128, 24], "preg": [128, 8], "postg": [128, 8]}


def build_fused():
    nc = bass.Bass("TRN2", target_bir_lowering=False)
    xT = dram_in(nc, "xT", [128, 8, NT])
    xH0 = dram_in(nc, "xH", [128, 8, 32])
    posO = dram_in(nc, "posO", [1, NT], I32)
    cT = dram_in(nc, "cT", [128, 8])
    hflag = dram_in(nc, "hflag", [128, 1])
    invf_d = dram_in(nc, "invf2", [32, 1])
    sgn_d = dram_in(nc, "sgn2", [32, 1])
    kb_d = dram_in(nc, "keybias", [128, 32])
    identd = dram_in(nc, "ident", [128, 128])
    L = []
    for l in range(4):
        d = {}
        shapes = dict(LAYER_SHAPES_COMMON)
        shapes.update(LAYER_SHAPES_EVEN if l % 2 == 0 else LAYER_SHAPES_ODD)
        for k, shp in shapes.items():
            d[k] = dram_in(nc, "L%d_%s" % (l, k), shp)
        L.append(d)
    outT = nc.dram_tensor("outT", [128, 8, NT], F32, kind="ExternalOutput").ap()
    xs_raw = [nc.dram_tensor("xs%d" % l, [8, 128, NT], F32, kind="Internal").ap() for l in range(3)]
    xs = [t_.rearrange("c p t -> p c t") for t_ in xs_raw]
    hal = [nc.dram_tensor("hal%d" % l, [128, 8, 32], F32, kind="Internal").ap() for l in range(3)]
    halg = [nc.dram_tensor("halg%d" % l, [2, 128, 8, 32], F32, kind="Internal").ap() for l in range(3)]
    lats = {l: nc.dram_tensor("lat%d" % l, [160, NT], BF16, kind="Internal").ap() for l in (0, 2)}
    latgs = {l: nc.dram_tensor("latg%d" % l, [2, 160, NT], BF16, kind="Internal").ap() for l in (0, 2)}

    cx = Ctx(nc)
    P = cx.P
    cx.setup_consts(identd, hflag, invf_d, sgn_d, kb_d)
    cx.ada_setup(cT)
    ar = Arena(P, "arena", 49 * 1024 + 512)
    Rhalg = [P.res("halg%d" % l) for l in range(3)]
    Rsrc = P.res("ccsrc")

    for l in range(4):
        io = dict(L[l])
        io["cT"] = cT
        io["layer"] = l
        if l == 0:
            io["ada_jobs"] = [(l2, L[l2]) for l2 in (1, 2, 3)]
        io["xT"] = xT if l == 0 else xs[l - 1]
        io["out"] = outT if l == 3 else xs[l]
        if l < 3:
            io["halo_out"] = hal[l]
        pre = []
        if l == 0 or _DBG_NOCC:
            io["xF"], io["xH"] = None, xH0
        else:
            def gather_halo(l=l):
                P.cc("AllGather", hal[l - 1].rearrange("p c t -> p (c t)"), halg[l - 1].rearrange("r p c t -> (r p) (c t)"), PAIRS, reads=[Rsrc], writes=[Rhalg[l - 1]])
            pre.append(gather_halo)
            io["xH"] = halg[l - 1][0]
            io["xHres"] = [Rhalg[l - 1]]
            io["xFres"] = [Rhalg[l - 1]]
        io["pre"] = pre
        io["posO"], io["posF"] = posO, None
        if l % 2 == 0:
            io["lat"], io["latg"] = lats[l], latgs[l]
            io.setdefault("xF", None)
        if l % 2 == 0:
            emit_even(cx, ar, io)
        else:
            emit_odd(cx, ar, io)
    P.emit()
    return nc


def fm(v):
    v = np.asarray(v)
    return np.ascontiguousarray(v.reshape(-1, 128).T)


def wk(w):
    w = np.asarray(w)
    k, n = w.shape
    return np.ascontiguousarray(w.reshape(k // 128, 128, n).transpose(1, 0, 2))


def xfm(xb):
    t = xb.shape[0]
    return np.ascontiguousarray(xb.T.reshape(8, 128, t).transpose(1, 0, 2))


def xfm_inv(o):
    t = o.shape[2]
    return o.transpose(1, 0, 2).reshape(D, t).T


def _perm_uq(w_uq):
    w = np.asarray(w_uq)
    blocks = []
    for h in range(8):
        b = h * 96
        blocks += [w[:, b:b + 64], w[:, b + 64:b + 96], w[:, b + 80:b + 96], w[:, b + 64:b + 80]]
    return np.concatenate(blocks, axis=1)


_NC_CACHE = {}


def kernel(x, c, positions, ada_w, ada_b, pre_norm_g, post_norm_g,
           even_w_in, even_sc_conv_w, even_sc_conv_b, even_q_norm_g, even_kv_norm_g,
           even_w_uq, even_w_ukv, even_w_out,
           odd_w_in, odd_conv_w, odd_conv_b, odd_ln_g, odd_ln_b, odd_w_out):
    x = np.ascontiguousarray(np.asarray(x, dtype=np.float32))
    c = np.asarray(c, dtype=np.float32)
    positions = np.asarray(positions)
    if "fused" not in _NC_CACHE:
        _NC_CACHE["fused"] = build_fused()
    nc = _NC_CACHE["fused"]
    invf = (np.float32(1.0) / (np.float32(10000.0) ** (np.arange(0, 32, 2, dtype=np.float32) / np.float32(32)))).astype(np.float32)
    shared = {
        "ident": np.eye(128, dtype=np.float32),
        "invf2": np.ascontiguousarray(np.concatenate([invf, invf])[:, None]),
        "sgn2": np.ascontiguousarray(np.concatenate([-np.ones(16, np.float32), np.ones(16, np.float32)])[:, None]),
    }
    for l in range(4):
        i = l // 2
        p = "L%d_" % l
        shared[p + "adaw"] = wk(ada_w[l])
        shared[p + "adab"] = fm(ada_b[l])
        shared[p + "preg"] = fm(pre_norm_g[l])
        shared[p + "postg"] = fm(post_norm_g[l])
        if l % 2 == 0:
            w_in = np.asarray(even_w_in[i])
            w_ukv = np.asarray(even_w_ukv[i])
            shared[p + "w_in"] = wk(np.concatenate([w_in, w_in[:, 2448:2464], w_in[:, 2432:2448]], axis=1))
            shared[p + "w_uq"] = wk(_perm_uq(even_w_uq[i]))
            shared[p + "w_ukvk"] = np.ascontiguousarray(np.concatenate([w_ukv[:, h * 128:h * 128 + 64] for h in range(8)], axis=1)[:, None, :])
            shared[p + "w_ukvv"] = np.ascontiguousarray(np.concatenate([w_ukv[:, h * 128 + 64:h * 128 + 128] for h in range(8)], axis=1)[:, None, :])
            shared[p + "w_out"] = wk(even_w_out[i])
            shared[p + "scw"] = np.ascontiguousarray(np.asarray(even_sc_conv_w[i]).T.reshape(4, 128, 3).transpose(1, 0, 2))
            shared[p + "scb"] = fm(even_sc_conv_b[i])
            shared[p + "qg"] = fm(even_q_norm_g[i])
            shared[p + "kvg"] = fm(even_kv_norm_g[i])
        else:
            shared[p + "w_in"] = wk(odd_w_in[i])
            shared[p + "w_out"] = wk(odd_w_out[i])
            shared[p + "convw"] = np.ascontiguousarray(np.asarray(odd_conv_w[i]).T.reshape(8, 128, 31).transpose(1, 0, 2))
            shared[p + "convb"] = fm(odd_conv_b[i])
            shared[p + "lng"] = fm(odd_ln_g[i])
            shared[p + "lnb"] = fm(odd_ln_b[i])
    maps = []
    for core in range(8):
        b, half = core // 2, core % 2
        t0 = half * NT
        m = dict(shared)
        m["xT"] = xfm(x[b, t0:t0 + NT])
        m["xH"] = xfm(x[b, NT - 32:NT]) if half == 1 else np.zeros((128, 8, 32), np.float32)
        m["cT"] = fm(c[b])
        m["hflag"] = np.full((128, 1), float(half), np.float32)
        m["posO"] = np.ascontiguousarray(positions[b, t0:t0 + NT][None, :]).astype(np.int32)
        kbias = np.zeros((128, 32), np.float32)
        if half == 0:
            kbias[:, 0:16] = -30000.0
        m["keybias"] = kbias
        maps.append(m)
    res = run_bass_kernel_spmd(nc, maps, core_ids=list(range(8)))
    out = np.empty_like(x)
    for core in range(8):
        b, half = core // 2, core % 2
        out[b, half * NT:(half + 1) * NT] = xfm_inv(res.results[core]["outT"])
    return out
```

```python
from contextlib import ExitStack
import numpy as np
import concourse.bass as bass
import concourse.mybir as mybir
from concourse.bass_utils import run_bass_kernel_spmd

F32 = mybir.dt.float32
BF16 = mybir.dt.bfloat16
I32 = mybir.dt.int32
ALU = mybir.AluOpType
AF = mybir.ActivationFunctionType

D = 1024
NT = 2048
SL = 512
NSL = 4
EPS = 1e-6
PI = 3.141592653589793
TWO_PI = 6.283185307179586

ENGS = ("pe", "dve", "act", "pool", "sp")
SAME_ENGINE_SYNC = True


class Sem:
    def __init__(self, handle, name, is_dma):
        self.h = handle
        self.name = name
        self.total = 0
        self.is_dma = is_dma


class Res:
    __slots__ = ("name", "writer", "readers", "dsem")

    def __init__(self, name):
        self.name = name
        self.writer = None
        self.readers = []
        self.dsem = None


class Buf:
    __slots__ = ("t", "r")

    def __init__(self, t, r):
        self.t = t
        self.r = r


class Prog:
    def __init__(self, nc):
        self.nc = nc
        self.stack = ExitStack()
        self.ops = {e: [] for e in ENGS}
        self.esem = {}
        for e in ENGS:
            self.esem[e] = Sem(self.stack.enter_context(nc.semaphore("s_" + e)), e, False)
        self.waited = {e: {} for e in ENGS}
        self.dma_sems = []
        self.free_dsems = []
        self.all_res = []
        self.nm = 0

    def sbuf(self, name, shape, dtype):
        return self.stack.enter_context(self.nc.sbuf_tensor("sb_" + name, list(shape), dtype))

    def psum(self, name, shape, dtype=F32):
        return self.stack.enter_context(self.nc.psum_tensor("ps_" + name, list(shape), dtype))

    def res(self, name="r"):
        r = Res(name)
        self.all_res.append(r)
        return r

    def buf(self, name, shape, dtype):
        return Buf(self.sbuf(name, shape, dtype), self.res(name))

    def _dsem(self, r):
        if r.dsem is None:
            if self.free_dsems:
                r.dsem = self.free_dsems.pop()
            else:
                self.nm += 1
                nm = "d%d" % self.nm
                r.dsem = Sem(self.stack.enter_context(self.nc.semaphore(nm)), nm, True)
                self.dma_sems.append(r.dsem)
        return r.dsem

    def _collect(self, eng, reads, writes):
        need = {}

        def add(tok):
            if tok is None:
                return
            sem, val, teng = tok
            if teng == eng and (eng == "pe" or not SAME_ENGINE_SYNC):
                return
            if sem.is_dma:
                val = max(val, sem.total)
            if need.get(sem, 0) < val:
                need[sem] = val

        for r in reads:
            add(r.writer)
        for w in writes:
            add(w.writer)
            for t in w.readers:
                add(t)
        wl = []
        wd = self.waited[eng]
        for sem, val in need.items():
            if wd.get(sem, 0) < val:
                wd[sem] = val
                wl.append((sem, val))
        return wl

    def _commit(self, tok, reads, writes):
        for r in reads:
            r.readers.append(tok)
            if len(r.readers) > 64:
                best = {}
                for t in r.readers:
                    if best.get(t[0], (None, 0, None))[1] < t[1]:
                        best[t[0]] = t
                r.readers = list(best.values())
        for w in writes:
            w.writer = tok
            w.readers = []

    def op(self, eng, fn, reads=(), writes=(), inc=True):
        waits = self._collect(eng, reads, writes)
        sem = self.esem[eng]
        if inc:
            sem.total += 1
            tok = (sem, sem.total, eng)
        else:
            tok = (sem, sem.total + 1, eng)
        self.ops[eng].append((waits, fn, (sem, 1) if inc else None))
        self._commit(tok, reads, writes)

    def dma(self, eng, out_ap, in_ap, reads=(), writes=(), sem_res=None):
        waits = self._collect(eng, reads, writes)
        if sem_res is None:
            sem_res = writes[0] if writes else reads[0]
        sem = self._dsem(sem_res)
        sem.total += 16
        tok = (sem, sem.total, "dma")

        def fn(e, out_ap=out_ap, in_ap=in_ap):
            return e.dma_start(out=out_ap, in_=in_ap)

        self.ops[eng].append((waits, fn, (sem, 16)))
        self._commit(tok, reads, writes)

    def cc(self, kind, in_ap, out_ap, groups, reads=(), writes=()):
        waits = self._collect("pool", reads, writes)
        sem = self._dsem(writes[0])
        sem.total += 1
        tok = (sem, sem.total, "dma")

        def fn(e):
            return e.collective_compute(kind, ALU.bypass, replica_groups=groups, ins=[in_ap], outs=[out_ap])

        self.ops["pool"].append((waits, fn, (sem, 1)))
        self._commit(tok, reads, writes)

    def barrier(self):
        for e in ENGS:
            waits = []
            wd = self.waited[e]
            for x in ENGS:
                s = self.esem[x]
                if x != e and s.total > wd.get(s, 0):
                    wd[s] = s.total
                    waits.append((s, s.total))
            for s in self.dma_sems:
                if s.total > wd.get(s, 0):
                    wd[s] = s.total
                    waits.append((s, s.total))
            if waits:
                self.ops[e].append((waits, None, None))
        for r in self.all_res:
            r.writer = None
            r.readers = []
            if r.dsem is not None:
                self.free_dsems.append(r.dsem)
                r.dsem = None

    def emit(self):
        nc = self.nc
        self.barrier()
        engmap = {"pe": "tensor", "dve": "vector", "act": "scalar", "pool": "gpsimd", "sp": "sync"}
        with nc.Block() as block:
            for e in ENGS:
                ops = self.ops[e]

                def body(eobj, ops=ops):
                    for waits, fn, inc in ops:
                        for sem, val in waits:
                            eobj.wait_ge(sem.h, val)
                        if fn is not None:
                            ins = fn(eobj)
                            if inc is not None:
                                ins.then_inc(inc[0].h, inc[1])

                getattr(block, engmap[e])(body)
        self.stack.close()


class Arena:
    def __init__(self, P, name, words):
        self.P = P
        self.t = P.sbuf(name, [128, words], F32)
        self.words = words
        self.off = 0

    def reset(self, mark=0):
        self.off = mark

    def mark(self):
        return self.off

    def take(self, shape, dtype, name="a"):
        n = 1
        for s in shape[1:]:
            n *= s
        esz = 4 if dtype in (F32, I32) else 2
        w = (n * esz + 3) // 4
        w = (w + 7) // 8 * 8
        assert self.off + w <= self.words, ("arena overflow", name, self.off, w, self.words)
        v = self.t[:, self.off:self.off + w]
        self.off += w
        if dtype != F32:
            v = v.bitcast(dtype)
        v = v[:, 0:n]
        if len(shape) == 3:
            v = v.rearrange("p (a b) -> p a b", a=shape[1])
        elif len(shape) == 4:
            v = v.rearrange("p (a b c) -> p a b c", a=shape[1], b=shape[2])
        if shape[0] != 128:
            v = v[0:shape[0]]
        return Buf(v, self.P.res(name))


class Ctx:
    def __init__(self, nc):
        self.nc = nc
        self.P = Prog(nc)
        P = self.P
        self.banks = [Buf(P.psum("bank%d" % i, [128, 512]), P.res("bank%d" % i)) for i in range(8)]
        self.bi = 0
        self.nbanks_rr = 8
        self.stg_i = 0

    def nb(self):
        b = self.banks[self.bi % self.nbanks_rr]
        self.bi += 1
        return b

    def tt(self, eng, out, in0, in1, op, reads, writes):
        self.P.op(eng, lambda e: e.tensor_tensor(out=out, in0=in0, in1=in1, op=op), reads, writes)

    def ts(self, eng, out, in0, s1, op0, reads, writes, s2=None, op1=None):
        if op1 is None:
            self.P.op(eng, lambda e: e.tensor_scalar(out=out, in0=in0, scalar1=s1, scalar2=None, op0=op0), reads, writes)
        else:
            self.P.op(eng, lambda e: e.tensor_scalar(out=out, in0=in0, scalar1=s1, scalar2=s2, op0=op0, op1=op1), reads, writes)

    def stt(self, out, in0, scalar, in1, op0, op1, reads, writes):
        self.P.op("dve", lambda e: e.scalar_tensor_tensor(out=out, in0=in0, scalar=scalar, in1=in1, op0=op0, op1=op1), reads, writes)

    def act(self, out, in_, func, reads, writes, bias=None, scale=None):
        kw = {}
        if bias is not None:
            kw["bias"] = bias
        if scale is not None:
            kw["scale"] = scale
        self.P.op("act", lambda e: e.activation(out=out, in_=in_, func=func, **kw), reads, writes)

    def copy(self, eng, out, in_, reads, writes):
        if eng == "act":
            self.P.op("act", lambda e: e.activation(out=out, in_=in_, func=AF.Copy), reads, writes)
        else:
            self.P.op(eng, lambda e: e.tensor_copy(out=out, in_=in_), reads, writes)

    def memset(self, eng, ap, val, writes):
        self.P.op(eng, lambda e: e.memset(ap, val), (), writes)

    def mm(self, out, lhsT, rhs, start, stop, reads, writes, inc=None):
        if inc is None:
            inc = stop
        self.P.op("pe", lambda e: e.matmul(out, lhsT=lhsT, rhs=rhs, start=start, stop=stop), reads, writes, inc=inc)

    def load(self, buf_ap, dram_ap, writes, eng="sp"):
        self.P.dma(eng, buf_ap, dram_ap, writes=writes)

    def setup_consts(self, ident_dram, hflag_d, invf_d, sgn_d, kb_d):
        P = self.P
        self.ones_bf = P.buf("ones_bf", [128, 128], BF16)
        self.memset("pool", self.ones_bf.t[:], 1.0, [self.ones_bf.r])
        self.ones_f = P.buf("ones_f", [128, 2], F32)
        self.memset("pool", self.ones_f.t[:], 1.0, [self.ones_f.r])
        self.epsb = P.buf("epsb", [128, 1], F32)
        self.memset("pool", self.epsb.t[:], EPS, [self.epsb.r])
        self.ident = P.buf("ident_bf", [128, 128], BF16)
        P.dma("pool", self.ident.t[:], ident_dram, writes=[self.ident.r])
        self.hf = P.buf("hf", [128, 1], F32)
        self.invf = P.buf("invf", [32, 1], F32)
        self.sgn = P.buf("sgn", [32, 1], F32)
        self.keyb = P.buf("keyb", [128, 32], F32)
        for b_, d_ in ((self.hf, hflag_d), (self.invf, invf_d), (self.sgn, sgn_d), (self.keyb, kb_d)):
            self.load(b_.t[:], d_, [b_.r])

    def ada_setup(self, cT_d):
        P = self.P
        self.cT = P.buf("cT", [128, 8], F32)
        self.cact = P.buf("cact", [128, 8], F32)
        self.load(self.cT.t[:], cT_d, [self.cT.r])
        self.act(self.cact.t[:], self.cT.t[:], AF.Silu, [self.cT.r], [self.cact.r])
        self.ada_sets = []
        for l in range(4):
            d = {k: P.buf("%s_%d" % (k, l), [128, n], F32) for k, n in
                 (("adab", 24), ("preg", 8), ("postg", 8), ("modT", 24), ("A", 8), ("G2", 8))}
            d["done"] = False
            self.ada_sets.append(d)

    def ada_select(self, l):
        d = self.ada_sets[l]
        self.adab, self.preg, self.postg = d["adab"], d["preg"], d["postg"]
        self.modT, self.A, self.G2 = d["modT"], d["A"], d["G2"]
        return d

    def ada_steps(self, l, io, wb, acc):
        d = self.ada_sets[l]
        adab, preg, postg, modT, A, G2 = d["adab"], d["preg"], d["postg"], d["modT"], d["A"], d["G2"]
        cact = self.cact
        self.load(adab.t[:], io["adab"], [adab.r])
        self.load(preg.t[:], io["preg"], [preg.r])
        self.load(postg.t[:], io["postg"], [postg.r])
        nb_ = len(wb)

        def dve_piece(kc):
            b = wb[kc % nb_]
            if kc == 0:
                self.ts("dve", acc.t[:], b.t[:], cact.t[:, 0:1], ALU.mult, [b.r, cact.r], [acc.r])
            else:
                self.stt(acc.t[:], b.t[:], cact.t[:, kc:kc + 1], acc.t[:], ALU.mult, ALU.add, [b.r, cact.r, acc.r], [acc.r])

        for g in range(0, 8, nb_):
            for kc in range(g, min(8, g + nb_)):
                b = wb[kc % nb_]
                self.load(b.t[:], io["adaw"][:, kc, :], [b.r])
            yield
            for kc in range(g, min(8, g + nb_)):
                dve_piece(kc)
        yield
        bank = self.nb()
        for m in range(24):
            self.mm(bank.t[:, 2 * m:2 * m + 2], acc.t[:, m * 128:(m + 1) * 128], self.ones_f.t[:, 0:2], True, True,
                    [acc.r, self.ones_f.r], [bank.r], inc=True)
        pv = bank.t[:, 0:48].rearrange("p (m t) -> p m t", t=2)[:, :, 0]
        self.tt("dve", modT.t[:], pv, adab.t[:], ALU.add, [bank.r, adab.r], [modT.r])
        self.stt(A.t[:], modT.t[:, 8:16], 1.0, preg.t[:], ALU.add, ALU.mult, [modT.r, preg.r], [A.r])
        self.tt("dve", G2.t[:], modT.t[:, 16:24], postg.t[:], ALU.mult, [modT.r, postg.r], [G2.r])
        d["done"] = True
        yield

    def ada_phase(self, ar, l, io):
        if self.ada_sets[l]["done"]:
            return False
        wb = [ar.take([128, 3072], F32, "adaw%d" % i) for i in range(2)]
        acc = ar.take([128, 3072], F32, "adaacc")
        for _ in self.ada_steps(l, io, wb, acc):
            pass
        return True

    def rstd_from(self, ssbank, n, dim, out):
        self.act(out.t[:, 0:n], ssbank.t[:, 0:n], AF.Ln, [ssbank.r, self.epsb.r], [out.r], bias=self.epsb.t[:], scale=1.0 / dim)
        self.act(out.t[:, 0:n], out.t[:, 0:n], AF.Exp, [out.r], [out.r], scale=-0.5)

    def prenorm(self, x_d, n, h, xcs, xsqs, rstd, tmps, xres=()):
        ss = self.nb()
        for c in range(8):
            xc = xcs[c % len(xcs)]
            self.P.dma("sp", xc.t[:, 0:n], x_d[:, c, :], reads=list(xres), writes=[xc.r], sem_res=xc.r)
            sq = xsqs[c % len(xsqs)]
            self.act(sq.t[:, 0:n], xc.t[:, 0:n], AF.Square, [xc.r], [sq.r])
            self.mm(ss.t[:, 0:n], self.ones_bf.t[:], sq.t[:, 0:n], c == 0, c == 7, [self.ones_bf.r, sq.r], [ss.r], inc=True)
        self.rstd_from(ss, n, float(D), rstd)
        for c in range(8):
            xc = xcs[c % len(xcs)]
            self.P.dma("sp", xc.t[:, 0:n], x_d[:, c, :], reads=list(xres), writes=[xc.r], sem_res=xc.r)
            tm = tmps[c % len(tmps)]
            self.stt(tm.t[:, 0:n], xc.t[:, 0:n], self.A.t[:, c:c + 1], rstd.t[:, 0:n], ALU.mult, ALU.mult,
                     [xc.r, self.A.r, rstd.r], [tm.r])
            self.act(h.t[:, c, 0:n], tm.t[:, 0:n], AF.Identity, [tm.r, self.modT.r], [h.r], bias=self.modT.t[:, c:c + 1])

    def out_mm(self, wout, rhs_fn, rhs_res, ysb, ysqs, rstd):
        ss = self.banks[7]
        self.nbanks_rr = 7
        for m in range(8):
            bank = self.nb()
            for k in range(8):
                self.mm(bank.t[:], wout.t[:, k, m * 128:(m + 1) * 128], rhs_fn(k), k == 0, k == 7,
                        [wout.r] + rhs_res, [bank.r])
            self.copy("act", ysb.t[:, m, :], bank.t[:], [bank.r], [ysb.r])
            sq = ysqs[m % len(ysqs)]
            self.act(sq.t[:], bank.t[:], AF.Square, [bank.r], [sq.r])
            self.mm(ss.t[:], self.ones_bf.t[:], sq.t[:], m == 0, m == 7, [self.ones_bf.r, sq.r], [ss.r], inc=True)
        self.rstd_from(ss, SL, float(D), rstd)
        self.nbanks_rr = 8

    def out_res(self, x_d, out_d, ysb, rstd, xcs, tmps, obufs, xres=(), halo_d=None):
        for c in range(8):
            xc = xcs[c % len(xcs)]
            self.P.dma("sp", xc.t[:], x_d[:, c, :], reads=list(xres), writes=[xc.r], sem_res=xc.r)
            tm = tmps[c % len(tmps)]
            self.tt("dve", tm.t[:], ysb.t[:, c, :], rstd.t[:], ALU.mult, [ysb.r, rstd.r], [tm.r])
            ob = obufs[c % len(obufs)]
            self.stt(ob.t[:], tm.t[:], self.G2.t[:, c:c + 1], xc.t[:], ALU.mult, ALU.add, [tm.r, self.G2.r, xc.r], [ob.r])
            self.P.dma("pool", out_d[:, c, :], ob.t[:], reads=[ob.r], sem_res=ob.r)
            if halo_d is not None:
                self.P.dma("pool", halo_d[:, c, :], ob.t[:, SL - 32:SL], reads=[ob.r], sem_res=ob.r)

    def out_phase(self, wout, rhs_fn, rhs_res, x_d, out_d, ysb, ysqs, rstd, xcs, tmps, obufs, xres=(), ores=(), halo_d=None):
        self.out_mm(wout, rhs_fn, rhs_res, ysb, ysqs, rstd)
        self.out_res(x_d, out_d, ysb, rstd, xcs, tmps, obufs, xres=xres, halo_d=halo_d)

    def load_weight_bf16(self, wbuf_ap_fn, dram, nk, ncols, res, stg):
        for k in range(nk):
            c0 = 0
            while c0 < ncols:
                c1 = min(ncols, c0 + 2048)
                i = self.stg_i
                self.stg_i += 1
                sb = stg[i % len(stg)]
                self.P.dma("sp", sb.t[:, 0:c1 - c0], dram[:, k, c0:c1], writes=[sb.r])
                self.copy("dve" if i % 2 == 0 else "act", wbuf_ap_fn(k, c0, c1), sb.t[:, 0:c1 - c0], [sb.r], [res])
                c0 = c1


def dram_in(nc, name, shape, dtype=F32):
    return nc.dram_tensor(name, list(shape), dtype, kind="ExternalInput").ap()


def emit_odd(cx, ar, io):
    P = cx.P
    xT, xH, outT = io["xT"], io["xH"], io["out"]
    xHres = io.get("xHres", ())
    P.barrier()
    ar.reset()
    cx.ada_select(io["layer"])
    win = ar.take([128, 8, 3072], BF16, "win")
    wout = ar.take([128, 8, 1024], BF16, "wout")
    cw = ar.take([128, 8, 31], F32, "cw")
    cb = ar.take([128, 8], F32, "cb")
    lg = ar.take([128, 8], F32, "lg")
    lb = ar.take([128, 8], F32, "lb")
    base = ar.mark()
    stg = [ar.take([128, 2048], F32, "stg%d" % i) for i in range(4)]
    cx.load_weight_bf16(lambda k, a, b: win.t[:, k, a:b], io["w_in"], 8, 3072, win.r, stg)
    cx.load_weight_bf16(lambda k, a, b: wout.t[:, k, a:b], io["w_out"], 8, 1024, wout.r, stg)
    for pre in io.get("pre", ()):
        pre()
    for b_, d_ in ((cw, io["convw"]), (cb, io["convb"]), (lg, io["lng"]), (lb, io["lnb"])):
        cx.load(b_.t[:], d_, [b_.r])
    hf = cx.hf
    cx.ada_phase(ar, io["layer"], io)
    P.barrier()
    ar.reset(base)

    xcs = [ar.take([128, 512], F32, "xc%d" % i) for i in range(3)]
    xsqs = [ar.take([128, 512], BF16, "xsq%d" % i) for i in range(2)]
    tmps = [ar.take([128, 512], F32, "tmp%d" % i) for i in range(3)]
    rstd = ar.take([128, 512], F32, "rstd")
    hs = [ar.take([128, 8, 512], BF16, "h%d" % i) for i in range(2)]
    u = [ar.take([128, 544], BF16, "u%d" % j) for j in range(8)]
    convs = [ar.take([128, 8, 512], F32, "conv%d" % i) for i in range(2)]
    sgates = [ar.take([128, 8, 512], BF16, "sgate%d" % i) for i in range(2)]
    gated = ar.take([128, 8, 512], BF16, "gated")
    diag = [ar.take([128, 31, 128], BF16, "diag%d" % i) for i in range(2)]
    cbf = [ar.take([128, 512], BF16, "cbf%d" % i) for i in range(2)]
    csq = [ar.take([128, 512], BF16, "csq%d" % i) for i in range(2)]
    means = [ar.take([128, 512], F32, "mean%d" % i) for i in range(2)]
    vars_ = [ar.take([128, 512], F32, "var%d" % i) for i in range(2)]
    obufs = [ar.take([128, 512], F32, "ob%d" % i) for i in range(2)]
    rstd_o = ar.take([128, 512], F32, "rstd_o")
    ysqs_o = [ar.take([128, 512], BF16, "ysqo%d" % i) for i in range(2)]

    hh = hs[1]
    cx.prenorm(xH, 32, hh, xcs, xsqs, rstd, tmps, xres=xHres)
    for j in range(8):
        bv = cx.nb()
        bg = cx.nb()
        for k in range(8):
            cx.mm(bv.t[:, 0:32], win.t[:, k, j * 128:(j + 1) * 128], hh.t[:, k, 0:32], k == 0, k == 7, [win.r, hh.r], [bv.r])
        for k in range(8):
            cx.mm(bg.t[:, 0:32], win.t[:, k, 1024 + j * 128:1024 + (j + 1) * 128], hh.t[:, k, 0:32], k == 0, k == 7, [win.r, hh.r], [bg.r])
        tm = tmps[j % 3]
        cx.act(tm.t[:, 0:32], bg.t[:, 0:32], AF.Sigmoid, [bg.r], [tm.r])
        cx.tt("dve", tm.t[:, 0:32], bv.t[:, 0:32], tm.t[:, 0:32], ALU.mult, [bv.r, tm.r], [tm.r])
        cx.ts("dve", u[j].t[:, 512:542], tm.t[:, 2:32], hf.t[:, 0:1], ALU.mult, [tm.r, hf.r], [u[j].r])

    def A_pre(s):
        cx.prenorm(xT[:, :, s * SL:(s + 1) * SL], SL, hs[s % 2], xcs, xsqs, rstd, tmps)

    def A_body(s, between=None):
        conv, sgate, mean, var = convs[s % 2], sgates[s % 2], means[s % 2], vars_[s % 2]
        h = hs[s % 2]
        mbank = cx.banks[6]
        ebank = cx.banks[7]
        cx.nbanks_rr = 6
        wb = {}

        def emit_win(j):
            b0 = 3 * (j % 2)
            bv, bg, bs = cx.banks[b0], cx.banks[b0 + 1], cx.banks[b0 + 2]
            for (bk, off) in ((bv, 0), (bg, 1024), (bs, 2048)):
                for k in range(8):
                    cx.mm(bk.t[:], win.t[:, k, off + j * 128:off + (j + 1) * 128], h.t[:, k, :], k == 0, k == 7, [win.r, h.r], [bk.r])
            dg = diag[j % 2]
            cx.tt("dve", dg.t[:], cx.ident.t[:].unsqueeze(1).to_broadcast([128, 31, 128]),
                  cw.t[:, j, :].unsqueeze(2).to_broadcast([128, 31, 128]), ALU.mult, [cx.ident.r, cw.r], [dg.r])
            wb[j] = (bv, bg, bs, dg)

        emit_win(0)
        for j in range(8):
            if j + 1 < 8:
                emit_win(j + 1)
            bv, bg, bs, dg = wb.pop(j)
            cx.copy("pool", u[j].t[:, 0:30], u[j].t[:, 512:542], [u[j].r], [u[j].r])
            tm = tmps[j % 3]
            cx.act(tm.t[:], bg.t[:], AF.Sigmoid, [bg.r], [tm.r])
            cx.tt("dve", u[j].t[:, 30:542], bv.t[:], tm.t[:], ALU.mult, [bv.r, tm.r], [u[j].r])
            cx.act(sgate.t[:, j, :], bs.t[:], AF.Silu, [bs.r], [sgate.r])
            cb_ = bg
            for k in range(31):
                cx.mm(cb_.t[:], dg.t[:, k, :], u[j].t[:, k:k + 512], k == 0, k == 30, [dg.r, u[j].r], [cb_.r])
            cx.act(conv.t[:, j, :], cb_.t[:], AF.Identity, [cb_.r, cb.r], [conv.r], bias=cb.t[:, j:j + 1])
            q_ = csq[j % 2]
            cx.act(q_.t[:], cb_.t[:], AF.Square, [cb_.r, cb.r], [q_.r], bias=cb.t[:, j:j + 1])
            c_ = cbf[j % 2]
            cx.copy("dve", c_.t[:], conv.t[:, j, :], [conv.r], [c_.r])
            cx.mm(mbank.t[:], cx.ones_bf.t[:], c_.t[:], j == 0, j == 7, [cx.ones_bf.r, c_.r], [mbank.r], inc=True)
            cx.mm(ebank.t[:], cx.ones_bf.t[:], q_.t[:], j == 0, j == 7, [cx.ones_bf.r, q_.r], [ebank.r], inc=True)
            if between is not None:
                between(j)
        cx.nbanks_rr = 8
        cx.ts("dve", mean.t[:], mbank.t[:], 1.0 / D, ALU.mult, [mbank.r], [mean.r])
        cx.tt("dve", var.t[:], mean.t[:], mean.t[:], ALU.mult, [mean.r], [var.r])
        cx.stt(var.t[:], ebank.t[:], 1.0 / D, var.t[:], ALU.mult, ALU.subtract, [ebank.r, var.r], [var.r])
        cx.act(var.t[:], var.t[:], AF.Ln, [var.r, cx.epsb.r], [var.r], bias=cx.epsb.t[:])
        cx.act(var.t[:], var.t[:], AF.Exp, [var.r], [var.r], scale=-0.5)
    def B_norm_j(s, j):
        conv, sgate, mean, var = convs[s % 2], sgates[s % 2], means[s % 2], vars_[s % 2]
        tm = tmps[j % 3]
        cx.tt("dve", tm.t[:], conv.t[:, j, :], mean.t[:], ALU.subtract, [conv.r, mean.r], [tm.r])
        cx.tt("dve", tm.t[:], tm.t[:], var.t[:], ALU.mult, [tm.r, var.r], [tm.r])
        cx.act(tm.t[:], tm.t[:], AF.Silu, [tm.r, lg.r, lb.r], [tm.r], bias=lb.t[:, j:j + 1], scale=lg.t[:, j:j + 1])
        cx.tt("pool", gated.t[:, j, :], tm.t[:], sgate.t[:, j, :], ALU.mult, [tm.r, sgate.r], [gated.r])

    def B_mm(s):
        cx.out_mm(wout, lambda k: gated.t[:, k, :], [gated.r], convs[s % 2], ysqs_o, rstd_o)

    def B_res(s):
        cx.out_res(xT[:, :, s * SL:(s + 1) * SL], outT[:, :, s * SL:(s + 1) * SL], convs[s % 2], rstd_o, xcs, tmps, obufs,
                   halo_d=(io.get("halo_out") if s == NSL - 1 else None))

    A_pre(0)
    A_body(0)
    A_pre(1)
    for s in range(1, NSL):
        A_body(s, between=lambda j, s=s: B_norm_j(s - 1, j))
        B_mm(s - 1)
        if s + 1 < NSL:
            A_pre(s + 1)
        B_res(s - 1)
    for j in range(8):
        B_norm_j(NSL - 1, j)
    B_mm(NSL - 1)
    B_res(NSL - 1)


def emit_even(cx, ar, io):
    P = cx.P
    xT, xF, xH, outT = io["xT"], io["xF"], io["xH"], io["out"]
    xFres = io.get("xFres", ())
    posO, posF = io["posO"], io["posF"]
    SCALE = 1.0 / float(np.sqrt(96.0))
    hf, invf, sgn, keyb = cx.hf, cx.invf, cx.sgn, cx.keyb
    P.barrier()
    ar.reset()
    cx.ada_select(io["layer"])
    wuq = ar.take([128, 2, 1024], BF16, "wuq")
    wkk = ar.take([128, 512], BF16, "wkk")
    wvv = ar.take([128, 512], BF16, "wvv")
    scw = ar.take([128, 4, 3], F32, "scw")
    scb = ar.take([128, 4], F32, "scb")
    qg = ar.take([128, 2], F32, "qg")
    kvg = ar.take([128, 1], F32, "kvg")
    cxh = ar.take([128, 4, 2], F32, "cxh")
    ckvn_t = ar.take([128, 4096], BF16, "ckvn_all").t
    ckvn_r = [P.res("ckvn%d" % i) for i in range(8)]
    krope_t = ar.take([128, 4096], BF16, "krope_all").t
    krope_r = [P.res("krope%d" % i) for i in range(8)]
    ya_t = ar.take([128, 4, NT], BF16, "ya_all").t
    ya_r = [P.res("ya%d" % i) for i in range(4)]
    sbg_t = ar.take([128, 4, NT], BF16, "sbg_all").t
    sbg_r = [P.res("sbg%d" % i) for i in range(4)]
    q_t = ar.take([128, 8, NT], BF16, "q_all").t
    q_r = [P.res("q%d" % i) for i in range(4)]
    base = ar.mark()
    for b_, d_ in ((scw, io["scw"]), (scb, io["scb"]), (qg, io["qg"]), (kvg, io["kvg"])):
        cx.load(b_.t[:], d_, [b_.r])
    win = ar.take([128, 8, 3008], BF16, "win")
    base1 = ar.mark()
    stg = [ar.take([128, 2048], F32, "stg%d" % i) for i in range(3)]
    cx.load_weight_bf16(lambda k, a, b: win.t[:, k, a:b], io["w_in"], 8, 3008, win.r, stg)
    cx.load_weight_bf16(lambda k, a, b: wuq.t[:, k, a:b], io["w_uq"], 2, 1024, wuq.r, stg)
    cx.load_weight_bf16(lambda k, a, b: wkk.t[:, a:b], io["w_ukvk"], 1, 512, wkk.r, stg)
    cx.load_weight_bf16(lambda k, a, b: wvv.t[:, a:b], io["w_ukvv"], 1, 512, wvv.r, stg)
    for pre in io.get("pre", ()):
        pre()
    cx.ada_phase(ar, io["layer"], io)
    P.barrier()
    ar.reset(base1)

    xcs = [ar.take([128, 512], F32, "xc%d" % i) for i in range(2)]
    xsqs = [ar.take([128, 512], BF16, "xsq%d" % i) for i in range(2)]
    tmps = [ar.take([128, 512], F32, "tmp%d" % i) for i in range(2)]
    tA = [ar.take([128, 512], F32, "tA%d" % i) for i in range(1)]
    tB = [ar.take([128, 512], F32, "tB%d" % i) for i in range(1)]
    acc2 = [ar.take([128, 512], F32, "acc%d" % i) for i in range(1)]
    rstd = ar.take([128, 512], F32, "rstd")
    rstd2 = rstd
    hs = [ar.take([128, 8, 512], BF16, "h%d" % i) for i in range(2)]
    cxb = [ar.take([128, 514], F32, "cxb%d" % i) for i in range(2)]
    cqn = ar.take([128, 2, 512], BF16, "cqn")
    sqb = [ar.take([128, 512], BF16, "sqb%d" % i) for i in range(2)]
    posi = ar.take([32, 512], I32, "posi")
    ang = ar.take([32, 512], F32, "ang")
    kfl = ar.take([32, 512], F32, "kfl")
    msk = ar.take([32, 512], F32, "msk")
    cos2 = ar.take([32, 512], F32, "cos2")
    sin2 = ar.take([32, 512], F32, "sin2")
    rt1 = [ar.take([128, 512], F32, "rt1_%d" % i) for i in range(1)]
    rt2 = [ar.take([128, 512], F32, "rt2_%d" % i) for i in range(1)]

    def rope_tables(pos_d, c0):
        P.dma("sp", posi.t[:], pos_d[0:1, c0:c0 + SL].partition_broadcast(32), writes=[posi.r])
        cx.copy("dve", ang.t[:], posi.t[:], [posi.r], [ang.r])
        cx.ts("dve", ang.t[:], ang.t[:], invf.t[:, 0:1], ALU.mult, [ang.r, invf.r], [ang.r])
        cx.ts("dve", posi.t[:], ang.t[:], 1.0 / TWO_PI, ALU.mult, [ang.r], [posi.r])
        cx.copy("dve", kfl.t[:], posi.t[:], [posi.r], [kfl.r])
        cx.stt(ang.t[:], kfl.t[:], -TWO_PI, ang.t[:], ALU.mult, ALU.add, [kfl.r, ang.r], [ang.r])
        cx.ts("dve", msk.t[:], ang.t[:], PI, ALU.is_gt, [ang.r], [msk.r], s2=-TWO_PI, op1=ALU.mult)
        cx.tt("dve", ang.t[:], ang.t[:], msk.t[:], ALU.add, [ang.r, msk.r], [ang.r])
        cx.ts("dve", msk.t[:], ang.t[:], -PI, ALU.is_lt, [ang.r], [msk.r], s2=TWO_PI, op1=ALU.mult)
        cx.tt("dve", ang.t[:], ang.t[:], msk.t[:], ALU.add, [ang.r, msk.r], [ang.r])
        cx.act(sin2.t[:], ang.t[:], AF.Sin, [ang.r], [sin2.r])
        cx.ts("dve", sin2.t[:], sin2.t[:], sgn.t[:, 0:1], ALU.mult, [sin2.r, sgn.r], [sin2.r])
        cx.ts("dve", kfl.t[:], ang.t[:], PI / 2, ALU.add, [ang.r], [kfl.r])
        cx.ts("dve", msk.t[:], kfl.t[:], PI, ALU.is_gt, [kfl.r], [msk.r], s2=-TWO_PI, op1=ALU.mult)
        cx.tt("dve", kfl.t[:], kfl.t[:], msk.t[:], ALU.add, [kfl.r, msk.r], [kfl.r])
        cx.act(cos2.t[:], kfl.t[:], AF.Sin, [kfl.r], [cos2.r])

    def kv_group(h, kcol0, blk):
        bkv = cx.nb()
        bkx = cx.nb()
        bky = cx.nb()
        for k in range(8):
            cx.mm(bkv.t[:], win.t[:, k, 2304:2432], h.t[:, k, :], k == 0, k == 7, [win.r, h.r], [bkv.r])
        for k in range(8):
            cx.mm(bkx.t[64:96, :], win.t[:, k, 2432:2464], h.t[:, k, :], k == 0, k == 7, [win.r, h.r], [bkx.r])
        for k in range(8):
            cx.mm(bky.t[64:96, :], win.t[:, k, 2976:3008], h.t[:, k, :], k == 0, k == 7, [win.r, h.r], [bky.r])
        sq = sqb[0]
        cx.act(sq.t[:], bkv.t[:], AF.Square, [bkv.r], [sq.r])
        ss = cx.nb()
        cx.mm(ss.t[:], cx.ones_bf.t[:], sq.t[:], True, True, [cx.ones_bf.r, sq.r], [ss.r])
        cx.rstd_from(ss, SL, 128.0, rstd2)
        cx.stt(ckvn_t[:, kcol0:kcol0 + SL], bkv.t[:], kvg.t[:, 0:1], rstd2.t[:], ALU.mult, ALU.mult,
               [bkv.r, kvg.r, rstd2.r], [ckvn_r[blk]])
        a, b = rt1[0], rt2[0]
        cx.tt("dve", a.t[64:96, :], bkx.t[64:96, :], cos2.t[:], ALU.mult, [bkx.r, cos2.r], [a.r])
        cx.tt("dve", b.t[64:96, :], bky.t[64:96, :], sin2.t[:], ALU.mult, [bky.r, sin2.r], [b.r])
        cx.tt("pool", krope_t[64:96, kcol0:kcol0 + SL], a.t[64:96, :], b.t[64:96, :], ALU.add, [a.r, b.r], [krope_r[blk]])

    hh = hs[1]
    cx.prenorm(xH, 32, hh, xcs, xsqs, rstd, tmps, xres=xFres)
    for j in range(4):
        bc = cx.nb()
        bx = cx.nb()
        for k in range(8):
            cx.mm(bc.t[:, 0:32], win.t[:, k, 512 + j * 128:512 + (j + 1) * 128], hh.t[:, k, 0:32], k == 0, k == 7, [win.r, hh.r], [bc.r])
        for k in range(8):
            cx.mm(bx.t[:, 0:32], win.t[:, k, 1024 + j * 128:1024 + (j + 1) * 128], hh.t[:, k, 0:32], k == 0, k == 7, [win.r, hh.r], [bx.r])
        tm = tA[0]
        cx.copy("act", tm.t[:, 0:32], bc.t[:, 0:32], [bc.r], [tm.r])
        cx.tt("dve", tm.t[:, 0:32], bx.t[:, 0:32], tm.t[:, 0:32], ALU.mult, [bx.r, tm.r], [tm.r])
        cx.ts("dve", cxh.t[:, j, 0:2], tm.t[:, 30:32], hf.t[:, 0:1], ALU.mult, [tm.r, hf.r], [cxh.r])

    for s in range(NSL):
        h = hs[s % 2]
        c0 = s * SL
        cx.prenorm(xT[:, :, c0:c0 + SL], SL, h, xcs, xsqs, rstd, tmps)
        rope_tables(posO, c0)
        for j in range(4):
            bb, bc, bx, bg = cx.nb(), cx.nb(), cx.nb(), cx.nb()
            for (bk, off) in ((bb, 0), (bc, 512), (bx, 1024), (bg, 1536)):
                for k in range(8):
                    cx.mm(bk.t[:], win.t[:, k, off + j * 128:off + (j + 1) * 128], h.t[:, k, :], k == 0, k == 7, [win.r, h.r], [bk.r])
            ta = tA[0]
            tb = tB[0]
            cb_ = cxb[j % 2]
            ac = acc2[0]
            cx.copy("act", ta.t[:], bc.t[:], [bc.r], [ta.r])
            cx.copy("pool", cb_.t[:, 0:2], cxh.t[:, j, 0:2], [cxh.r], [cb_.r])
            cx.tt("dve", cb_.t[:, 2:514], bx.t[:], ta.t[:], ALU.mult, [bx.r, ta.r], [cb_.r])
            cx.copy("pool", cxh.t[:, j, 0:2], cb_.t[:, 512:514], [cb_.r], [cxh.r])
            cx.ts("dve", ac.t[:], cb_.t[:, 2:514], scw.t[:, j, 2:3], ALU.mult, [cb_.r, scw.r, scb.r], [ac.r], s2=scb.t[:, j:j + 1], op1=ALU.add)
            cx.stt(ac.t[:], cb_.t[:, 1:513], scw.t[:, j, 1:2], ac.t[:], ALU.mult, ALU.add, [cb_.r, scw.r, ac.r], [ac.r])
            cx.stt(ac.t[:], cb_.t[:, 0:512], scw.t[:, j, 0:1], ac.t[:], ALU.mult, ALU.add, [cb_.r, scw.r, ac.r], [ac.r])
            cx.act(tb.t[:], bg.t[:], AF.Silu, [bg.r], [tb.r])
            cx.tt("dve", ac.t[:], ac.t[:], bb.t[:], ALU.mult, [ac.r, bb.r], [ac.r])
            cx.tt("pool", ya_t[:, j, c0:c0 + SL], ac.t[:], tb.t[:], ALU.mult, [ac.r, tb.r], [ya_r[s]])
        kv_group(h, NT + c0, 4 + s)
        bq = [cx.nb(), cx.nb()]
        for i in range(2):
            for k in range(8):
                cx.mm(bq[i].t[:], win.t[:, k, 2048 + i * 128:2048 + (i + 1) * 128], h.t[:, k, :], k == 0, k == 7, [win.r, h.r], [bq[i].r])
        ss = cx.nb()
        for i in range(2):
            sq = sqb[i]
            cx.act(sq.t[:], bq[i].t[:], AF.Square, [bq[i].r], [sq.r])
            cx.mm(ss.t[:], cx.ones_bf.t[:], sq.t[:], i == 0, i == 1, [cx.ones_bf.r, sq.r], [ss.r], inc=True)
        cx.rstd_from(ss, SL, 256.0, rstd2)
        for i in range(2):
            cx.stt(cqn.t[:, i, :], bq[i].t[:], qg.t[:, i:i + 1], rstd2.t[:], ALU.mult, ALU.mult, [bq[i].r, qg.r, rstd2.r], [cqn.r])
        for hd in range(8):
            b1, b2 = cx.nb(), cx.nb()
            for i in range(2):
                cx.mm(b1.t[0:96, :], wuq.t[:, i, hd * 128:hd * 128 + 96], cqn.t[:, i, :], i == 0, i == 1, [wuq.r, cqn.r], [b1.r])
            for i in range(2):
                cx.mm(b2.t[64:96, :], wuq.t[:, i, hd * 128 + 96:hd * 128 + 128], cqn.t[:, i, :], i == 0, i == 1, [wuq.r, cqn.r], [b2.r])
            a, b = rt1[0], rt2[0]
            cx.tt("dve", a.t[64:96, :], b1.t[64:96, :], cos2.t[:], ALU.mult, [b1.r, cos2.r], [a.r])
            cx.tt("dve", b.t[64:96, :], b2.t[64:96, :], sin2.t[:], ALU.mult, [b2.r, sin2.r], [b.r])
            cx.tt("pool", q_t[64:96, hd, c0:c0 + SL], a.t[64:96, :], b.t[64:96, :], ALU.add, [a.r, b.r], [q_r[s]])
            cx.copy("act", q_t[0:64, hd, c0:c0 + SL], b1.t[0:64, :], [b1.r], [q_r[s]])
        for j in range(4):
            bk = cx.nb()
            for k in range(8):
                cx.mm(bk.t[:], win.t[:, k, 2464 + j * 128:2464 + (j + 1) * 128], h.t[:, k, :], k == 0, k == 7, [win.r, h.r], [bk.r])
            cx.act(sbg_t[:, j, c0:c0 + SL], bk.t[:], AF.Silu, [bk.r], [sbg_r[s]])

    lat, latg = io["lat"], io["latg"]
    Rlat, Rlatg = P.res("lat"), P.res("latg")
    P.dma("sp", lat[0:128, :], ckvn_t[:, NT:2 * NT], reads=ckvn_r[4:8], writes=[Rlat])
    P.dma("sp", lat[128:160, :], krope_t[64:96, NT:2 * NT], reads=krope_r[4:8], writes=[Rlat])
    P.cc("AllGather", lat, latg.rearrange("r p t -> (r p) t"), PAIRS, reads=[Rlat], writes=[Rlatg])
    P.dma("sp", ckvn_t[:, 0:NT], latg[0, 0:128, :], reads=[Rlatg], writes=ckvn_r[0:4], sem_res=ckvn_r[0])
    P.dma("sp", krope_t[64:96, 0:NT], latg[0, 128:160, :], reads=[Rlatg], writes=krope_r[0:4], sem_res=krope_r[0])

    P.barrier()
    ar.reset(base)

    yb = ar.take([128, 4, NT], BF16, "yb_all")
    yb_r = [P.res("yb%d" % i) for i in range(4)]
    mark = ar.mark()
    vp = [ar.take([128, 32, 192], BF16, "vp%d" % i) for i in range(2)]
    kh = [ar.take([128, 4096], BF16, "kh%d" % i) for i in range(2)]
    pts = [ar.take([128, 512], BF16, "pt%d" % i) for i in range(6)]
    rds = [ar.take([128, 512], F32, "rd%d" % i) for i in range(2)]
    tms = [ar.take([128, 512], F32, "tm%d" % i) for i in range(2)]
    for v_ in vp:
        cx.memset("pool", v_.t[:, :, 64:128], 1.0, [v_.r])
    cx.nbanks_rr = 6
    ada_job = None
    if io.get("ada_jobs"):
        awb = [ar.take([128, 3072], F32, "adaw%d" % i) for i in range(2)]
        aacc = ar.take([128, 3072], F32, "adaacc")

        def _jobs():
            for (l2, io2) in io["ada_jobs"]:
                for _ in cx.ada_steps(l2, io2, awb, aacc):
                    yield
        ada_job = _jobs()
    cnt = 0
    pti = 0
    for j in range(4):
        v_ = vp[j % 2]
        for g in range(8):
            bank = cx.nb()
            for i in range(4):
                kb = 4 * g + i
                cx.mm(bank.t[:, i * 128:(i + 1) * 128], ckvn_t[:, kb * 128:(kb + 1) * 128], wvv.t[:, j * 128:(j + 1) * 128], True, True,
                      [ckvn_r[kb // 4], wvv.r], [bank.r], inc=(i == 3))
            bv = bank.t[:, :].rearrange("p (a b) -> p a b", a=4)
            cx.copy("dve", v_.t[:, 4 * g:4 * g + 4, 0:64], bv[:, :, 0:64], [bank.r], [v_.r])
            cx.copy("act", v_.t[:, 4 * g:4 * g + 4, 128:192], bv[:, :, 64:128], [bank.r], [v_.r])
        for hd in (2 * j, 2 * j + 1):
            k_ = kh[hd % 2]
            cx.copy("pool", k_.t[64:96, :], krope_t[64:96, :], krope_r, [k_.r])
            for kc in range(8):
                bank = cx.nb()
                cx.mm(bank.t[0:64, :], wkk.t[:, hd * 64:(hd + 1) * 64], ckvn_t[:, kc * SL:(kc + 1) * SL], True, True,
                      [wkk.r, ckvn_r[kc]], [bank.r])
                cx.copy("dve", k_.t[0:64, kc * SL:(kc + 1) * SL], bank.t[0:64, :], [bank.r], [k_.r])
            vc0 = 0 if hd % 2 == 0 else 64
            for s in range(NSL):
                if ada_job is not None:
                    try:
                        next(ada_job)
                    except StopIteration:
                        ada_job = None
                accb = cx.banks[6 + cnt % 2]
                cnt += 1
                nkb = 16 + 4 * (s + 1)
                LA = 3
                scs = {}

                def emit_sc(kb, s=s, hd=hd, k_=k_, nkb=nkb):
                    jd = kb - (nkb - 4)
                    lo = 128 * jd if jd >= 0 else 0
                    sc = cx.nb()
                    cx.mm(sc.t[:, lo:SL], k_.t[0:96, kb * 128:(kb + 1) * 128], q_t[0:96, hd, s * SL + lo:(s + 1) * SL], True, True,
                          [k_.r, q_r[s]], [sc.r])
                    scs[kb] = (sc, lo, jd)

                for kb in range(min(LA, nkb)):
                    emit_sc(kb)
                for kb in range(nkb):
                    sc, lo, jd = scs.pop(kb)
                    pt = pts[pti % 6]
                    pti += 1
                    cx.act(pt.t[:, lo:SL], sc.t[:, lo:SL], AF.Exp, [sc.r, keyb.r], [pt.r], bias=keyb.t[:, kb:kb + 1], scale=SCALE)
                    if jd >= 0:
                        cx.memset("dve", pt.t[64:128, lo:lo + 64], 0.0, [pt.r])
                    if kb + LA < nkb:
                        emit_sc(kb + LA)
                    cx.mm(accb.t[:, lo:SL], v_.t[:, kb, vc0:vc0 + 128], pt.t[:, lo:SL], kb == 0, kb == nkb - 1,
                          [v_.r, pt.r], [accb.r], inc=True)
                if hd % 2 == 0:
                    npart, dpart = slice(0, 64), slice(64, 128)
                else:
                    npart, dpart = slice(64, 128), slice(0, 64)
                rd = rds[cnt % 2]
                tm = tms[cnt % 2]
                P.op("dve", lambda e, rd=rd, accb=accb, dpart=dpart: e.reciprocal(out=rd.t[dpart, :], in_=accb.t[dpart, :]), [accb.r], [rd.r])
                cx.tt("dve", tm.t[npart, :], accb.t[npart, :], rd.t[dpart, :], ALU.mult, [accb.r, rd.r], [tm.r])
                cx.tt("pool", yb.t[npart, j, s * SL:(s + 1) * SL], tm.t[npart, :], sbg_t[npart, j, s * SL:(s + 1) * SL], ALU.mult,
                      [tm.r, sbg_r[s]], [yb_r[s]])
    cx.nbanks_rr = 8
    if ada_job is not None:
        for _ in ada_job:
            pass

    P.barrier()
    ar.reset(mark)

    wout = ar.take([128, 8, 1024], BF16, "wout")
    ysbs = [ar.take([128, 8, 512], F32, "ysb%d" % i) for i in range(2)]
    ysqs = [ar.take([128, 512], BF16, "ysq%d" % i) for i in range(2)]
    xcs = [ar.take([128, 512], F32, "xc%d" % i) for i in range(6)]
    tmps = [ar.take([128, 512], F32, "tmp%d" % i) for i in range(2)]
    obufs = [ar.take([128, 512], F32, "ob%d" % i) for i in range(3)]
    rstds = [ar.take([128, 512], F32, "rstd%d" % i) for i in range(2)]
    stg = [ar.take([128, 2048], F32, "stg%d" % i) for i in range(2)]
    cx.load_weight_bf16(lambda k, a, b: wout.t[:, k, a:b], io["w_out"], 8, 1024, wout.r, stg)
    for s in range(NSL):
        ysb = ysbs[s % 2]
        rstd = rstds[s % 2]
        c0 = s * SL

        def rhs_fn(k, c0=c0):
            if k < 4:
                return ya_t[:, k, c0:c0 + SL]
            return yb.t[:, k - 4, c0:c0 + SL]

        cx.out_phase(wout, rhs_fn, [ya_r[s], yb_r[s]], xT[:, :, c0:c0 + SL], outT[:, :, c0:c0 + SL],
                     ysb, ysqs, rstd, xcs, tmps, obufs, halo_d=(io.get("halo_out") if s == NSL - 1 else None))


PAIRS = [[0, 1], [2, 3], [4, 5], [6, 7]]
_DBG_NOCC = False
LAYER_KEYS_EVEN = ["w_in", "w_uq", "w_ukvk", "w_ukvv", "w_out", "scw", "scb", "qg", "kvg"]
LAYER_SHAPES_EVEN = {"w_in": [128, 8, 3008], "w_uq": [128, 2, 1024], "w_ukvk": [128, 1, 512], "w_ukvv": [128, 1, 512],
                     "w_out": [128, 8, 1024], "scw": [128, 4, 3], "scb": [128, 4], "qg": [128, 2], "kvg": [128, 1]}
LAYER_SHAPES_ODD = {"w_in": [128, 8, 3072], "w_out": [128, 8, 1024], "convw": [128, 8, 31], "convb": [128, 8],
                    "lng": [128, 8], "lnb": [128, 8]}
LAYER_SHAPES_COMMON = {"adaw": [128, 8, 3072], "adab": [128, 24], "preg": [128, 8], "postg": [128, 8]}


def build_fused():
    nc = bass.Bass("TRN2", target_bir_lowering=False)
    xT = dram_in(nc, "xT", [128, 8, NT])
    xH0 = dram_in(nc, "xH", [128, 8, 32])
    posO = dram_in(nc, "posO", [1, NT], I32)
    cT = dram_in(nc, "cT", [128, 8])
    hflag = dram_in(nc, "hflag", [128, 1])
    invf_d = dram_in(nc, "invf2", [32, 1])
    sgn_d = dram_in(nc, "sgn2", [32, 1])
    kb_d = dram_in(nc, "keybias", [128, 32])
    identd = dram_in(nc, "ident", [128, 128])
    L = []
    for l in range(4):
        d = {}
        shapes = dict(LAYER_SHAPES_COMMON)
        shapes.update(LAYER_SHAPES_EVEN if l % 2 == 0 else LAYER_SHAPES_ODD)
        for k, shp in shapes.items():
            d[k] = dram_in(nc, "L%d_%s" % (l, k), shp)
        L.append(d)
    outT = nc.dram_tensor("outT", [128, 8, NT], F32, kind="ExternalOutput").ap()
    xs_raw = [nc.dram_tensor("xs%d" % l, [8, 128, NT], F32, kind="Internal").ap() for l in range(3)]
    xs = [t_.rearrange("c p t -> p c t") for t_ in xs_raw]
    hal = [nc.dram_tensor("hal%d" % l, [128, 8, 32], F32, kind="Internal").ap() for l in range(3)]
    halg = [nc.dram_tensor("halg%d" % l, [2, 128, 8, 32], F32, kind="Internal").ap() for l in range(3)]
    lats = {l: nc.dram_tensor("lat%d" % l, [160, NT], BF16, kind="Internal").ap() for l in (0, 2)}
    latgs = {l: nc.dram_tensor("latg%d" % l, [2, 160, NT], BF16, kind="Internal").ap() for l in (0, 2)}

    cx = Ctx(nc)
    P = cx.P
    cx.setup_consts(identd, hflag, invf_d, sgn_d, kb_d)
    cx.ada_setup(cT)
    ar = Arena(P, "arena", 49 * 1024 + 512)
    Rhalg = [P.res("halg%d" % l) for l in range(3)]
    Rsrc = P.res("ccsrc")

    for l in range(4):
        io = dict(L[l])
        io["cT"] = cT
        io["layer"] = l
        if l == 0:
            io["ada_jobs"] = [(l2, L[l2]) for l2 in (1, 2, 3)]
        io["xT"] = xT if l == 0 else xs[l - 1]
        io["out"] = outT if l == 3 else xs[l]
        if l < 3:
            io["halo_out"] = hal[l]
        pre = []
        if l == 0 or _DBG_NOCC:
            io["xF"], io["xH"] = None, xH0
        else:
            def gather_halo(l=l):
                P.cc("AllGather", hal[l - 1].rearrange("p c t -> p (c t)"), halg[l - 1].rearrange("r p c t -> (r p) (c t)"), PAIRS, reads=[Rsrc], writes=[Rhalg[l - 1]])
            pre.append(gather_halo)
            io["xH"] = halg[l - 1][0]
            io["xHres"] = [Rhalg[l - 1]]
            io["xFres"] = [Rhalg[l - 1]]
        io["pre"] = pre
        io["posO"], io["posF"] = posO, None
        if l % 2 == 0:
            io["lat"], io["latg"] = lats[l], latgs[l]
            io.setdefault("xF", None)
        if l % 2 == 0:
            emit_even(cx, ar, io)
        else:
            emit_odd(cx, ar, io)
    P.emit()
    return nc


def fm(v):
    v = np.asarray(v)
    return np.ascontiguousarray(v.reshape(-1, 128).T)


def wk(w):
    w = np.asarray(w)
    k, n = w.shape
    return np.ascontiguousarray(w.reshape(k // 128, 128, n).transpose(1, 0, 2))


def xfm(xb):
    t = xb.shape[0]
    return np.ascontiguousarray(xb.T.reshape(8, 128, t).transpose(1, 0, 2))


def xfm_inv(o):
    t = o.shape[2]
    return o.transpose(1, 0, 2).reshape(D, t).T


def _perm_uq(w_uq):
    w = np.asarray(w_uq)
    blocks = []
    for h in range(8):
        b = h * 96
        blocks += [w[:, b:b + 64], w[:, b + 64:b + 96], w[:, b + 80:b + 96], w[:, b + 64:b + 80]]
    return np.concatenate(blocks, axis=1)


_NC_CACHE = {}


def kernel(x, c, positions, ada_w, ada_b, pre_norm_g, post_norm_g,
           even_w_in, even_sc_conv_w, even_sc_conv_b, even_q_norm_g, even_kv_norm_g,
           even_w_uq, even_w_ukv, even_w_out,
           odd_w_in, odd_conv_w, odd_conv_b, odd_ln_g, odd_ln_b, odd_w_out):
    x = np.ascontiguousarray(np.asarray(x, dtype=np.float32))
    c = np.asarray(c, dtype=np.float32)
    positions = np.asarray(positions)
    if "fused" not in _NC_CACHE:
        _NC_CACHE["fused"] = build_fused()
    nc = _NC_CACHE["fused"]
    invf = (np.float32(1.0) / (np.float32(10000.0) ** (np.arange(0, 32, 2, dtype=np.float32) / np.float32(32)))).astype(np.float32)
    shared = {
        "ident": np.eye(128, dtype=np.float32),
        "invf2": np.ascontiguousarray(np.concatenate([invf, invf])[:, None]),
        "sgn2": np.ascontiguousarray(np.concatenate([-np.ones(16, np.float32), np.ones(16, np.float32)])[:, None]),
    }
    for l in range(4):
        i = l // 2
        p = "L%d_" % l
        shared[p + "adaw"] = wk(ada_w[l])
        shared[p + "adab"] = fm(ada_b[l])
        shared[p + "preg"] = fm(pre_norm_g[l])
        shared[p + "postg"] = fm(post_norm_g[l])
        if l % 2 == 0:
            w_in = np.asarray(even_w_in[i])
            w_ukv = np.asarray(even_w_ukv[i])
            shared[p + "w_in"] = wk(np.concatenate([w_in, w_in[:, 2448:2464], w_in[:, 2432:2448]], axis=1))
            shared[p + "w_uq"] = wk(_perm_uq(even_w_uq[i]))
            shared[p + "w_ukvk"] = np.ascontiguousarray(np.concatenate([w_ukv[:, h * 128:h * 128 + 64] for h in range(8)], axis=1)[:, None, :])
            shared[p + "w_ukvv"] = np.ascontiguousarray(np.concatenate([w_ukv[:, h * 128 + 64:h * 128 + 128] for h in range(8)], axis=1)[:, None, :])
            shared[p + "w_out"] = wk(even_w_out[i])
            shared[p + "scw"] = np.ascontiguousarray(np.asarray(even_sc_conv_w[i]).T.reshape(4, 128, 3).transpose(1, 0, 2))
            shared[p + "scb"] = fm(even_sc_conv_b[i])
            shared[p + "qg"] = fm(even_q_norm_g[i])
            shared[p + "kvg"] = fm(even_kv_norm_g[i])
        else:
            shared[p + "w_in"] = wk(odd_w_in[i])
            shared[p + "w_out"] = wk(odd_w_out[i])
            shared[p + "convw"] = np.ascontiguousarray(np.asarray(odd_conv_w[i]).T.reshape(8, 128, 31).transpose(1, 0, 2))
            shared[p + "convb"] = fm(odd_conv_b[i])
            shared[p + "lng"] = fm(odd_ln_g[i])
            shared[p + "lnb"] = fm(odd_ln_b[i])
    maps = []
    for core in range(8):
        b, half = core // 2, core % 2
        t0 = half * NT
        m = dict(shared)
        m["xT"] = xfm(x[b, t0:t0 + NT])
        m["xH"] = xfm(x[b, NT - 32:NT]) if half == 1 else np.zeros((128, 8, 32), np.float32)
        m["cT"] = fm(c[b])
        m["hflag"] = np.full((128, 1), float(half), np.float32)
        m["posO"] = np.ascontiguousarray(positions[b, t0:t0 + NT][None, :]).astype(np.int32)
        kbias = np.zeros((128, 32), np.float32)
        if half == 0:
            kbias[:, 0:16] = -30000.0
        m["keybias"] = kbias
        maps.append(m)
    res = run_bass_kernel_spmd(nc, maps, core_ids=list(range(8)))
    out = np.empty_like(x)
    for core in range(8):
        b, half = core // 2, core % 2
        out[b, half * NT:(half + 1) * NT] = xfm_inv(res.results[core]["outT"])
    return out
```

```python
from contextlib import ExitStack
import numpy as np
import concourse.bass as bass
import concourse.mybir as mybir
from concourse.bass_utils import run_bass_kernel_spmd

F32 = mybir.dt.float32
BF16 = mybir.dt.bfloat16
I32 = mybir.dt.int32
ALU = mybir.AluOpType
AF = mybir.ActivationFunctionType

D = 1024
NT = 2048
SL = 512
NSL = 4
EPS = 1e-6
PI = 3.141592653589793
TWO_PI = 6.283185307179586

ENGS = ("pe", "dve", "act", "pool", "sp")
SAME_ENGINE_SYNC = True


class Sem:
    def __init__(self, handle, name, is_dma):
        self.h = handle
        self.name = name
        self.total = 0
        self.is_dma = is_dma


class Res:
    __slots__ = ("name", "writer", "readers", "dsem")

    def __init__(self, name):
        self.name = name
        self.writer = None
        self.readers = []
        self.dsem = None


class Buf:
    __slots__ = ("t", "r")

    def __init__(self, t, r):
        self.t = t
        self.r = r


class Prog:
    def __init__(self, nc):
        self.nc = nc
        self.stack = ExitStack()
        self.ops = {e: [] for e in ENGS}
        self.esem = {}
        for e in ENGS:
            self.esem[e] = Sem(self.stack.enter_context(nc.semaphore("s_" + e)), e, False)
        self.waited = {e: {} for e in ENGS}
        self.dma_sems = []
        self.free_dsems = []
        self.all_res = []
        self.nm = 0

    def sbuf(self, name, shape, dtype):
        return self.stack.enter_context(self.nc.sbuf_tensor("sb_" + name, list(shape), dtype))

    def psum(self, name, shape, dtype=F32):
        return self.stack.enter_context(self.nc.psum_tensor("ps_" + name, list(shape), dtype))

    def res(self, name="r"):
        r = Res(name)
        self.all_res.append(r)
        return r

    def buf(self, name, shape, dtype):
        return Buf(self.sbuf(name, shape, dtype), self.res(name))

    def _dsem(self, r):
        if r.dsem is None:
            if self.free_dsems:
                r.dsem = self.free_dsems.pop()
            else:
                self.nm += 1
                nm = "d%d" % self.nm
                r.dsem = Sem(self.stack.enter_context(self.nc.semaphore(nm)), nm, True)
                self.dma_sems.append(r.dsem)
        return r.dsem

    def _collect(self, eng, reads, writes):
        need = {}

        def add(tok):
            if tok is None:
                return
            sem, val, teng = tok
            if teng == eng and (eng == "pe" or not SAME_ENGINE_SYNC):
                return
            if sem.is_dma:
                val = max(val, sem.total)
            if need.get(sem, 0) < val:
                need[sem] = val

        for r in reads:
            add(r.writer)
        for w in writes:
            add(w.writer)
            for t in w.readers:
                add(t)
        wl = []
        wd = self.waited[eng]
        for sem, val in need.items():
            if wd.get(sem, 0) < val:
                wd[sem] = val
                wl.append((sem, val))
        return wl

    def _commit(self, tok, reads, writes):
        for r in reads:
            r.readers.append(tok)
            if len(r.readers) > 64:
                best = {}
                for t in r.readers:
                    if best.get(t[0], (None, 0, None))[1] < t[1]:
                        best[t[0]] = t
                r.readers = list(best.values())
        for w in writes:
            w.writer = tok
            w.readers = []

    def op(self, eng, fn, reads=(), writes=(), inc=True):
        waits = self._collect(eng, reads, writes)
        sem = self.esem[eng]
        if inc:
            sem.total += 1
            tok = (sem, sem.total, eng)
        else:
            tok = (sem, sem.total + 1, eng)
        self.ops[eng].append((waits, fn, (sem, 1) if inc else None))
        self._commit(tok, reads, writes)

    def dma(self, eng, out_ap, in_ap, reads=(), writes=(), sem_res=None):
        waits = self._collect(eng, reads, writes)
        if sem_res is None:
            sem_res = writes[0] if writes else reads[0]
        sem = self._dsem(sem_res)
        sem.total += 16
        tok = (sem, sem.total, "dma")

        def fn(e, out_ap=out_ap, in_ap=in_ap):
            return e.dma_start(out=out_ap, in_=in_ap)

        self.ops[eng].append((waits, fn, (sem, 16)))
        self._commit(tok, reads, writes)

    def cc(self, kind, in_ap, out_ap, groups, reads=(), writes=()):
        waits = self._collect("pool", reads, writes)
        sem = self._dsem(writes[0])
        sem.total += 1
        tok = (sem, sem.total, "dma")

        def fn(e):
            return e.collective_compute(kind, ALU.bypass, replica_groups=groups, ins=[in_ap], outs=[out_ap])

        self.ops["pool"].append((waits, fn, (sem, 1)))
        self._commit(tok, reads, writes)

    def barrier(self):
        for e in ENGS:
            waits = []
            wd = self.waited[e]
            for x in ENGS:
                s = self.esem[x]
                if x != e and s.total > wd.get(s, 0):
                    wd[s] = s.total
                    waits.append((s, s.total))
            for s in self.dma_sems:
                if s.total > wd.get(s, 0):
                    wd[s] = s.total
                    waits.append((s, s.total))
            if waits:
                self.ops[e].append((waits, None, None))
        for r in self.all_res:
            r.writer = None
            r.readers = []
            if r.dsem is not None:
                self.free_dsems.append(r.dsem)
                r.dsem = None

    def emit(self):
        nc = self.nc
        self.barrier()
        engmap = {"pe": "tensor", "dve": "vector", "act": "scalar", "pool": "gpsimd", "sp": "sync"}
        with nc.Block() as block:
            for e in ENGS:
                ops = self.ops[e]

                def body(eobj, ops=ops):
                    for waits, fn, inc in ops:
                        for sem, val in waits:
                            eobj.wait_ge(sem.h, val)
                        if fn is not None:
                            ins = fn(eobj)
                            if inc is not None:
                                ins.then_inc(inc[0].h, inc[1])

                getattr(block, engmap[e])(body)
        self.stack.close()


class Arena:
    def __init__(self, P, name, words):
        self.P = P
        self.t = P.sbuf(name, [128, words], F32)
        self.words = words
        self.off = 0

    def reset(self, mark=0):
        self.off = mark

    def mark(self):
        return self.off

    def take(self, shape, dtype, name="a"):
        n = 1
        for s in shape[1:]:
            n *= s
        esz = 4 if dtype in (F32, I32) else 2
        w = (n * esz + 3) // 4
        w = (w + 7) // 8 * 8
        assert self.off + w <= self.words, ("arena overflow", name, self.off, w, self.words)
        v = self.t[:, self.off:self.off + w]
        self.off += w
        if dtype != F32:
            v = v.bitcast(dtype)
        v = v[:, 0:n]
        if len(shape) == 3:
            v = v.rearrange("p (a b) -> p a b", a=shape[1])
        elif len(shape) == 4:
            v = v.rearrange("p (a b c) -> p a b c", a=shape[1], b=shape[2])
        if shape[0] != 128:
            v = v[0:shape[0]]
        return Buf(v, self.P.res(name))


class Ctx:
    def __init__(self, nc):
        self.nc = nc
        self.P = Prog(nc)
        P = self.P
        self.banks = [Buf(P.psum("bank%d" % i, [128, 512]), P.res("bank%d" % i)) for i in range(8)]
        self.bi = 0
        self.nbanks_rr = 8
        self.stg_i = 0

    def nb(self):
        b = self.banks[self.bi % self.nbanks_rr]
        self.bi += 1
        return b

    def tt(self, eng, out, in0, in1, op, reads, writes):
        self.P.op(eng, lambda e: e.tensor_tensor(out=out, in0=in0, in1=in1, op=op), reads, writes)

    def ts(self, eng, out, in0, s1, op0, reads, writes, s2=None, op1=None):
        if op1 is None:
            self.P.op(eng, lambda e: e.tensor_scalar(out=out, in0=in0, scalar1=s1, scalar2=None, op0=op0), reads, writes)
        else:
            self.P.op(eng, lambda e: e.tensor_scalar(out=out, in0=in0, scalar1=s1, scalar2=s2, op0=op0, op1=op1), reads, writes)

    def stt(self, out, in0, scalar, in1, op0, op1, reads, writes):
        self.P.op("dve", lambda e: e.scalar_tensor_tensor(out=out, in0=in0, scalar=scalar, in1=in1, op0=op0, op1=op1), reads, writes)

    def act(self, out, in_, func, reads, writes, bias=None, scale=None):
        kw = {}
        if bias is not None:
            kw["bias"] = bias
        if scale is not None:
            kw["scale"] = scale
        self.P.op("act", lambda e: e.activation(out=out, in_=in_, func=func, **kw), reads, writes)

    def copy(self, eng, out, in_, reads, writes):
        if eng == "act":
            self.P.op("act", lambda e: e.activation(out=out, in_=in_, func=AF.Copy), reads, writes)
        else:
            self.P.op(eng, lambda e: e.tensor_copy(out=out, in_=in_), reads, writes)

    def memset(self, eng, ap, val, writes):
        self.P.op(eng, lambda e: e.memset(ap, val), (), writes)

    def mm(self, out, lhsT, rhs, start, stop, reads, writes, inc=None):
        if inc is None:
            inc = stop
        self.P.op("pe", lambda e: e.matmul(out, lhsT=lhsT, rhs=rhs, start=start, stop=stop), reads, writes, inc=inc)

    def load(self, buf_ap, dram_ap, writes, eng="sp"):
        self.P.dma(eng, buf_ap, dram_ap, writes=writes)

    def setup_consts(self, ident_dram, hflag_d, invf_d, sgn_d, kb_d):
        P = self.P
        self.ones_bf = P.buf("ones_bf", [128, 128], BF16)
        self.memset("pool", self.ones_bf.t[:], 1.0, [self.ones_bf.r])
        self.ones_f = P.buf("ones_f", [128, 2], F32)
        self.memset("pool", self.ones_f.t[:], 1.0, [self.ones_f.r])
        self.epsb = P.buf("epsb", [128, 1], F32)
        self.memset("pool", self.epsb.t[:], EPS, [self.epsb.r])
        self.ident = P.buf("ident_bf", [128, 128], BF16)
        P.dma("pool", self.ident.t[:], ident_dram, writes=[self.ident.r])
        self.hf = P.buf("hf", [128, 1], F32)
        self.invf = P.buf("invf", [32, 1], F32)
        self.sgn = P.buf("sgn", [32, 1], F32)
        self.keyb = P.buf("keyb", [128, 32], F32)
        for b_, d_ in ((self.hf, hflag_d), (self.invf, invf_d), (self.sgn, sgn_d), (self.keyb, kb_d)):
            self.load(b_.t[:], d_, [b_.r])

    def ada_setup(self, cT_d):
        P = self.P
        self.cT = P.buf("cT", [128, 8], F32)
        self.cact = P.buf("cact", [128, 8], F32)
        self.load(self.cT.t[:], cT_d, [self.cT.r])
        self.act(self.cact.t[:], self.cT.t[:], AF.Silu, [self.cT.r], [self.cact.r])
        self.ada_sets = []
        for l in range(4):
            d = {k: P.buf("%s_%d" % (k, l), [128, n], F32) for k, n in
                 (("adab", 24), ("preg", 8), ("postg", 8), ("modT", 24), ("A", 8), ("G2", 8))}
            d["done"] = False
            self.ada_sets.append(d)

    def ada_select(self, l):
        d = self.ada_sets[l]
        self.adab, self.preg, self.postg = d["adab"], d["preg"], d["postg"]
        self.modT, self.A, self.G2 = d["modT"], d["A"], d["G2"]
        return d

    def ada_steps(self, l, io, wb, acc):
        d = self.ada_sets[l]
        adab, preg, postg, modT, A, G2 = d["adab"], d["preg"], d["postg"], d["modT"], d["A"], d["G2"]
        cact = self.cact
        self.load(adab.t[:], io["adab"], [adab.r])
        self.load(preg.t[:], io["preg"], [preg.r])
        self.load(postg.t[:], io["postg"], [postg.r])
        nb_ = len(wb)

        def dve_piece(kc):
            b = wb[kc % nb_]
            if kc == 0:
                self.ts("dve", acc.t[:], b.t[:], cact.t[:, 0:1], ALU.mult, [b.r, cact.r], [acc.r])
            else:
                self.stt(acc.t[:], b.t[:], cact.t[:, kc:kc + 1], acc.t[:], ALU.mult, ALU.add, [b.r, cact.r, acc.r], [acc.r])

        for g in range(0, 8, nb_):
            for kc in range(g, min(8, g + nb_)):
                b = wb[kc % nb_]
                self.load(b.t[:], io["adaw"][:, kc, :], [b.r])
            yield
            for kc in range(g, min(8, g + nb_)):
                dve_piece(kc)
        yield
        bank = self.nb()
        for m in range(24):
            self.mm(bank.t[:, 2 * m:2 * m + 2], acc.t[:, m * 128:(m + 1) * 128], self.ones_f.t[:, 0:2], True, True,
                    [acc.r, self.ones_f.r], [bank.r], inc=True)
        pv = bank.t[:, 0:48].rearrange("p (m t) -> p m t", t=2)[:, :, 0]
        self.tt("dve", modT.t[:], pv, adab.t[:], ALU.add, [bank.r, adab.r], [modT.r])
        self.stt(A.t[:], modT.t[:, 8:16], 1.0, preg.t[:], ALU.add, ALU.mult, [modT.r, preg.r], [A.r])
        self.tt("dve", G2.t[:], modT.t[:, 16:24], postg.t[:], ALU.mult, [modT.r, postg.r], [G2.r])
        d["done"] = True
        yield

    def ada_phase(self, ar, l, io):
        if self.ada_sets[l]["done"]:
            return False
        wb = [ar.take([128, 3072], F32, "adaw%d" % i) for i in range(2)]
        acc = ar.take([128, 3072], F32, "adaacc")
        for _ in self.ada_steps(l, io, wb, acc):
            pass
        return True

    def rstd_from(self, ssbank, n, dim, out):
        self.act(out.t[:, 0:n], ssbank.t[:, 0:n], AF.Ln, [ssbank.r, self.epsb.r], [out.r], bias=self.epsb.t[:], scale=1.0 / dim)
        self.act(out.t[:, 0:n], out.t[:, 0:n], AF.Exp, [out.r], [out.r], scale=-0.5)

    def prenorm(self, x_d, n, h, xcs, xsqs, rstd, tmps, xres=()):
        ss = self.nb()
        for c in range(8):
            xc = xcs[c % len(xcs)]
            self.P.dma("sp", xc.t[:, 0:n], x_d[:, c, :], reads=list(xres), writes=[xc.r], sem_res=xc.r)
            sq = xsqs[c % len(xsqs)]
            self.act(sq.t[:, 0:n], xc.t[:, 0:n], AF.Square, [xc.r], [sq.r])
            self.mm(ss.t[:, 0:n], self.ones_bf.t[:], sq.t[:, 0:n], c == 0, c == 7, [self.ones_bf.r, sq.r], [ss.r], inc=True)
        self.rstd_from(ss, n, float(D), rstd)
        for c in range(8):
            xc = xcs[c % len(xcs)]
            self.P.dma("sp", xc.t[:, 0:n], x_d[:, c, :], reads=list(xres), writes=[xc.r], sem_res=xc.r)
            tm = tmps[c % len(tmps)]
            self.stt(tm.t[:, 0:n], xc.t[:, 0:n], self.A.t[:, c:c + 1], rstd.t[:, 0:n], ALU.mult, ALU.mult,
                     [xc.r, self.A.r, rstd.r], [tm.r])
            self.act(h.t[:, c, 0:n], tm.t[:, 0:n], AF.Identity, [tm.r, self.modT.r], [h.r], bias=self.modT.t[:, c:c + 1])

    def out_mm(self, wout, rhs_fn, rhs_res, ysb, ysqs, rstd):
        ss = self.banks[7]
        self.nbanks_rr = 7
        for m in range(8):
            bank = self.nb()
            for k in range(8):
                self.mm(bank.t[:], wout.t[:, k, m * 128:(m + 1) * 128], rhs_fn(k), k == 0, k == 7,
                        [wout.r] + rhs_res, [bank.r])
            self.copy("act", ysb.t[:, m, :], bank.t[:], [bank.r], [ysb.r])
            sq = ysqs[m % len(ysqs)]
            self.act(sq.t[:], bank.t[:], AF.Square, [bank.r], [sq.r])
            self.mm(ss.t[:], self.ones_bf.t[:], sq.t[:], m == 0, m == 7, [self.ones_bf.r, sq.r], [ss.r], inc=True)
        self.rstd_from(ss, SL, float(D), rstd)
        self.nbanks_rr = 8

    def out_res(self, x_d, out_d, ysb, rstd, xcs, tmps, obufs, xres=(), halo_d=None):
        for c in range(8):
            xc = xcs[c % len(xcs)]
            self.P.dma("sp", xc.t[:], x_d[:, c, :], reads=list(xres), writes=[xc.r], sem_res=xc.r)
            tm = tmps[c % len(tmps)]
            self.tt("dve", tm.t[:], ysb.t[:, c, :], rstd.t[:], ALU.mult, [ysb.r, rstd.r], [tm.r])
            ob = obufs[c % len(obufs)]
            self.stt(ob.t[:], tm.t[:], self.G2.t[:, c:c + 1], xc.t[:], ALU.mult, ALU.add, [tm.r, self.G2.r, xc.r], [ob.r])
            self.P.dma("pool", out_d[:, c, :], ob.t[:], reads=[ob.r], sem_res=ob.r)
            if halo_d is not None:
                self.P.dma("pool", halo_d[:, c, :], ob.t[:, SL - 32:SL], reads=[ob.r], sem_res=ob.r)

    def out_phase(self, wout, rhs_fn, rhs_res, x_d, out_d, ysb, ysqs, rstd, xcs, tmps, obufs, xres=(), ores=(), halo_d=None):
        self.out_mm(wout, rhs_fn, rhs_res, ysb, ysqs, rstd)
        self.out_res(x_d, out_d, ysb, rstd, xcs, tmps, obufs, xres=xres, halo_d=halo_d)

    def load_weight_bf16(self, wbuf_ap_fn, dram, nk, ncols, res, stg):
        for k in range(nk):
            c0 = 0
            while c0 < ncols:
                c1 = min(ncols, c0 + 2048)
                i = self.stg_i
                self.stg_i += 1
                sb = stg[i % len(stg)]
                self.P.dma("sp", sb.t[:, 0:c1 - c0], dram[:, k, c0:c1], writes=[sb.r])
                self.copy("dve" if i % 2 == 0 else "act", wbuf_ap_fn(k, c0, c1), sb.t[:, 0:c1 - c0], [sb.r], [res])
                c0 = c1


def dram_in(nc, name, shape, dtype=F32):
    return nc.dram_tensor(name, list(shape), dtype, kind="ExternalInput").ap()


def emit_odd(cx, ar, io):
    P = cx.P
    xT, xH, outT = io["xT"], io["xH"], io["out"]
    xHres = io.get("xHres", ())
    P.barrier()
    ar.reset()
    cx.ada_select(io["layer"])
    win = ar.take([128, 8, 3072], BF16, "win")
    wout = ar.take([128, 8, 1024], BF16, "wout")
    cw = ar.take([128, 8, 31], F32, "cw")
    cb = ar.take([128, 8], F32, "cb")
    lg = ar.take([128, 8], F32, "lg")
    lb = ar.take([128, 8], F32, "lb")
    base = ar.mark()
    stg = [ar.take([128, 2048], F32, "stg%d" % i) for i in range(4)]
    cx.load_weight_bf16(lambda k, a, b: win.t[:, k, a:b], io["w_in"], 8, 3072, win.r, stg)
    cx.load_weight_bf16(lambda k, a, b: wout.t[:, k, a:b], io["w_out"], 8, 1024, wout.r, stg)
    for pre in io.get("pre", ()):
        pre()
    for b_, d_ in ((cw, io["convw"]), (cb, io["convb"]), (lg, io["lng"]), (lb, io["lnb"])):
        cx.load(b_.t[:], d_, [b_.r])
    hf = cx.hf
    cx.ada_phase(ar, io["layer"], io)
    P.barrier()
    ar.reset(base)

    xcs = [ar.take([128, 512], F32, "xc%d" % i) for i in range(3)]
    xsqs = [ar.take([128, 512], BF16, "xsq%d" % i) for i in range(2)]
    tmps = [ar.take([128, 512], F32, "tmp%d" % i) for i in range(3)]
    rstd = ar.take([128, 512], F32, "rstd")
    hs = [ar.take([128, 8, 512], BF16, "h%d" % i) for i in range(2)]
    u = [ar.take([128, 544], BF16, "u%d" % j) for j in range(8)]
    convs = [ar.take([128, 8, 512], F32, "conv%d" % i) for i in range(2)]
    sgates = [ar.take([128, 8, 512], BF16, "sgate%d" % i) for i in range(2)]
    gated = ar.take([128, 8, 512], BF16, "gated")
    diag = [ar.take([128, 31, 128], BF16, "diag%d" % i) for i in range(2)]
    cbf = [ar.take([128, 512], BF16, "cbf%d" % i) for i in range(2)]
    csq = [ar.take([128, 512], BF16, "csq%d" % i) for i in range(2)]
    means = [ar.take([128, 512], F32, "mean%d" % i) for i in range(2)]
    vars_ = [ar.take([128, 512], F32, "var%d" % i) for i in range(2)]
    obufs = [ar.take([128, 512], F32, "ob%d" % i) for i in range(2)]
    rstd_o = ar.take([128, 512], F32, "rstd_o")
    ysqs_o = [ar.take([128, 512], BF16, "ysqo%d" % i) for i in range(2)]

    hh = hs[1]
    cx.prenorm(xH, 32, hh, xcs, xsqs, rstd, tmps, xres=xHres)
    for j in range(8):
        bv = cx.nb()
        bg = cx.nb()
        for k in range(8):
            cx.mm(bv.t[:, 0:32], win.t[:, k, j * 128:(j + 1) * 128], hh.t[:, k, 0:32], k == 0, k == 7, [win.r, hh.r], [bv.r])
        for k in range(8):
            cx.mm(bg.t[:, 0:32], win.t[:, k, 1024 + j * 128:1024 + (j + 1) * 128], hh.t[:, k, 0:32], k == 0, k == 7, [win.r, hh.r], [bg.r])
        tm = tmps[j % 3]
        cx.act(tm.t[:, 0:32], bg.t[:, 0:32], AF.Sigmoid, [bg.r], [tm.r])
        cx.tt("dve", tm.t[:, 0:32], bv.t[:, 0:32], tm.t[:, 0:32], ALU.mult, [bv.r, tm.r], [tm.r])
        cx.ts("dve", u[j].t[:, 512:542], tm.t[:, 2:32], hf.t[:, 0:1], ALU.mult, [tm.r, hf.r], [u[j].r])

    def A_pre(s):
        cx.prenorm(xT[:, :, s * SL:(s + 1) * SL], SL, hs[s % 2], xcs, xsqs, rstd, tmps)

    def A_body(s, between=None):
        conv, sgate, mean, var = convs[s % 2], sgates[s % 2], means[s % 2], vars_[s % 2]
        h = hs[s % 2]
        mbank = cx.banks[6]
        ebank = cx.banks[7]
        cx.nbanks_rr = 6
        wb = {}

        def emit_win(j):
            b0 = 3 * (j % 2)
            bv, bg, bs = cx.banks[b0], cx.banks[b0 + 1], cx.banks[b0 + 2]
            for (bk, off) in ((bv, 0), (bg, 1024), (bs, 2048)):
                for k in range(8):
                    cx.mm(bk.t[:], win.t[:, k, off + j * 128:off + (j + 1) * 128], h.t[:, k, :], k == 0, k == 7, [win.r, h.r], [bk.r])
            dg = diag[j % 2]
            cx.tt("dve", dg.t[:], cx.ident.t[:].unsqueeze(1).to_broadcast([128, 31, 128]),
                  cw.t[:, j, :].unsqueeze(2).to_broadcast([128, 31, 128]), ALU.mult, [cx.ident.r, cw.r], [dg.r])
            wb[j] = (bv, bg, bs, dg)

        emit_win(0)
        for j in range(8):
            if j + 1 < 8:
                emit_win(j + 1)
            bv, bg, bs, dg = wb.pop(j)
            cx.copy("pool", u[j].t[:, 0:30], u[j].t[:, 512:542], [u[j].r], [u[j].r])
            tm = tmps[j % 3]
            cx.act(tm.t[:], bg.t[:], AF.Sigmoid, [bg.r], [tm.r])
            cx.tt("dve", u[j].t[:, 30:542], bv.t[:], tm.t[:], ALU.mult, [bv.r, tm.r], [u[j].r])
            cx.act(sgate.t[:, j, :], bs.t[:], AF.Silu, [bs.r], [sgate.r])
            cb_ = bg
            for k in range(31):
                cx.mm(cb_.t[:], dg.t[:, k, :], u[j].t[:, k:k + 512], k == 0, k == 30, [dg.r, u[j].r], [cb_.r])
            cx.act(conv.t[:, j, :], cb_.t[:], AF.Identity, [cb_.r, cb.r], [conv.r], bias=cb.t[:, j:j + 1])
            q_ = csq[j % 2]
            cx.act(q_.t[:], cb_.t[:], AF.Square, [cb_.r, cb.r], [q_.r], bias=cb.t[:, j:j + 1])
            c_ = cbf[j % 2]
            cx.copy("dve", c_.t[:], conv.t[:, j, :], [conv.r], [c_.r])
            cx.mm(mbank.t[:], cx.ones_bf.t[:], c_.t[:], j == 0, j == 7, [cx.ones_bf.r, c_.r], [mbank.r], inc=True)
            cx.mm(ebank.t[:], cx.ones_bf.t[:], q_.t[:], j == 0, j == 7, [cx.ones_bf.r, q_.r], [ebank.r], inc=True)
            if between is not None:
                between(j)
        cx.nbanks_rr = 8
        cx.ts("dve", mean.t[:], mbank.t[:], 1.0 / D, ALU.mult, [mbank.r], [mean.r])
        cx.tt("dve", var.t[:], mean.t[:], mean.t[:], ALU.mult, [mean.r], [var.r])
        cx.stt(var.t[:], ebank.t[:], 1.0 / D, var.t[:], ALU.mult, ALU.subtract, [ebank.r, var.r], [var.r])
        cx.act(var.t[:], var.t[:], AF.Ln, [var.r, cx.epsb.r], [var.r], bias=cx.epsb.t[:])
        cx.act(var.t[:], var.t[:], AF.Exp, [var.r], [var.r], scale=-0.5)
    def B_norm_j(s, j):
        conv, sgate, mean, var = convs[s % 2], sgates[s % 2], means[s % 2], vars_[s % 2]
        tm = tmps[j % 3]
        cx.tt("dve", tm.t[:], conv.t[:, j, :], mean.t[:], ALU.subtract, [conv.r, mean.r], [tm.r])
        cx.tt("dve", tm.t[:], tm.t[:], var.t[:], ALU.mult, [tm.r, var.r], [tm.r])
        cx.act(tm.t[:], tm.t[:], AF.Silu, [tm.r, lg.r, lb.r], [tm.r], bias=lb.t[:, j:j + 1], scale=lg.t[:, j:j + 1])
        cx.tt("pool", gated.t[:, j, :], tm.t[:], sgate.t[:, j, :], ALU.mult, [tm.r, sgate.r], [gated.r])

    def B_mm(s):
        cx.out_mm(wout, lambda k: gated.t[:, k, :], [gated.r], convs[s % 2], ysqs_o, rstd_o)

    def B_res(s):
        cx.out_res(xT[:, :, s * SL:(s + 1) * SL], outT[:, :, s * SL:(s + 1) * SL], convs[s % 2], rstd_o, xcs, tmps, obufs,
                   halo_d=(io.get("halo_out") if s == NSL - 1 else None))

    A_pre(0)
    A_body(0)
    A_pre(1)
    for s in range(1, NSL):
        A_body(s, between=lambda j, s=s: B_norm_j(s - 1, j))
        B_mm(s - 1)
        if s + 1 < NSL:
            A_pre(s + 1)
        B_res(s - 1)
    for j in range(8):
        B_norm_j(NSL - 1, j)
    B_mm(NSL - 1)
    B_res(NSL - 1)


def emit_even(cx, ar, io):
    P = cx.P
    xT, xF, xH, outT = io["xT"], io["xF"], io["xH"], io["out"]
    xFres = io.get("xFres", ())
    posO, posF = io["posO"], io["posF"]
    SCALE = 1.0 / float(np.sqrt(96.0))
    hf, invf, sgn, keyb = cx.hf, cx.invf, cx.sgn, cx.keyb
    P.barrier()
    ar.reset()
    cx.ada_select(io["layer"])
    wuq = ar.take([128, 2, 1024], BF16, "wuq")
    wkk = ar.take([128, 512], BF16, "wkk")
    wvv = ar.take([128, 512], BF16, "wvv")
    scw = ar.take([128, 4, 3], F32, "scw")
    scb = ar.take([128, 4], F32, "scb")
    qg = ar.take([128, 2], F32, "qg")
    kvg = ar.take([128, 1], F32, "kvg")
    cxh = ar.take([128, 4, 2], F32, "cxh")
    ckvn_t = ar.take([128, 4096], BF16, "ckvn_all").t
    ckvn_r = [P.res("ckvn%d" % i) for i in range(8)]
    krope_t = ar.take([128, 4096], BF16, "krope_all").t
    krope_r = [P.res("krope%d" % i) for i in range(8)]
    ya_t = ar.take([128, 4, NT], BF16, "ya_all").t
    ya_r = [P.res("ya%d" % i) for i in range(4)]
    sbg_t = ar.take([128, 4, NT], BF16, "sbg_all").t
    sbg_r = [P.res("sbg%d" % i) for i in range(4)]
    q_t = ar.take([128, 8, NT], BF16, "q_all").t
    q_r = [P.res("q%d" % i) for i in range(4)]
    base = ar.mark()
    for b_, d_ in ((scw, io["scw"]), (scb, io["scb"]), (qg, io["qg"]), (kvg, io["kvg"])):
        cx.load(b_.t[:], d_, [b_.r])
    win = ar.take([128, 8, 3008], BF16, "win")
    base1 = ar.mark()
    stg = [ar.take([128, 2048], F32, "stg%d" % i) for i in range(3)]
    cx.load_weight_bf16(lambda k, a, b: win.t[:, k, a:b], io["w_in"], 8, 3008, win.r, stg)
    cx.load_weight_bf16(lambda k, a, b: wuq.t[:, k, a:b], io["w_uq"], 2, 1024, wuq.r, stg)
    cx.load_weight_bf16(lambda k, a, b: wkk.t[:, a:b], io["w_ukvk"], 1, 512, wkk.r, stg)
    cx.load_weight_bf16(lambda k, a, b: wvv.t[:, a:b], io["w_ukvv"], 1, 512, wvv.r, stg)
    for pre in io.get("pre", ()):
        pre()
    cx.ada_phase(ar, io["layer"], io)
    P.barrier()
    ar.reset(base1)

    xcs = [ar.take([128, 512], F32, "xc%d" % i) for i in range(2)]
    xsqs = [ar.take([128, 512], BF16, "xsq%d" % i) for i in range(2)]
    tmps = [ar.take([128, 512], F32, "tmp%d" % i) for i in range(2)]
    tA = [ar.take([128, 512], F32, "tA%d" % i) for i in range(1)]
    tB = [ar.take([128, 512], F32, "tB%d" % i) for i in range(1)]
    acc2 = [ar.take([128, 512], F32, "acc%d" % i) for i in range(1)]
    rstd = ar.take([128, 512], F32, "rstd")
    rstd2 = ar.take([128, 512], F32, "rstd2")
    hs = [ar.take([128, 8, 512], BF16, "h%d" % i) for i in range(2)]
    cxb = [ar.take([128, 514], F32, "cxb%d" % i) for i in range(2)]
    cqn = ar.take([128, 2, 512], BF16, "cqn")
    sqb = [ar.take([128, 512], BF16, "sqb%d" % i) for i in range(2)]
    posi = ar.take([32, 512], I32, "posi")
    ang = ar.take([32, 512], F32, "ang")
    kfl = ar.take([32, 512], F32, "kfl")
    msk = ar.take([32, 512], F32, "msk")
    cos2 = ar.take([32, 512], F32, "cos2")
    sin2 = ar.take([32, 512], F32, "sin2")
    rt1 = [ar.take([128, 512], F32, "rt1_%d" % i) for i in range(1)]
    rt2 = [ar.take([128, 512], F32, "rt2_%d" % i) for i in range(1)]

    def rope_tables(pos_d, c0):
        P.dma("sp", posi.t[:], pos_d[0:1, c0:c0 + SL].partition_broadcast(32), writes=[posi.r])
        cx.copy("dve", ang.t[:], posi.t[:], [posi.r], [ang.r])
        cx.ts("dve", ang.t[:], ang.t[:], invf.t[:, 0:1], ALU.mult, [ang.r, invf.r], [ang.r])
        cx.ts("dve", posi.t[:], ang.t[:], 1.0 / TWO_PI, ALU.mult, [ang.r], [posi.r])
        cx.copy("dve", kfl.t[:], posi.t[:], [posi.r], [kfl.r])
        cx.stt(ang.t[:], kfl.t[:], -TWO_PI, ang.t[:], ALU.mult, ALU.add, [kfl.r, ang.r], [ang.r])
        cx.ts("dve", msk.t[:], ang.t[:], PI, ALU.is_gt, [ang.r], [msk.r], s2=-TWO_PI, op1=ALU.mult)
        cx.tt("dve", ang.t[:], ang.t[:], msk.t[:], ALU.add, [ang.r, msk.r], [ang.r])
        cx.ts("dve", msk.t[:], ang.t[:], -PI, ALU.is_lt, [ang.r], [msk.r], s2=TWO_PI, op1=ALU.mult)
        cx.tt("dve", ang.t[:], ang.t[:], msk.t[:], ALU.add, [ang.r, msk.r], [ang.r])
        cx.act(sin2.t[:], ang.t[:], AF.Sin, [ang.r], [sin2.r])
        cx.ts("dve", sin2.t[:], sin2.t[:], sgn.t[:, 0:1], ALU.mult, [sin2.r, sgn.r], [sin2.r])
        cx.ts("dve", kfl.t[:], ang.t[:], PI / 2, ALU.add, [ang.r], [kfl.r])
        cx.ts("dve", msk.t[:], kfl.t[:], PI, ALU.is_gt, [kfl.r], [msk.r], s2=-TWO_PI, op1=ALU.mult)
        cx.tt("dve", kfl.t[:], kfl.t[:], msk.t[:], ALU.add, [kfl.r, msk.r], [kfl.r])
        cx.act(cos2.t[:], kfl.t[:], AF.Sin, [kfl.r], [cos2.r])

    def kv_group(h, kcol0, blk):
        bkv = cx.nb()
        bkx = cx.nb()
        bky = cx.nb()
        for k in range(8):
            cx.mm(bkv.t[:], win.t[:, k, 2304:2432], h.t[:, k, :], k == 0, k == 7, [win.r, h.r], [bkv.r])
        for k in range(8):
            cx.mm(bkx.t[64:96, :], win.t[:, k, 2432:2464], h.t[:, k, :], k == 0, k == 7, [win.r, h.r], [bkx.r])
        for k in range(8):
            cx.mm(bky.t[64:96, :], win.t[:, k, 2976:3008], h.t[:, k, :], k == 0, k == 7, [win.r, h.r], [bky.r])
        sq = sqb[0]
        cx.act(sq.t[:], bkv.t[:], AF.Square, [bkv.r], [sq.r])
        ss = cx.nb()
        cx.mm(ss.t[:], cx.ones_bf.t[:], sq.t[:], True, True, [cx.ones_bf.r, sq.r], [ss.r])
        cx.rstd_from(ss, SL, 128.0, rstd2)
        cx.stt(ckvn_t[:, kcol0:kcol0 + SL], bkv.t[:], kvg.t[:, 0:1], rstd2.t[:], ALU.mult, ALU.mult,
               [bkv.r, kvg.r, rstd2.r], [ckvn_r[blk]])
        a, b = rt1[0], rt2[0]
        cx.tt("dve", a.t[64:96, :], bkx.t[64:96, :], cos2.t[:], ALU.mult, [bkx.r, cos2.r], [a.r])
        cx.tt("dve", b.t[64:96, :], bky.t[64:96, :], sin2.t[:], ALU.mult, [bky.r, sin2.r], [b.r])
        cx.tt("pool", krope_t[64:96, kcol0:kcol0 + SL], a.t[64:96, :], b.t[64:96, :], ALU.add, [a.r, b.r], [krope_r[blk]])

    hh = hs[1]
    cx.prenorm(xH, 32, hh, xcs, xsqs, rstd, tmps, xres=xFres)
    for j in range(4):
        bc = cx.nb()
        bx = cx.nb()
        for k in range(8):
            cx.mm(bc.t[:, 0:32], win.t[:, k, 512 + j * 128:512 + (j + 1) * 128], hh.t[:, k, 0:32], k == 0, k == 7, [win.r, hh.r], [bc.r])
        for k in range(8):
            cx.mm(bx.t[:, 0:32], win.t[:, k, 1024 + j * 128:1024 + (j + 1) * 128], hh.t[:, k, 0:32], k == 0, k == 7, [win.r, hh.r], [bx.r])
        tm = tA[0]
        cx.copy("act", tm.t[:, 0:32], bc.t[:, 0:32], [bc.r], [tm.r])
        cx.tt("dve", tm.t[:, 0:32], bx.t[:, 0:32], tm.t[:, 0:32], ALU.mult, [bx.r, tm.r], [tm.r])
        cx.ts("dve", cxh.t[:, j, 0:2], tm.t[:, 30:32], hf.t[:, 0:1], ALU.mult, [tm.r, hf.r], [cxh.r])

    cx.prenorm(xT[:, :, 0:SL], SL, hs[0], xcs, xsqs, rstd, tmps)
    for s in range(NSL):
        h = hs[s % 2]
        c0 = s * SL
        rope_tables(posO, c0)
        for j in range(4):
            bb, bc, bx, bg = cx.nb(), cx.nb(), cx.nb(), cx.nb()
            for (bk, off) in ((bb, 0), (bc, 512), (bx, 1024), (bg, 1536)):
                for k in range(8):
                    cx.mm(bk.t[:], win.t[:, k, off + j * 128:off + (j + 1) * 128], h.t[:, k, :], k == 0, k == 7, [win.r, h.r], [bk.r])
            ta = tA[0]
            tb = tB[0]
            cb_ = cxb[j % 2]
            ac = acc2[0]
            cx.copy("act", ta.t[:], bc.t[:], [bc.r], [ta.r])
            cx.copy("pool", cb_.t[:, 0:2], cxh.t[:, j, 0:2], [cxh.r], [cb_.r])
            cx.tt("dve", cb_.t[:, 2:514], bx.t[:], ta.t[:], ALU.mult, [bx.r, ta.r], [cb_.r])
            cx.copy("pool", cxh.t[:, j, 0:2], cb_.t[:, 512:514], [cb_.r], [cxh.r])
            cx.ts("dve", ac.t[:], cb_.t[:, 2:514], scw.t[:, j, 2:3], ALU.mult, [cb_.r, scw.r, scb.r], [ac.r], s2=scb.t[:, j:j + 1], op1=ALU.add)
            cx.stt(ac.t[:], cb_.t[:, 1:513], scw.t[:, j, 1:2], ac.t[:], ALU.mult, ALU.add, [cb_.r, scw.r, ac.r], [ac.r])
            cx.stt(ac.t[:], cb_.t[:, 0:512], scw.t[:, j, 0:1], ac.t[:], ALU.mult, ALU.add, [cb_.r, scw.r, ac.r], [ac.r])
            cx.act(tb.t[:], bg.t[:], AF.Silu, [bg.r], [tb.r])
            cx.tt("dve", ac.t[:], ac.t[:], bb.t[:], ALU.mult, [ac.r, bb.r], [ac.r])
            cx.tt("pool", ya_t[:, j, c0:c0 + SL], ac.t[:], tb.t[:], ALU.mult, [ac.r, tb.r], [ya_r[s]])
        if s + 1 < NSL:
            cx.prenorm(xT[:, :, c0 + SL:c0 + 2 * SL], SL, hs[(s + 1) % 2], xcs, xsqs, rstd, tmps)
        kv_group(h, NT + c0, 4 + s)
        bq = [cx.nb(), cx.nb()]
        for i in range(2):
            for k in range(8):
                cx.mm(bq[i].t[:], win.t[:, k, 2048 + i * 128:2048 + (i + 1) * 128], h.t[:, k, :], k == 0, k == 7, [win.r, h.r], [bq[i].r])
        ss = cx.nb()
        for i in range(2):
            sq = sqb[i]
            cx.act(sq.t[:], bq[i].t[:], AF.Square, [bq[i].r], [sq.r])
            cx.mm(ss.t[:], cx.ones_bf.t[:], sq.t[:], i == 0, i == 1, [cx.ones_bf.r, sq.r], [ss.r], inc=True)
        cx.rstd_from(ss, SL, 256.0, rstd2)
        for i in range(2):
            cx.stt(cqn.t[:, i, :], bq[i].t[:], qg.t[:, i:i + 1], rstd2.t[:], ALU.mult, ALU.mult, [bq[i].r, qg.r, rstd2.r], [cqn.r])
        for hd in range(8):
            b1, b2 = cx.nb(), cx.nb()
            for i in range(2):
                cx.mm(b1.t[0:96, :], wuq.t[:, i, hd * 128:hd * 128 + 96], cqn.t[:, i, :], i == 0, i == 1, [wuq.r, cqn.r], [b1.r])
            for i in range(2):
                cx.mm(b2.t[64:96, :], wuq.t[:, i, hd * 128 + 96:hd * 128 + 128], cqn.t[:, i, :], i == 0, i == 1, [wuq.r, cqn.r], [b2.r])
            a, b = rt1[0], rt2[0]
            cx.tt("dve", a.t[64:96, :], b1.t[64:96, :], cos2.t[:], ALU.mult, [b1.r, cos2.r], [a.r])
            cx.tt("dve", b.t[64:96, :], b2.t[64:96, :], sin2.t[:], ALU.mult, [b2.r, sin2.r], [b.r])
            cx.tt("pool", q_t[64:96, hd, c0:c0 + SL], a.t[64:96, :], b.t[64:96, :], ALU.add, [a.r, b.r], [q_r[s]])
            cx.copy("act", q_t[0:64, hd, c0:c0 + SL], b1.t[0:64, :], [b1.r], [q_r[s]])
        for j in range(4):
            bk = cx.nb()
            for k in range(8):
                cx.mm(bk.t[:], win.t[:, k, 2464 + j * 128:2464 + (j + 1) * 128], h.t[:, k, :], k == 0, k == 7, [win.r, h.r], [bk.r])
            cx.act(sbg_t[:, j, c0:c0 + SL], bk.t[:], AF.Silu, [bk.r], [sbg_r[s]])

    lat, latg = io["lat"], io["latg"]
    Rlat, Rlatg = P.res("lat"), P.res("latg")
    P.dma("sp", lat[0:128, :], ckvn_t[:, NT:2 * NT], reads=ckvn_r[4:8], writes=[Rlat])
    P.dma("sp", lat[128:160, :], krope_t[64:96, NT:2 * NT], reads=krope_r[4:8], writes=[Rlat])
    P.cc("AllGather", lat, latg.rearrange("r p t -> (r p) t"), PAIRS, reads=[Rlat], writes=[Rlatg])
    P.dma("sp", ckvn_t[:, 0:NT], latg[0, 0:128, :], reads=[Rlatg], writes=ckvn_r[0:4], sem_res=ckvn_r[0])
    P.dma("sp", krope_t[64:96, 0:NT], latg[0, 128:160, :], reads=[Rlatg], writes=krope_r[0:4], sem_res=krope_r[0])

    P.barrier()
    ar.reset(base)

    yb = ar.take([128, 4, NT], BF16, "yb_all")
    yb_r = [P.res("yb%d" % i) for i in range(4)]
    mark = ar.mark()
    vp = [ar.take([128, 32, 192], BF16, "vp%d" % i) for i in range(2)]
    kh = [ar.take([128, 4096], BF16, "kh%d" % i) for i in range(2)]
    pts = [ar.take([128, 512], BF16, "pt%d" % i) for i in range(6)]
    rds = [ar.take([128, 512], F32, "rd%d" % i) for i in range(2)]
    tms = [ar.take([128, 512], F32, "tm%d" % i) for i in range(2)]
    for v_ in vp:
        cx.memset("pool", v_.t[:, :, 64:128], 1.0, [v_.r])
    cx.nbanks_rr = 6
    ada_job = None
    if io.get("ada_jobs"):
        awb = [ar.take([128, 3072], F32, "adaw%d" % i) for i in range(2)]
        aacc = ar.take([128, 3072], F32, "adaacc")

        def _jobs():
            for (l2, io2) in io["ada_jobs"]:
                for _ in cx.ada_steps(l2, io2, awb, aacc):
                    yield
        ada_job = _jobs()
    cnt = 0
    pti = 0
    for j in range(4):
        v_ = vp[j % 2]
        for g in range(8):
            bank = cx.nb()
            for i in range(4):
                kb = 4 * g + i
                cx.mm(bank.t[:, i * 128:(i + 1) * 128], ckvn_t[:, kb * 128:(kb + 1) * 128], wvv.t[:, j * 128:(j + 1) * 128], True, True,
                      [ckvn_r[kb // 4], wvv.r], [bank.r], inc=(i == 3))
            bv = bank.t[:, :].rearrange("p (a b) -> p a b", a=4)
            cx.copy("dve", v_.t[:, 4 * g:4 * g + 4, 0:64], bv[:, :, 0:64], [bank.r], [v_.r])
            cx.copy("act", v_.t[:, 4 * g:4 * g + 4, 128:192], bv[:, :, 64:128], [bank.r], [v_.r])
        for hd in (2 * j, 2 * j + 1):
            k_ = kh[hd % 2]
            cx.copy("pool", k_.t[64:96, :], krope_t[64:96, :], krope_r, [k_.r])
            for kc in range(8):
                bank = cx.nb()
                cx.mm(bank.t[0:64, :], wkk.t[:, hd * 64:(hd + 1) * 64], ckvn_t[:, kc * SL:(kc + 1) * SL], True, True,
                      [wkk.r, ckvn_r[kc]], [bank.r])
                cx.copy("dve", k_.t[0:64, kc * SL:(kc + 1) * SL], bank.t[0:64, :], [bank.r], [k_.r])
            vc0 = 0 if hd % 2 == 0 else 64
            for s in range(NSL):
                if ada_job is not None:
                    try:
                        next(ada_job)
                    except StopIteration:
                        ada_job = None
                accb = cx.banks[6 + cnt % 2]
                cnt += 1
                nkb = 16 + 4 * (s + 1)
                LA = 3
                scs = {}

                def emit_sc(kb, s=s, hd=hd, k_=k_, nkb=nkb):
                    jd = kb - (nkb - 4)
                    lo = 128 * jd if jd >= 0 else 0
                    sc = cx.nb()
                    cx.mm(sc.t[:, lo:SL], k_.t[0:96, kb * 128:(kb + 1) * 128], q_t[0:96, hd, s * SL + lo:(s + 1) * SL], True, True,
                          [k_.r, q_r[s]], [sc.r])
                    scs[kb] = (sc, lo, jd)

                for kb in range(min(LA, nkb)):
                    emit_sc(kb)
                for kb in range(nkb):
                    sc, lo, jd = scs.pop(kb)
                    pt = pts[pti % 6]
                    pti += 1
                    cx.act(pt.t[:, lo:SL], sc.t[:, lo:SL], AF.Exp, [sc.r, keyb.r], [pt.r], bias=keyb.t[:, kb:kb + 1], scale=SCALE)
                    if jd >= 0:
                        cx.memset("dve", pt.t[64:128, lo:lo + 64], 0.0, [pt.r])
                    if kb + LA < nkb:
                        emit_sc(kb + LA)
                    cx.mm(accb.t[:, lo:SL], v_.t[:, kb, vc0:vc0 + 128], pt.t[:, lo:SL], kb == 0, kb == nkb - 1,
                          [v_.r, pt.r], [accb.r], inc=True)
                if hd % 2 == 0:
                    npart, dpart = slice(0, 64), slice(64, 128)
                else:
                    npart, dpart = slice(64, 128), slice(0, 64)
                rd = rds[cnt % 2]
                tm = tms[cnt % 2]
                P.op("dve", lambda e, rd=rd, accb=accb, dpart=dpart: e.reciprocal(out=rd.t[dpart, :], in_=accb.t[dpart, :]), [accb.r], [rd.r])
                cx.tt("dve", tm.t[npart, :], accb.t[npart, :], rd.t[dpart, :], ALU.mult, [accb.r, rd.r], [tm.r])
                cx.tt("pool", yb.t[npart, j, s * SL:(s + 1) * SL], tm.t[npart, :], sbg_t[npart, j, s * SL:(s + 1) * SL], ALU.mult,
                      [tm.r, sbg_r[s]], [yb_r[s]])
    cx.nbanks_rr = 8
    if ada_job is not None:
        for _ in ada_job:
            pass

    P.barrier()
    ar.reset(mark)

    wout = ar.take([128, 8, 1024], BF16, "wout")
    ysbs = [ar.take([128, 8, 512], F32, "ysb%d" % i) for i in range(2)]
    ysqs = [ar.take([128, 512], BF16, "ysq%d" % i) for i in range(2)]
    xcs = [ar.take([128, 512], F32, "xc%d" % i) for i in range(6)]
    tmps = [ar.take([128, 512], F32, "tmp%d" % i) for i in range(2)]
    obufs = [ar.take([128, 512], F32, "ob%d" % i) for i in range(3)]
    rstds = [ar.take([128, 512], F32, "rstd%d" % i) for i in range(2)]
    stg = [ar.take([128, 2048], F32, "stg%d" % i) for i in range(2)]
    cx.load_weight_bf16(lambda k, a, b: wout.t[:, k, a:b], io["w_out"], 8, 1024, wout.r, stg)
    for s in range(NSL):
        ysb = ysbs[s % 2]
        rstd = rstds[s % 2]
        c0 = s * SL

        def rhs_fn(k, c0=c0):
            if k < 4:
                return ya_t[:, k, c0:c0 + SL]
            return yb.t[:, k - 4, c0:c0 + SL]

        cx.out_phase(wout, rhs_fn, [ya_r[s], yb_r[s]], xT[:, :, c0:c0 + SL], outT[:, :, c0:c0 + SL],
                     ysb, ysqs, rstd, xcs, tmps, obufs, halo_d=(io.get("halo_out") if s == NSL - 1 else None))


PAIRS = [[0, 1], [2, 3], [4, 5], [6, 7]]
_DBG_NOCC = False
LAYER_KEYS_EVEN = ["w_in", "w_uq", "w_ukvk", "w_ukvv", "w_out", "scw", "scb", "qg", "kvg"]
LAYER_SHAPES_EVEN = {"w_in": [128, 8, 3008], "w_uq": [128, 2, 1024], "w_ukvk": [128, 1, 512], "w_ukvv": [128, 1, 512],
                     "w_out": [128, 8, 1024], "scw": [128, 4, 3], "scb": [128, 4], "qg": [128, 2], "kvg": [128, 1]}
LAYER_SHAPES_ODD = {"w_in": [128, 8, 3072], "w_out": [128, 8, 1024], "convw": [128, 8, 31], "convb": [128, 8],
                    "lng": [128, 8], "lnb": [128, 8]}
LAYER_SHAPES_COMMON = {"adaw": [128, 8, 3072], "adab": [128, 24], "preg": [128, 8], "postg": [128, 8]}


def build_fused():
    nc = bass.Bass("TRN2", target_bir_lowering=False)
    xT = dram_in(nc, "xT", [128, 8, NT])
    xH0 = dram_in(nc, "xH", [128, 8, 32])
    posO = dram_in(nc, "posO", [1, NT], I32)
    cT = dram_in(nc, "cT", [128, 8])
    hflag = dram_in(nc, "hflag", [128, 1])
    invf_d = dram_in(nc, "invf2", [32, 1])
    sgn_d = dram_in(nc, "sgn2", [32, 1])
    kb_d = dram_in(nc, "keybias", [128, 32])
    identd = dram_in(nc, "ident", [128, 128])
    L = []
    for l in range(4):
        d = {}
        shapes = dict(LAYER_SHAPES_COMMON)
        shapes.update(LAYER_SHAPES_EVEN if l % 2 == 0 else LAYER_SHAPES_ODD)
        for k, shp in shapes.items():
            d[k] = dram_in(nc, "L%d_%s" % (l, k), shp)
        L.append(d)
    outT = nc.dram_tensor("outT", [128, 8, NT], F32, kind="ExternalOutput").ap()
    xs_raw = [nc.dram_tensor("xs%d" % l, [8, 128, NT], F32, kind="Internal").ap() for l in range(3)]
    xs = [t_.rearrange("c p t -> p c t") for t_ in xs_raw]
    hal = [nc.dram_tensor("hal%d" % l, [128, 8, 32], F32, kind="Internal").ap() for l in range(3)]
    halg = [nc.dram_tensor("halg%d" % l, [2, 128, 8, 32], F32, kind="Internal").ap() for l in range(3)]
    lats = {l: nc.dram_tensor("lat%d" % l, [160, NT], BF16, kind="Internal").ap() for l in (0, 2)}
    latgs = {l: nc.dram_tensor("latg%d" % l, [2, 160, NT], BF16, kind="Internal").ap() for l in (0, 2)}

    cx = Ctx(nc)
    P = cx.P
    cx.setup_consts(identd, hflag, invf_d, sgn_d, kb_d)
    cx.ada_setup(cT)
    ar = Arena(P, "arena", 49 * 1024 + 512)
    Rhalg = [P.res("halg%d" % l) for l in range(3)]
    Rsrc = P.res("ccsrc")

    for l in range(4):
        io = dict(L[l])
        io["cT"] = cT
        io["layer"] = l
        if l == 0:
            io["ada_jobs"] = [(l2, L[l2]) for l2 in (1, 2, 3)]
        io["xT"] = xT if l == 0 else xs[l - 1]
        io["out"] = outT if l == 3 else xs[l]
        if l < 3:
            io["halo_out"] = hal[l]
        pre = []
        if l == 0 or _DBG_NOCC:
            io["xF"], io["xH"] = None, xH0
        else:
            def gather_halo(l=l):
                P.cc("AllGather", hal[l - 1].rearrange("p c t -> p (c t)"), halg[l - 1].rearrange("r p c t -> (r p) (c t)"), PAIRS, reads=[Rsrc], writes=[Rhalg[l - 1]])
            pre.append(gather_halo)
            io["xH"] = halg[l - 1][0]
            io["xHres"] = [Rhalg[l - 1]]
            io["xFres"] = [Rhalg[l - 1]]
        io["pre"] = pre
        io["posO"], io["posF"] = posO, None
        if l % 2 == 0:
            io["lat"], io["latg"] = lats[l], latgs[l]
            io.setdefault("xF", None)
        if l % 2 == 0:
            emit_even(cx, ar, io)
        else:
            emit_odd(cx, ar, io)
    P.emit()
    return nc


def fm(v):
    v = np.asarray(v)
    return np.ascontiguousarray(v.reshape(-1, 128).T)


def wk(w):
    w = np.asarray(w)
    k, n = w.shape
    return np.ascontiguousarray(w.reshape(k // 128, 128, n).transpose(1, 0, 2))


def xfm(xb):
    t = xb.shape[0]
    return np.ascontiguousarray(xb.T.reshape(8, 128, t).transpose(1, 0, 2))


def xfm_inv(o):
    t = o.shape[2]
    return o.transpose(1, 0, 2).reshape(D, t).T


def _perm_uq(w_uq):
    w = np.asarray(w_uq)
    blocks = []
    for h in range(8):
        b = h * 96
        blocks += [w[:, b:b + 64], w[:, b + 64:b + 96], w[:, b + 80:b + 96], w[:, b + 64:b + 80]]
    return np.concatenate(blocks, axis=1)


_NC_CACHE = {}


def kernel(x, c, positions, ada_w, ada_b, pre_norm_g, post_norm_g,
           even_w_in, even_sc_conv_w, even_sc_conv_b, even_q_norm_g, even_kv_norm_g,
           even_w_uq, even_w_ukv, even_w_out,
           odd_w_in, odd_conv_w, odd_conv_b, odd_ln_g, odd_ln_b, odd_w_out):
    x = np.ascontiguousarray(np.asarray(x, dtype=np.float32))
    c = np.asarray(c, dtype=np.float32)
    positions = np.asarray(positions)
    if "fused" not in _NC_CACHE:
        _NC_CACHE["fused"] = build_fused()
    nc = _NC_CACHE["fused"]
    invf = (np.float32(1.0) / (np.float32(10000.0) ** (np.arange(0, 32, 2, dtype=np.float32) / np.float32(32)))).astype(np.float32)
    shared = {
        "ident": np.eye(128, dtype=np.float32),
        "invf2": np.ascontiguousarray(np.concatenate([invf, invf])[:, None]),
        "sgn2": np.ascontiguousarray(np.concatenate([-np.ones(16, np.float32), np.ones(16, np.float32)])[:, None]),
    }
    for l in range(4):
        i = l // 2
        p = "L%d_" % l
        shared[p + "adaw"] = wk(ada_w[l])
        shared[p + "adab"] = fm(ada_b[l])
        shared[p + "preg"] = fm(pre_norm_g[l])
        shared[p + "postg"] = fm(post_norm_g[l])
        if l % 2 == 0:
            w_in = np.asarray(even_w_in[i])
            w_ukv = np.asarray(even_w_ukv[i])
            shared[p + "w_in"] = wk(np.concatenate([w_in, w_in[:, 2448:2464], w_in[:, 2432:2448]], axis=1))
            shared[p + "w_uq"] = wk(_perm_uq(even_w_uq[i]))
            shared[p + "w_ukvk"] = np.ascontiguousarray(np.concatenate([w_ukv[:, h * 128:h * 128 + 64] for h in range(8)], axis=1)[:, None, :])
            shared[p + "w_ukvv"] = np.ascontiguousarray(np.concatenate([w_ukv[:, h * 128 + 64:h * 128 + 128] for h in range(8)], axis=1)[:, None, :])
            shared[p + "w_out"] = wk(even_w_out[i])
            shared[p + "scw"] = np.ascontiguousarray(np.asarray(even_sc_conv_w[i]).T.reshape(4, 128, 3).transpose(1, 0, 2))
            shared[p + "scb"] = fm(even_sc_conv_b[i])
            shared[p + "qg"] = fm(even_q_norm_g[i])
            shared[p + "kvg"] = fm(even_kv_norm_g[i])
        else:
            shared[p + "w_in"] = wk(odd_w_in[i])
            shared[p + "w_out"] = wk(odd_w_out[i])
            shared[p + "convw"] = np.ascontiguousarray(np.asarray(odd_conv_w[i]).T.reshape(8, 128, 31).transpose(1, 0, 2))
            shared[p + "convb"] = fm(odd_conv_b[i])
            shared[p + "lng"] = fm(odd_ln_g[i])
            shared[p + "lnb"] = fm(odd_ln_b[i])
    maps = []
    for core in range(8):
        b, half = core // 2, core % 2
        t0 = half * NT
        m = dict(shared)
        m["xT"] = xfm(x[b, t0:t0 + NT])
        m["xH"] = xfm(x[b, NT - 32:NT]) if half == 1 else np.zeros((128, 8, 32), np.float32)
        m["cT"] = fm(c[b])
        m["hflag"] = np.full((128, 1), float(half), np.float32)
        m["posO"] = np.ascontiguousarray(positions[b, t0:t0 + NT][None, :]).astype(np.int32)
        kbias = np.zeros((128, 32), np.float32)
        if half == 0:
            kbias[:, 0:16] = -30000.0
        m["keybias"] = kbias
        maps.append(m)
    res = run_bass_kernel_spmd(nc, maps, core_ids=list(range(8)))
    out = np.empty_like(x)
    for core in range(8):
        b, half = core // 2, core % 2
        out[b, half * NT:(half + 1) * NT] = xfm_inv(res.results[core]["outT"])
    return out
```
